# Optimizing a Trainium2 kernel written in Bass

```python
import math
import jax, jax.numpy as jnp
from jax import lax
import numpy as np

D_MODEL = 1024
BATCH = 8
SEQ = 8192
DEPTH = 4

N_A_LAYERS = DEPTH // 2
N_B_LAYERS = DEPTH - N_A_LAYERS

GLA_HEADS = 4
GLA_DK = D_MODEL // 2 // GLA_HEADS
GLA_DV = D_MODEL // GLA_HEADS
GLA_GATE_RANK = 16
GLA_GATE_NORMALIZER = 16.0
GLA_CHUNK = 64
GLA_IN_WIDTH = 2 * GLA_HEADS * GLA_DK + 2 * GLA_HEADS * GLA_DV + GLA_GATE_RANK

DIL_PAIRS = ((128, 1), (512, 4), (2048, 16))
DIL_GROUPS = len(DIL_PAIRS)
DIL_HEADS = 16
DIL_HEAD_DIM = D_MODEL // DIL_HEADS
DIL_STEPS = DIL_PAIRS[0][0] // DIL_PAIRS[0][1]

FFN_HIDDEN = -(-8 * D_MODEL // (3 * 256)) * 256
DEEPNORM_ALPHA = (2.0 * DEPTH) ** 0.25
DEEPNORM_BETA = (8.0 * DEPTH) ** -0.25
LN_EPS = 1e-5
RMS_EPS = 1e-5

kernel_name = "yoco_gla_dilated_deepnorm_adaln"


def layer_norm(x, g, b):
    xf = x.astype(jnp.float32)
    mu = xf.mean(-1, keepdims=True)
    var = jnp.square(xf - mu).mean(-1, keepdims=True)
    return ((xf - mu) * lax.rsqrt(var + LN_EPS) * g + b).astype(x.dtype)


def modulate(x, shift, scale):
    return x * (1.0 + scale[:, None]) + shift[:, None]


def gla_chunked(q, k, v, gk):
    B, S, H, dk = q.shape
    dv = v.shape[-1]
    C = GLA_CHUNK
    n = S // C

    def chunk(a):
        return a.astype(jnp.float32).reshape(B, n, C, H, a.shape[-1]).transpose(0, 1, 3, 2, 4)

    q, k, v, gk = chunk(q), chunk(k), chunk(v), chunk(gk)
    b = jnp.cumsum(gk, axis=3)
    q_e = q * jnp.exp(b)
    k_e = k * jnp.exp(-b)
    causal = jnp.tril(jnp.ones((C, C), dtype=bool))
    attn = jnp.where(causal, jnp.einsum('bnhik,bnhjk->bnhij', q_e, k_e), 0.0)
    o_intra = jnp.einsum('bnhij,bnhjv->bnhiv', attn, v)
    b_last = b[:, :, :, -1]
    k_d = k * jnp.exp(b_last[:, :, :, None] - b)
    decay = jnp.exp(b_last)

    def step(state, xs):
        q_c, k_c, v_c, dec = xs
        o = jnp.einsum('bhik,bhkv->bhiv', q_c, state)
        state = dec[..., None] * state + jnp.einsum('bhjk,bhjv->bhkv', k_c, v_c)
        return state, o

    xs = tuple(a.swapaxes(0, 1) for a in (q_e, k_d, v, decay))
    s0 = jnp.zeros((B, H, dk, dv), jnp.float32)
    _, o_inter = lax.scan(step, s0, xs)
    o = o_intra + o_inter.swapaxes(0, 1)
    return o.transpose(0, 1, 3, 2, 4).reshape(B, S, H, dv)


def gla_mixer(h, w_in, w_gate_up, b_gate, norm_g, w_out):
    B, S, _ = h.shape
    H, dk, dv = GLA_HEADS, GLA_DK, GLA_DV
    proj = h @ w_in
    cuts = [H * dk, 2 * H * dk, 2 * H * dk + H * dv, 2 * H * dk + 2 * H * dv]
    q, k, v, r, glr = jnp.split(proj, cuts, axis=-1)
    gk = jax.nn.log_sigmoid((glr @ w_gate_up + b_gate).astype(jnp.float32)) / GLA_GATE_NORMALIZER
    o = gla_chunked((q * dk ** -0.5).reshape(B, S, H, dk), k.reshape(B, S, H, dk),
                    v.reshape(B, S, H, dv), gk.reshape(B, S, H, dk))
    o = o * lax.rsqrt(jnp.mean(jnp.square(o), -1, keepdims=True) + RMS_EPS) * norm_g
    o = o.reshape(B, S, H * dv).astype(h.dtype) * jax.nn.silu(r)
    return o @ w_out


def dilated_group(q, k, v, dilation):
    B, S, H, dh = q.shape
    L = DIL_STEPS
    span = dilation * L
    s_pad = -(-S // span) * span
    padw = ((0, 0), (0, s_pad - S), (0, 0), (0, 0))
    n = s_pad // dilation
    nb = n // L

    def to_strided(a):
        a = jnp.pad(a.astype(jnp.float32), padw)
        return a.reshape(B, n, dilation, H, dh).transpose(0, 2, 1, 3, 4).reshape(B, dilation, nb, L, H, dh)

    def with_prev(a):
        prev = jnp.concatenate([jnp.zeros_like(a[:, :, :1]), a[:, :, :-1]], axis=2)
        return jnp.concatenate([prev, a], axis=3)

    def from_strided(a):
        rest = a.shape[4:]
        a = a.reshape((B, dilation, n) + rest)
        a = jnp.moveaxis(a, 1, 2).reshape((B, s_pad) + rest)
        return a[:, :S]

    qs = to_strided(q)
    kk = with_prev(to_strided(k))
    vv = with_prev(to_strided(v))
    s = jnp.einsum('brnihd,brnjhd->brnhij', qs, kk) * dh ** -0.5
    i = jnp.arange(L)[:, None]
    j = jnp.arange(2 * L)[None, :]
    band = (j >= i) & (j <= i + L)
    valid = (jnp.arange(nb)[:, None, None] > 0) | (j >= L)[None]
    mask = band[None] & valid
    s = jnp.where(mask[None, None, :, None], s, -jnp.inf)
    m = s.max(-1, keepdims=True)
    p = jnp.exp(s - m)
    den = p.sum(-1)
    o = jnp.einsum('brnhij,brnjhd->brnihd', p, vv) / den.swapaxes(3, 4)[..., None]
    return from_strided(o), from_strided(m[..., 0].swapaxes(3, 4)), from_strided(den.swapaxes(3, 4))


def dilated_mixer(h, k_sh, v_sh, w_q, w_out):
    B, S, _ = h.shape
    q = (h @ w_q).reshape(B, S, DIL_GROUPS, DIL_HEADS, DIL_HEAD_DIM)
    outs, maxs, dens = [], [], []
    for g, (_, dil) in enumerate(DIL_PAIRS):
        o, m, d = dilated_group(q[:, :, g], k_sh, v_sh, dil)
        outs.append(o); maxs.append(m); dens.append(d)
    outs, maxs, dens = jnp.stack(outs), jnp.stack(maxs), jnp.stack(dens)
    w = dens * jnp.exp(maxs - maxs.max(0))
    o = (w[..., None] * outs).sum(0) / w.sum(0)[..., None]
    return o.reshape(B, S, D_MODEL).astype(h.dtype) @ w_out


def swiglu(h, w_in, w_out):
    g, u = jnp.split(h @ w_in, 2, axis=-1)
    return (jax.nn.silu(g) * u) @ w_out


def setup_inputs(seed: int = 0) -> dict:
    key = jax.random.key(seed)
    ks = jax.random.split(key, 20)
    D, f32 = D_MODEL, jnp.float32

    def nrm(k, shape, scale):
        return jax.random.normal(k, shape, f32) * scale

    H, dk, dv = GLA_HEADS, GLA_DK, GLA_DV
    v_col_scale = jnp.concatenate([jnp.ones((2 * H * dk,), f32), jnp.full((H * dv,), DEEPNORM_BETA, f32),
                                   jnp.ones((H * dv + GLA_GATE_RANK,), f32)])
    kv_col_scale = jnp.concatenate([jnp.ones((D,), f32), jnp.full((D,), DEEPNORM_BETA, f32)])
    return {
        "x": nrm(ks[0], (BATCH, SEQ, D), 1.0),
        "c": nrm(ks[1], (BATCH, D), 1.0),
        "gla_w_in": nrm(ks[2], (N_A_LAYERS, D, GLA_IN_WIDTH), D ** -0.5) * v_col_scale,
        "gla_w_gate_up": nrm(ks[3], (N_A_LAYERS, GLA_GATE_RANK, H * dk), GLA_GATE_RANK ** -0.5),
        "gla_b_gate": nrm(ks[4], (N_A_LAYERS, H * dk), 0.1),
        "gla_norm_g": 1.0 + nrm(ks[5], (N_A_LAYERS, dv), 0.02),
        "gla_w_out": nrm(ks[6], (N_A_LAYERS, H * dv, D), (H * dv) ** -0.5 * DEEPNORM_BETA),
        "dil_w_q": nrm(ks[7], (N_B_LAYERS, D, DIL_GROUPS * D), D ** -0.5),
        "dil_w_out": nrm(ks[8], (N_B_LAYERS, D, D), D ** -0.5 * DEEPNORM_BETA),
        "kv_ada_w": nrm(ks[9], (D, 2 * D), 0.1 * D ** -0.5),
        "kv_ada_b": nrm(ks[10], (2 * D,), 0.02),
        "w_kv": nrm(ks[11], (D, 2 * D), D ** -0.5) * kv_col_scale,
        "ffn_w_in": nrm(ks[12], (DEPTH, D, 2 * FFN_HIDDEN), D ** -0.5),
        "ffn_w_out": nrm(ks[13], (DEPTH, FFN_HIDDEN, D), FFN_HIDDEN ** -0.5 * DEEPNORM_BETA),
        "ada_w": nrm(ks[14], (DEPTH, D, 6 * D), 0.1 * D ** -0.5),
        "ada_b": nrm(ks[15], (DEPTH, 6 * D), 0.02),
        "ln_g": 1.0 + nrm(ks[16], (DEPTH, 2, D), 0.02),
        "ln_b": nrm(ks[17], (DEPTH, 2, D), 0.02),
    }


def reference(x, c, gla_w_in, gla_w_gate_up, gla_b_gate, gla_norm_g, gla_w_out, dil_w_q, dil_w_out,
              kv_ada_w, kv_ada_b, w_kv, ffn_w_in, ffn_w_out, ada_w, ada_b, ln_g, ln_b):
    B, S, D = x.shape
    alpha = DEEPNORM_ALPHA
    sc = jax.nn.silu(c)
    k_sh = v_sh = None
    for l in range(DEPTH):
        sh1, s1, g1, sh2, s2, g2 = jnp.split(sc @ ada_w[l] + ada_b[l], 6, axis=-1)
        h = modulate(x, sh1, s1)
        if l < N_A_LAYERS:
            y = gla_mixer(h, gla_w_in[l], gla_w_gate_up[l], gla_b_gate[l], gla_norm_g[l], gla_w_out[l])
        else:
            if l == N_A_LAYERS:
                kv_shift, kv_scale = jnp.split(sc @ kv_ada_w + kv_ada_b, 2, axis=-1)
                k_flat, v_flat = jnp.split(modulate(x, kv_shift, kv_scale) @ w_kv, 2, axis=-1)
                k_sh = k_flat.reshape(B, S, DIL_HEADS, DIL_HEAD_DIM)
                v_sh = v_flat.reshape(B, S, DIL_HEADS, DIL_HEAD_DIM)
            y = dilated_mixer(h, k_sh, v_sh, dil_w_q[l - N_A_LAYERS], dil_w_out[l - N_A_LAYERS])
        x = layer_norm(alpha * x + (1.0 + g1)[:, None] * y, ln_g[l, 0], ln_b[l, 0])
        h = modulate(x, sh2, s2)
        x = layer_norm(alpha * x + (1.0 + g2)[:, None] * swiglu(h, ffn_w_in[l], ffn_w_out[l]), ln_g[l, 1], ln_b[l, 1])
    return x
```

```python
import contextlib
import os
DBG = os.environ.get('KDBG', '')
import numpy as np
import concourse.bass as bass
import concourse.mybir as mybir
from concourse.bass_utils import run_bass_kernel_spmd

F32 = mybir.dt.float32
BF16 = mybir.dt.bfloat16
ACT = mybir.ActivationFunctionType
ALU = mybir.AluOpType
AX = mybir.AxisListType

D = 1024
DEPTH = 4
FH = 2816
ALPHA = (2.0 * DEPTH) ** 0.25
LN_EPS = 1e-5
RMS_EPS = 1e-5
GW = 3088
DILS = (1, 4, 16)
NEG = -30000.0

COMPUTE = ("pe", "act", "dve", "pool")
ALL = COMPUTE + ("sp",)


class Buf:
    __slots__ = ("name", "w", "r", "dsem", "dcnt")

    def __init__(self, name):
        self.name = name
        self.w = None
        self.r = {}
        self.dsem = None
        self.dcnt = 0


class Sched:
    def __init__(self, nc, esems, dma_sems):
        self.nc = nc
        self.ops = {e: [] for e in ALL}
        self.seq = {e: 0 for e in COMPUTE}
        self.esem = esems
        self.free_dsems = [(s_, 0) for s_ in dma_sems]
        self.known = {e: {} for e in ALL}
        self.semobj = dict(esems)
        self.ninst = 0

    def _need(self, eng, tok, acc):
        if tok is None:
            return
        k, v = tok
        if k == eng and eng == "pe":
            return
        if acc.get(k, 0) < v:
            acc[k] = v

    @staticmethod
    def _flat(lst):
        out = []
        for b in lst:
            if isinstance(b, (list, tuple)):
                out.extend(b)
            else:
                out.append(b)
        return out

    def _deps(self, eng, reads, writes):
        acc = {}
        for b in reads:
            self._need(eng, b.w, acc)
        for b in writes:
            self._need(eng, b.w, acc)
            for k, v in b.r.items():
                self._need(eng, (k, v), acc)
        kn = self.known[eng]
        for k, v in acc.items():
            if kn.get(k, 0) < v:
                kn[k] = v
                self.ops[eng].append(("wait", self.semobj[k], v))
                self.ninst += 1

    def _mark(self, tok, reads, writes):
        k, v = tok
        for b in reads:
            if b.r.get(k, 0) < v:
                b.r[k] = v
        for b in writes:
            b.w = tok
            b.r = {}

    def op(self, eng, fn, reads=(), writes=()):
        reads = self._flat(reads)
        writes = self._flat(writes)
        self._deps(eng, reads, writes)
        self.seq[eng] += 1
        tok = (eng, self.seq[eng])
        self.ops[eng].append(("op", fn, self.esem[eng]))
        self.ninst += 1
        self._mark(tok, reads, writes)
        return tok

    def dma(self, q, fn, reads=(), writes=(), sembuf=None):
        reads = self._flat(reads)
        writes = self._flat(writes)
        self._deps(q, reads, writes)
        b = sembuf
        if b.dsem is None:
            b.dsem, b.dcnt = self.free_dsems.pop()
            self.semobj[("d", b.name)] = b.dsem
        b.dcnt += 16
        tok = (("d", b.name), b.dcnt)
        self.ops[q].append(("dma", fn, b.dsem))
        self.ninst += 1
        self._mark(tok, reads, writes)
        return tok

    def wait_all(self, eng, bufs):
        self._deps(eng, (), bufs)

    def emit(self, block):
        amap = {"pe": block.tensor, "act": block.scalar, "dve": block.vector,
                "pool": block.gpsimd, "sp": block.sync}
        for e in ALL:
            lst = self.ops[e]

            def body(engobj, lst=lst):
                for item in lst:
                    if item[0] == "wait":
                        engobj.wait_ge(item[1], item[2])
                    elif item[0] == "op":
                        item[1](engobj).then_inc(item[2], 1)
                    else:
                        item[1](engobj).then_inc(item[2], 16)
            if lst:
                amap[e](body)
            self.ops[e] = []


class T:
    def __init__(self, t, name):
        self.t = t
        self.b = Buf(name)

    def __getitem__(self, k):
        return self.t[k]


def build_nc(S, stop_after=None, dbg=False):
    NT = S // 128
    nc = bass.Bass("TRN2", target_bir_lowering=False)

    def din(name, shape):
        return nc.dram_tensor(name, list(shape), F32, kind="ExternalInput").ap()

    x_in = din("x", [S, D])
    c_t = din("c_t", [128, 8])
    gla_w_in = din("gla_w_in", [2, D, GW])
    gla_wgu = din("gla_w_gate_up", [2, 16, 512])
    gla_bg = din("gla_b_gate", [2, 512])
    gla_ng = din("gla_norm_g", [2, 256])
    gla_w_out = din("gla_w_out", [2, D, D])
    dil_w_q = din("dil_w_q", [2, D, 3 * D])
    dil_w_out = din("dil_w_out", [2, D, D])
    kv_ada_w = din("kv_ada_w", [D, 2 * D])
    kv_ada_b = din("kv_ada_b", [1, 2 * D])
    w_kv = din("w_kv", [D, 2 * D])
    ffn_w_in = din("ffn_w_in", [4, D, 2 * FH])
    ffn_w_out = din("ffn_w_out", [4, FH, D])
    ada_w = din("ada_w", [4, D, 6 * D])
    ada_b = din("ada_b", [4, 6 * D])
    ln_g = din("ln_g", [4, 2, D])
    ln_b = din("ln_b", [4, 2, D])
    cst = din("cst", [128, 6, 128])
    out = nc.dram_tensor("out", [S, D], F32, kind="ExternalOutput").ap()

    xs = nc.dram_tensor("xs", [S, D], F32).ap()
    ada_d = nc.dram_tensor("ada_d", [5, 6 * D], F32).ap()
    kT_d = nc.dram_tensor("kT_d", [D, S], BF16).ap()
    v_d = nc.dram_tensor("v_d", [S, D], BF16).ap()
    qT_d = nc.dram_tensor("qT_d", [3 * D, S], BF16).ap()
    og_d = [nc.dram_tensor(f"og_d{g}", [S, D], BF16).ap() for g in range(3)]
    md_d = [nc.dram_tensor(f"md_d{g}", [S, 32], F32).ap() for g in range(3)]

    DX = [Buf(f"dx{t}") for t in range(NT)]
    B_ada = Buf("ada_d")
    B_kT = [Buf(f"dkT{i}") for i in range(max(1, S // 512))]
    B_vd = [Buf(f"dv{t}") for t in range(NT)]
    B_qT = [Buf(f"dqT{i}") for i in range(max(1, S // 512))]
    B_og = Buf("dog")

    with contextlib.ExitStack() as es0:
        esems = {e: es0.enter_context(nc.semaphore("s_" + e)) for e in COMPUTE}
        dsems = [es0.enter_context(nc.semaphore(f"d{i}")) for i in range(92)]
        S_ = Sched(nc, esems, dsems)
        op = S_.op
        dma = S_.dma
        uid = [0]

        def flush():
            for e_ in ALL:
                for k_ in COMPUTE:
                    if k_ != e_ and S_.known[e_].get(k_, 0) < S_.seq[k_]:
                        S_.known[e_][k_] = S_.seq[k_]
                        S_.ops[e_].append(("wait", S_.esem[k_], S_.seq[k_]))
            with nc.Block() as block:
                S_.emit(block)

        def free_dsem(tiles):
            for tt in tiles:
                b = tt.b if isinstance(tt, T) else tt
                if b.dsem is not None:
                    S_.free_dsems.append((b.dsem, b.dcnt))
                    b.dsem = None

        PS = es0.enter_context(nc.psum_tensor("PS", [128, 8, 512], F32))
        PB = [Buf(f"ps{i}") for i in range(8)]

        def sbt(es, name, shape, dt):
            uid[0] += 1
            nm = f"{name}_{uid[0]}"
            return T(es.enter_context(nc.sbuf_tensor(nm, list(shape), dt)), nm)

        cst_f = sbt(es0, "cst_f", [128, 6, 128], F32)
        ident_b = sbt(es0, "ident_b", [128, 128], BF16)
        mask4 = sbt(es0, "mask4", [128, 4, 128], F32)
        ones_b = sbt(es0, "ones_b", [1, 128], BF16)
        dma("sp", lambda e: e.dma_start(out=cst_f[:], in_=cst), writes=[cst_f.b], sembuf=cst_f.b)
        op("dve", lambda e: e.tensor_copy(out=ident_b[:], in_=cst_f[:, 0, :]), reads=[cst_f.b], writes=[ident_b.b])
        for h in range(4):
            op("dve", lambda e, h=h: e.tensor_copy(out=mask4[:, h, :], in_=cst_f[:, 1, :]), reads=[cst_f.b], writes=[mask4.b])
        op("dve", lambda e: e.memset(ones_b[:], 1.0), writes=[ones_b.b])
        tri_incl = cst_f[:, 1, :]
        tri_after = cst_f[:, 2, :]

        with contextlib.ExitStack() as es:
            ct = sbt(es, "ct", [128, 8], F32)
            sc = sbt(es, "sc", [128, 8], F32)
            dma("sp", lambda e: e.dma_start(out=ct[:], in_=c_t), writes=[ct.b], sembuf=ct.b)
            op("act", lambda e: e.activation(out=sc[:], in_=ct[:], func=ACT.Silu), reads=[ct.b], writes=[sc.b])
            wa = [sbt(es, f"wa{i}", [128, 8, 512], F32) for i in range(3)]
            brow = sbt(es, "brow", [1, 6 * D], F32)
            arow = [sbt(es, f"arow{i}", [1, 6 * D], F32) for i in range(2)]
            cnt = 0
            for l in range(5):
                ncb = 12 if l < 4 else 4
                width = ncb * 512
                wsrc = ada_w[l] if l < 4 else kv_ada_w
                bsrc = ada_b[l:l + 1, :] if l < 4 else kv_ada_b
                ar = arow[l % 2]
                dma("sp", lambda e, bsrc=bsrc, width=width: e.dma_start(out=brow[:, 0:width], in_=bsrc),
                    writes=[brow.b], sembuf=brow.b)
                for cb in range(ncb):
                    w = wa[cnt % 3]
                    cnt += 1
                    dma("sp", lambda e, w=w, wsrc=wsrc, cb=cb: e.dma_start(
                        out=w[:], in_=wsrc[:, cb * 512:(cb + 1) * 512].rearrange("(k p) n -> p k n", p=128)),
                        writes=[w.b], sembuf=w.b)
                    pb = cnt % 2
                    for k in range(8):
                        op("pe", lambda e, w=w, k=k, pb=pb: e.matmul(PS[0:1, pb, :], lhsT=sc[:, k:k + 1], rhs=w[:, k, :],
                                                                      start=(k == 0), stop=(k == 7)),
                           reads=[sc.b, w.b], writes=[PB[pb]])
                    op("dve", lambda e, ar=ar, cb=cb, pb=pb: e.tensor_tensor(
                        out=ar[:, cb * 512:(cb + 1) * 512], in0=PS[0:1, pb, :], in1=brow[:, cb * 512:(cb + 1) * 512], op=ALU.add),
                       reads=[PB[pb], brow.b], writes=[ar.b])
                if l < 4:
                    for a0 in (1024, 4096):
                        op("dve", lambda e, ar=ar, a0=a0: e.tensor_scalar_add(out=ar[:, a0:a0 + 2048], in0=ar[:, a0:a0 + 2048], scalar1=1.0),
                           reads=[ar.b], writes=[ar.b])
                else:
                    op("dve", lambda e, ar=ar: e.tensor_scalar_add(out=ar[:, 1024:2048], in0=ar[:, 1024:2048], scalar1=1.0),
                       reads=[ar.b], writes=[ar.b])
                dma("sp", lambda e, ar=ar, l=l, width=width: e.dma_start(out=ada_d[l:l + 1, 0:width], in_=ar[:, 0:width]),
                    reads=[ar.b], writes=[B_ada], sembuf=ar.b)
            S_.wait_all("sp", [B_ada])
            flush()
            free_dsem([ct, brow] + wa + arow)

        def load_bc(es, name, src_row):
            t = sbt(es, name, [128, D], F32)
            dma("sp", lambda e: e.dma_start(out=t[:], in_=src_row.partition_broadcast(128)),
                reads=[B_ada], writes=[t.b], sembuf=t.b)
            return t

        cast_rr = [0]

        def load_w(es, name, src, kch, n, col0=0, stg=None):
            t = es.enter_context(nc.sbuf_tensor(f"{name}_{uid[0]}", [128, kch, n], BF16))
            uid[0] += 1
            bufs = []
            engs = ("dve", "pool", "act")
            for k in range(kch):
                kb = []
                for c0 in range(0, n, 1024):
                    wd = min(1024, n - c0)
                    b = Buf(f"{name}k{k}c{c0}_{uid[0]}")
                    uid[0] += 1
                    sg_ = stg[cast_rr[0] % len(stg)]
                    eng = engs[cast_rr[0] % 3]
                    cast_rr[0] += 1
                    dma("sp", lambda e, k=k, c0=c0, wd=wd, sg_=sg_: e.dma_start(
                        out=sg_[:, 0:wd], in_=src[k * 128:(k + 1) * 128, col0 + c0:col0 + c0 + wd]),
                        writes=[sg_.b], sembuf=sg_.b)
                    if eng == "act":
                        op("act", lambda e, k=k, c0=c0, wd=wd, sg_=sg_: e.copy(out=t[:, k, c0:c0 + wd], in_=sg_[:, 0:wd]),
                           reads=[sg_.b], writes=[b])
                    else:
                        op(eng, lambda e, k=k, c0=c0, wd=wd, sg_=sg_: e.tensor_copy(out=t[:, k, c0:c0 + wd], in_=sg_[:, 0:wd]),
                           reads=[sg_.b], writes=[b])
                    kb.append(b)
                bufs.append(kb)
            return t, bufs

        def src_tile(first, t):
            base = x_in if first else xs
            return base[t * 128:(t + 1) * 128, :]

        def modulate_T(xt, scp, sh, h, hT, pbank, tmp, ncol=128, col0=0):
            op("dve", lambda e: e.tensor_tensor(out=tmp[:], in0=xt[:], in1=scp[:], op=ALU.mult),
               reads=[xt.b, scp.b], writes=[tmp.b])
            op("pool", lambda e: e.tensor_tensor(out=h[:], in0=tmp[:], in1=sh[:], op=ALU.add),
               reads=[tmp.b, sh.b], writes=[h.b])
            pv = PS[:, pbank, :].bitcast(BF16)
            for k in range(8):
                op("pe", lambda e, k=k: e.transpose(out=pv[:, k * 128:(k + 1) * 128], in_=h[:, k * 128:(k + 1) * 128], identity=ident_b[:]),
                   reads=[h.b, ident_b.b], writes=[PB[pbank]])
            op("act", lambda e: e.copy(out=hT[:, :, col0:col0 + 128], in_=pv.rearrange("p (k n) -> p k n", k=8)),
               reads=[PB[pbank]], writes=[hT.b])

        def resid_ln(xt, yb0, gp, lng, lnb, tmp, z, xo, stt, mv):
            yv = PS[:, yb0:yb0 + 2, :].rearrange("p b n -> p (b n)")
            op("dve", lambda e: e.tensor_tensor(out=tmp[:], in0=yv, in1=gp[:], op=ALU.mult),
               reads=[PB[yb0], PB[yb0 + 1], gp.b], writes=[tmp.b])
            op("dve", lambda e: e.scalar_tensor_tensor(out=z[:], in0=xt[:], scalar=ALPHA, in1=tmp[:], op0=ALU.mult, op1=ALU.add),
               reads=[xt.b, tmp.b], writes=[z.b])
            for c in range(2):
                op("dve", lambda e, c=c: e.bn_stats(out=stt[:, c, :], in_=z[:, c * 512:(c + 1) * 512]), reads=[z.b], writes=[stt.b])
            op("dve", lambda e: e.bn_aggr(out=mv[:, 0:2], in_=stt[:]), reads=[stt.b], writes=[mv.b])
            op("act", lambda e: e.activation(out=mv[:, 2:3], in_=mv[:, 1:2], func=ACT.Ln, bias=LN_EPS), reads=[mv.b], writes=[mv.b])
            op("act", lambda e: e.activation(out=mv[:, 3:4], in_=mv[:, 2:3], func=ACT.Exp, scale=-0.5), reads=[mv.b], writes=[mv.b])
            op("dve", lambda e: e.scalar_tensor_tensor(out=mv[:, 4:5], in0=mv[:, 0:1], scalar=-1.0, in1=mv[:, 3:4], op0=ALU.mult, op1=ALU.mult),
               reads=[mv.b], writes=[mv.b])
            op("act", lambda e: e.activation(out=tmp[:], in_=z[:], func=ACT.Identity, scale=mv[:, 3:4], bias=mv[:, 4:5]),
               reads=[z.b, mv.b], writes=[tmp.b])
            op("dve", lambda e: e.tensor_tensor(out=z[:], in0=tmp[:], in1=lng[:], op=ALU.mult), reads=[tmp.b, lng.b], writes=[z.b])
            op("pool", lambda e: e.tensor_tensor(out=xo[:], in0=z[:], in1=lnb[:], op=ALU.add), reads=[z.b, lnb.b], writes=[xo.b])

        state = {"first": True}

        def x_dst(last):
            return out if last else xs

        def gla_phase(l, last=False):
            first = state["first"]
            state["first"] = False
            with contextlib.ExitStack() as es:
                scp = load_bc(es, "scp", ada_d[l:l + 1, 1024:2048])
                sh = load_bc(es, "sh", ada_d[l:l + 1, 0:1024])
                gp = load_bc(es, "gp", ada_d[l:l + 1, 2048:3072])
                lng = load_bc(es, "lng", ln_g[l, 0:1, :])
                lnb = load_bc(es, "lnb", ln_b[l, 0:1, :])
                ngb = sbt(es, "ngb", [128, 4, 256], F32)
                for h in range(4):
                    dma("sp", lambda e, h=h: e.dma_start(out=ngb[:, h, :], in_=gla_ng[l:l + 1, :].partition_broadcast(128)),
                        writes=[ngb.b], sembuf=ngb.b)
                xts = [sbt(es, f"xt{i}", [128, D], F32) for i in range(3)]
                xos = [sbt(es, f"xo{i}", [128, D], F32) for i in range(2)]
                tmp = sbt(es, "tmp", [128, D], F32)
                z = sbt(es, "z", [128, D], F32)
                stg = [tmp, z] + xos
                win, Bwin = load_w(es, "win", gla_w_in[l], 8, GW, stg=stg)
                wout, Bwout = load_w(es, "wout", gla_w_out[l], 8, D, stg=stg)
                wgu = sbt(es, "wgu", [16, 512], BF16)
                bgr = sbt(es, "bgr", [1, 512], BF16)
                wgu_f = sbt(es, "wgu_f", [16, 512], F32)
                bgr_f = sbt(es, "bgr_f", [1, 512], F32)
                dma("sp", lambda e: e.dma_start(out=wgu_f[:], in_=gla_wgu[l]), writes=[wgu_f.b], sembuf=wgu_f.b)
                dma("sp", lambda e: e.dma_start(out=bgr_f[:], in_=gla_bg[l:l + 1, :]), writes=[bgr_f.b], sembuf=bgr_f.b)
                op("dve", lambda e: e.tensor_copy(out=wgu[:], in_=wgu_f[:]), reads=[wgu_f.b], writes=[wgu.b])
                op("dve", lambda e: e.tensor_copy(out=bgr[:], in_=bgr_f[:]), reads=[bgr_f.b], writes=[bgr.b])
                h_ = sbt(es, "h", [128, D], BF16)
                hT = sbt(es, "hT", [128, 8, 128], BF16)
                glrT = sbt(es, "glrT", [16, 128], BF16)
                e1 = sbt(es, "e1", [128, 512], F32)
                sp_ = sbt(es, "sp", [128, 512], F32)
                E = sbt(es, "E", [128, 4, 128], F32)
                Einv = sbt(es, "Einv", [128, 4, 128], F32)
                Ea = sbt(es, "Ea", [128, 512], F32)
                qeT = sbt(es, "qeT", [128, 4, 128], BF16)
                keT = sbt(es, "keT", [128, 4, 128], BF16)
                kd = sbt(es, "kd", [128, 512], BF16)
                v_ = sbt(es, "v", [128, D], BF16)
                sr = sbt(es, "sr", [128, 4, 256], F32)
                gr = sbt(es, "gr", [128, 4, 256], F32)
                attnT = sbt(es, "attnT", [128, 4, 128], BF16)
                S32 = sbt(es, "S32", [128, 4, 256], F32)
                S16 = sbt(es, "S16", [128, 4, 256], BF16)
                osq = sbt(es, "osq", [128, 4, 256], F32)
                ss = sbt(es, "ss", [128, 8], F32)
                og = sbt(es, "og", [128, D], BF16)
                ogT = sbt(es, "ogT", [128, 8, 128], BF16)
                stt = sbt(es, "stt", [128, 2, 6], F32)
                mv = sbt(es, "mv", [128, 8], F32)
                op("dve", lambda e: e.memset(S32[:], 0.0), writes=[S32.b])
                op("dve", lambda e: e.memset(S16[:], 0.0), writes=[S16.b])

                def load_x(t):
                    xt = xts[t % 3]
                    dma("sp", lambda e: e.dma_start(out=xt[:], in_=src_tile(first, t)), reads=[DX[t]] if not first else [],
                        writes=[xt.b], sembuf=xt.b)

                load_x(0)
                if NT > 1:
                    load_x(1)
                for t in range(NT):
                    if t + 2 < NT:
                        load_x(t + 2)
                    xt = xts[t % 3]
                    xo = xos[t % 2]
                    modulate_T(xt, scp, sh, h_, hT, 0, tmp)
                    for (bank, c0) in ((1, 0), (2, 512)):
                        for m in range(4):
                            for k in range(8):
                                op("pe", lambda e, bank=bank, c0=c0, m=m, k=k: e.matmul(
                                    PS[:, bank, m * 128:(m + 1) * 128], lhsT=win[:, k, c0 + m * 128:c0 + (m + 1) * 128], rhs=hT[:, k, :],
                                    start=(k == 0), stop=(k == 7)), reads=[Bwin[k], hT.b], writes=[PB[bank]])
                    for k in range(8):
                        op("pe", lambda e, k=k: e.matmul(PS[0:16, 3, 0:128], lhsT=win[:, k, 3072:3088], rhs=hT[:, k, :],
                                                         start=(k == 0), stop=(k == 7)), reads=[Bwin[k], hT.b], writes=[PB[3]])
                    op("act", lambda e: e.copy(out=glrT[:], in_=PS[0:16, 3, 0:128]), reads=[PB[3]], writes=[glrT.b])
                    for (bank, c0) in ((4, 512), (5, 1024), (6, 1536), (7, 2048), (0, 2560)):
                        for k in range(8):
                            op("pe", lambda e, bank=bank, c0=c0, k=k: e.matmul(
                                PS[:, bank, :], lhsT=hT[:, k, :], rhs=win[:, k, c0:c0 + 512], start=(k == 0), stop=(k == 7)),
                               reads=[Bwin[k], hT.b], writes=[PB[bank]])
                    op("dve", lambda e: e.tensor_copy(out=v_[:, 0:512], in_=PS[:, 5, :]), reads=[PB[5]], writes=[v_.b])
                    op("dve", lambda e: e.tensor_copy(out=v_[:, 512:1024], in_=PS[:, 6, :]), reads=[PB[6]], writes=[v_.b])
                    op("act", lambda e: e.activation(out=sr[:, 0:2, :], in_=PS[:, 7, :].rearrange("p (h d) -> p h d", h=2), func=ACT.Silu),
                       reads=[PB[7]], writes=[sr.b])
                    op("act", lambda e: e.activation(out=sr[:, 2:4, :], in_=PS[:, 0, :].rearrange("p (h d) -> p h d", h=2), func=ACT.Silu),
                       reads=[PB[0]], writes=[sr.b])
                    op("pool", lambda e: e.tensor_tensor(out=gr[:], in0=sr[:], in1=ngb[:], op=ALU.mult), reads=[sr.b, ngb.b], writes=[gr.b])
                    op("pe", lambda e: e.matmul(PS[:, 3, :], lhsT=glrT[:], rhs=wgu[:], start=True, stop=False),
                       reads=[glrT.b, wgu.b], writes=[PB[3]])
                    op("pe", lambda e: e.matmul(PS[:, 3, :], lhsT=ones_b[:], rhs=bgr[:], start=False, stop=True),
                       reads=[ones_b.b, bgr.b], writes=[PB[3]])
                    op("act", lambda e: e.activation(out=e1[:], in_=PS[:, 3, :], func=ACT.Exp, scale=-1.0), reads=[PB[3]], writes=[e1.b])
                    op("act", lambda e: e.activation(out=sp_[:], in_=e1[:], func=ACT.Ln, bias=1.0), reads=[e1.b], writes=[sp_.b])
                    for hh in range(4):
                        op("pe", lambda e, hh=hh: e.matmul(PS[:, 5, hh * 128:(hh + 1) * 128], lhsT=sp_[:, hh * 128:(hh + 1) * 128],
                                                           rhs=tri_incl, start=True, stop=True),
                           reads=[sp_.b, cst_f.b], writes=[PB[5]])
                    op("pe", lambda e: e.matmul(PS[:, 6, :], lhsT=tri_after, rhs=sp_[:], start=True, stop=True),
                       reads=[sp_.b, cst_f.b], writes=[PB[6]])
                    p5 = PS[:, 5, :].rearrange("p (h n) -> p h n", h=4)
                    op("act", lambda e: e.activation(out=E[:], in_=p5, func=ACT.Exp, scale=-1.0 / 16), reads=[PB[5]], writes=[E.b])
                    op("act", lambda e: e.activation(out=Einv[:], in_=p5, func=ACT.Exp, scale=1.0 / 16), reads=[PB[5]], writes=[Einv.b])
                    op("act", lambda e: e.activation(out=Ea[:], in_=PS[:, 6, :], func=ACT.Exp, scale=-1.0 / 16), reads=[PB[6]], writes=[Ea.b])
                    op("dve", lambda e: e.scalar_tensor_tensor(out=qeT[:], in0=PS[:, 1, :].rearrange("p (h n) -> p h n", h=4), scalar=128.0 ** -0.5,
                                                               in1=E[:], op0=ALU.mult, op1=ALU.mult), reads=[PB[1], E.b], writes=[qeT.b])
                    op("dve", lambda e: e.tensor_tensor(out=keT[:], in0=PS[:, 2, :].rearrange("p (h n) -> p h n", h=4), in1=Einv[:], op=ALU.mult),
                       reads=[PB[2], Einv.b], writes=[keT.b])
                    op("dve", lambda e: e.tensor_tensor(out=kd[:], in0=PS[:, 4, :], in1=Ea[:], op=ALU.mult), reads=[PB[4], Ea.b], writes=[kd.b])
                    for hh in range(4):
                        op("pe", lambda e, hh=hh: e.matmul(PS[:, 7, hh * 128:(hh + 1) * 128], lhsT=keT[:, hh, :], rhs=qeT[:, hh, :], start=True, stop=True),
                           reads=[keT.b, qeT.b], writes=[PB[7]])
                    op("dve", lambda e: e.tensor_tensor(out=attnT[:], in0=PS[:, 7, :].rearrange("p (h n) -> p h n", h=4), in1=mask4[:], op=ALU.mult),
                       reads=[PB[7], mask4.b], writes=[attnT.b])
                    for hh in range(4):
                        ob = hh // 2
                        oc = (hh % 2) * 256
                        op("pe", lambda e, hh=hh, ob=ob, oc=oc: e.matmul(PS[:, ob, oc:oc + 256], lhsT=attnT[:, hh, :], rhs=v_[:, hh * 256:(hh + 1) * 256],
                                                                         start=True, stop=False), reads=[attnT.b, v_.b], writes=[PB[ob]])
                        op("pe", lambda e, hh=hh, ob=ob, oc=oc: e.matmul(PS[:, ob, oc:oc + 256], lhsT=qeT[:, hh, :], rhs=S16[:, hh, :],
                                                                         start=False, stop=True), reads=[qeT.b, S16.b], writes=[PB[ob]])
                    for hh in range(4):
                        ob = 2 + hh // 2
                        oc = (hh % 2) * 256
                        op("pe", lambda e, hh=hh, ob=ob, oc=oc: e.matmul(PS[:, ob, oc:oc + 256], lhsT=kd[:, hh * 128:(hh + 1) * 128],
                                                                         rhs=v_[:, hh * 256:(hh + 1) * 256], start=True, stop=True),
                           reads=[kd.b, v_.b], writes=[PB[ob]])
                    for hh in range(4):
                        ob = 2 + hh // 2
                        oc = (hh % 2) * 256
                        op("dve", lambda e, hh=hh, ob=ob, oc=oc: e.scalar_tensor_tensor(
                            out=S32[:, hh, :], in0=S32[:, hh, :], scalar=E[:, hh, 127:128], in1=PS[:, ob, oc:oc + 256], op0=ALU.mult, op1=ALU.add),
                           reads=[S32.b, E.b, PB[ob]], writes=[S32.b])
                    op("pool", lambda e: e.tensor_copy(out=S16[:], in_=S32[:]), reads=[S32.b], writes=[S16.b])
                    ov = PS[:, 0:2, :].rearrange("p b (h d) -> p (b h) d", h=2)
                    op("act", lambda e: e.activation(out=osq[:], in_=ov, func=ACT.Square), reads=[PB[0], PB[1]], writes=[osq.b])
                    op("dve", lambda e: e.tensor_reduce(out=ss[:, 0:4], in_=osq[:], axis=AX.X, op=ALU.add), reads=[osq.b], writes=[ss.b])
                    op("act", lambda e: e.activation(out=ss[:, 4:8], in_=ss[:, 0:4], func=ACT.Ln, scale=1.0 / 256, bias=RMS_EPS),
                       reads=[ss.b], writes=[ss.b])
                    op("act", lambda e: e.activation(out=ss[:, 0:4], in_=ss[:, 4:8], func=ACT.Exp, scale=-0.5), reads=[ss.b], writes=[ss.b])
                    for hh in range(4):
                        ob = hh // 2
                        oc = (hh % 2) * 256
                        op("dve", lambda e, hh=hh, ob=ob, oc=oc: e.scalar_tensor_tensor(
                            out=og[:, hh * 256:(hh + 1) * 256], in0=PS[:, ob, oc:oc + 256], scalar=ss[:, hh:hh + 1], in1=gr[:, hh, :],
                            op0=ALU.mult, op1=ALU.mult), reads=[PB[ob], ss.b, gr.b], writes=[og.b])
                    pv = PS[:, 4, :].bitcast(BF16)
                    for k in range(8):
                        op("pe", lambda e, k=k: e.transpose(out=pv[:, k * 128:(k + 1) * 128], in_=og[:, k * 128:(k + 1) * 128], identity=ident_b[:]),
                           reads=[og.b, ident_b.b], writes=[PB[4]])
                    op("act", lambda e: e.copy(out=ogT[:], in_=pv.rearrange("p (k n) -> p k n", k=8)), reads=[PB[4]], writes=[ogT.b])
                    for cb in range(2):
                        for k in range(8):
                            op("pe", lambda e, cb=cb, k=k: e.matmul(PS[:, 5 + cb, :], lhsT=ogT[:, k, :], rhs=wout[:, k, cb * 512:(cb + 1) * 512],
                                                                    start=(k == 0), stop=(k == 7)), reads=[ogT.b, Bwout[k]], writes=[PB[5 + cb]])
                    resid_ln(xt, 5, gp, lng, lnb, tmp, z, xo, stt, mv)
                    dma("sp", lambda e, t=t, xo=xo: e.dma_start(out=x_dst(last)[t * 128:(t + 1) * 128, :], in_=xo[:]),
                        reads=[xo.b], writes=[DX[t]], sembuf=xo.b)
                S_.wait_all("sp", DX)
                flush()
                free_dsem([scp, sh, gp, lng, lnb, ngb, wgu_f, bgr_f, tmp, z] + xts + xos)

        def ffn_phase(l, last=False):
            first = state["first"]
            state["first"] = False
            with contextlib.ExitStack() as es:
                scp = load_bc(es, "scp", ada_d[l:l + 1, 4096:5120])
                sh = load_bc(es, "sh", ada_d[l:l + 1, 3072:4096])
                gp = load_bc(es, "gp", ada_d[l:l + 1, 5120:6144])
                lng = load_bc(es, "lng", ln_g[l, 1:2, :])
                lnb = load_bc(es, "lnb", ln_b[l, 1:2, :])
                xts = [sbt(es, f"xt{i}", [128, D], F32) for i in range(2)]
                xos = [sbt(es, f"xo{i}", [128, D], F32) for i in range(2)]
                tmp = sbt(es, "tmp", [128, D], F32)
                z = sbt(es, "z", [128, D], F32)
                stg = [tmp, z] + xos
                win, Bwin = load_w(es, "fwin", ffn_w_in[l], 8, 2 * FH, stg=stg)
                wout, Bwout = load_w(es, "fwout", ffn_w_out[l], 22, D, stg=stg)
                h_ = sbt(es, "h", [128, D], BF16)
                hT = sbt(es, "hT", [128, 8, 128], BF16)
                sg = sbt(es, "sg", [128, 512], F32)
                a_ = sbt(es, "a", [128, FH], BF16)
                aT = sbt(es, "aT", [128, 22, 128], BF16)
                stt = sbt(es, "stt", [128, 2, 6], F32)
                mv = sbt(es, "mv", [128, 8], F32)

                def load_x(t):
                    xt = xts[t % 2]
                    dma("sp", lambda e: e.dma_start(out=xt[:], in_=src_tile(first, t)), reads=[DX[t]] if not first else [],
                        writes=[xt.b], sembuf=xt.b)

                load_x(0)
                for t in range(NT):
                    if t + 1 < NT:
                        load_x(t + 1)
                    xt = xts[t % 2]
                    xo = xos[t % 2]
                    modulate_T(xt, scp, sh, h_, hT, 0, tmp)
                    for j in range(6):
                        wd = 512 if j < 5 else 256
                        gb = 1 + 2 * (j % 2)
                        ub = gb + 1
                        for (bank, c0) in ((gb, j * 512), (ub, FH + j * 512)):
                            for k in range(8):
                                op("pe", lambda e, bank=bank, c0=c0, k=k, wd=wd: e.matmul(
                                    PS[:, bank, 0:wd], lhsT=hT[:, k, :], rhs=win[:, k, c0:c0 + wd], start=(k == 0), stop=(k == 7)),
                                   reads=[hT.b, Bwin[k]], writes=[PB[bank]])
                        op("act", lambda e, gb=gb, wd=wd: e.activation(out=sg[:, 0:wd], in_=PS[:, gb, 0:wd], func=ACT.Silu),
                           reads=[PB[gb]], writes=[sg.b])
                        op("dve", lambda e, ub=ub, wd=wd, j=j: e.tensor_tensor(out=a_[:, j * 512:j * 512 + wd], in0=PS[:, ub, 0:wd], in1=sg[:, 0:wd], op=ALU.mult),
                           reads=[PB[ub], sg.b], writes=[a_.b])
                    for rnd in range(3):
                        bank = 5 if rnd % 2 == 0 else 0
                        n = 8 if rnd < 2 else 6
                        pv = PS[:, bank, :].bitcast(BF16)
                        for i in range(n):
                            kk = rnd * 8 + i
                            op("pe", lambda e, i=i, kk=kk, pv=pv: e.transpose(out=pv[:, i * 128:(i + 1) * 128], in_=a_[:, kk * 128:(kk + 1) * 128], identity=ident_b[:]),
                               reads=[a_.b, ident_b.b], writes=[PB[bank]])
                        op("act", lambda e, rnd=rnd, n=n, pv=pv: e.copy(out=aT[:, rnd * 8:rnd * 8 + n, :], in_=pv[:, 0:n * 128].rearrange("p (k n) -> p k n", k=n)),
                           reads=[PB[bank]], writes=[aT.b])
                    for cb in range(2):
                        for k in range(22):
                            op("pe", lambda e, cb=cb, k=k: e.matmul(PS[:, 6 + cb, :], lhsT=aT[:, k, :], rhs=wout[:, k, cb * 512:(cb + 1) * 512],
                                                                    start=(k == 0), stop=(k == 21)), reads=[aT.b, Bwout[k]], writes=[PB[6 + cb]])
                    resid_ln(xt, 6, gp, lng, lnb, tmp, z, xo, stt, mv)
                    dma("sp", lambda e, t=t, xo=xo: e.dma_start(out=x_dst(last)[t * 128:(t + 1) * 128, :], in_=xo[:]),
                        reads=[xo.b], writes=[DX[t]], sembuf=xo.b)
                S_.wait_all("sp", DX)
                flush()
                free_dsem([scp, sh, gp, lng, lnb, tmp, z] + xts + xos)


        def proj_phase(row, wsrc, nfm, fm_col0, dstT, BdT, fm_scale, tok_cols=None):
            with contextlib.ExitStack() as es:
                scp = load_bc(es, "scp", ada_d[row:row + 1, 1024:2048])
                sh = load_bc(es, "sh", ada_d[row:row + 1, 0:1024])
                ncols = fm_col0 + nfm * 128 if tok_cols is None else max(fm_col0 + nfm * 128, tok_cols[0] + tok_cols[1])
                xts = [sbt(es, f"xt{i}", [128, D], F32) for i in range(3)]
                tmp = sbt(es, "tmp", [128, D], F32)
                stg2 = sbt(es, "stg2", [128, D], F32)
                w, Bw = load_w(es, "pw", wsrc, 8, ncols, stg=[tmp, stg2])
                h_ = sbt(es, "h", [128, D], BF16)
                hT4 = [sbt(es, f"hT4{i}", [128, 8, 512], BF16) for i in range(2)]
                fmo = [sbt(es, f"fmo{i}", [128, nfm, 512], BF16) for i in range(2)]
                vo = [sbt(es, f"vo{i}", [128, D], BF16) for i in range(2)]

                def load_x(t):
                    xt = xts[t % 3]
                    dma("sp", lambda e: e.dma_start(out=xt[:], in_=xs[t * 128:(t + 1) * 128, :]), reads=[DX[t]],
                        writes=[xt.b], sembuf=xt.b)
                load_x(0)
                load_x(1)
                for blk in range(S // 512):
                    hT = hT4[blk % 2]
                    fo = fmo[blk % 2]
                    for tt in range(4):
                        t = blk * 4 + tt
                        if t + 2 < NT:
                            load_x(t + 2)
                        modulate_T(xts[t % 3], scp, sh, h_, hT, 0, tmp, col0=tt * 128)
                        if tok_cols is not None:
                            v16 = vo[t % 2]
                            for cb in range(2):
                                for k in range(8):
                                    op("pe", lambda e, cb=cb, k=k, hT=hT, tt=tt: e.matmul(
                                        PS[:, 1 + cb, :], lhsT=hT[:, k, tt * 128:(tt + 1) * 128],
                                        rhs=w[:, k, tok_cols[0] + cb * 512:tok_cols[0] + (cb + 1) * 512], start=(k == 0), stop=(k == 7)),
                                       reads=[hT.b, Bw[k]], writes=[PB[1 + cb]])
                            op("dve", lambda e, v16=v16: e.tensor_copy(out=v16[:], in_=PS[:, 1:3, :].rearrange("p b n -> p (b n)")),
                               reads=[PB[1], PB[2]], writes=[v16.b])
                            dma("sp", lambda e, v16=v16, t=t: e.dma_start(out=v_d[t * 128:(t + 1) * 128, :], in_=v16[:]),
                                reads=[v16.b], writes=[B_vd[t]], sembuf=v16.b)
                    for m in range(nfm):
                        bank = 3 + (m % 4)
                        for k in range(8):
                            op("pe", lambda e, bank=bank, m=m, k=k, hT=hT: e.matmul(
                                PS[:, bank, :], lhsT=w[:, k, fm_col0 + m * 128:fm_col0 + (m + 1) * 128], rhs=hT[:, k, :],
                                start=(k == 0), stop=(k == 7)), reads=[hT.b, Bw[k]], writes=[PB[bank]])
                        eng = "act" if m % 2 == 0 else "dve"
                        if eng == "act":
                            op("act", lambda e, bank=bank, m=m, fo=fo: e.activation(out=fo[:, m, :], in_=PS[:, bank, :], func=ACT.Identity, scale=fm_scale),
                               reads=[PB[bank]], writes=[fo.b])
                        else:
                            op("dve", lambda e, bank=bank, m=m, fo=fo: e.tensor_scalar_mul(out=fo[:, m, :], in0=PS[:, bank, :], scalar1=fm_scale),
                               reads=[PB[bank]], writes=[fo.b])
                    dma("sp", lambda e, fo=fo, blk=blk: e.dma_start(
                        out=dstT[:, blk * 512:(blk + 1) * 512].rearrange("(m p) n -> p m n", p=128), in_=fo[:]),
                        reads=[fo.b], writes=[BdT[blk]], sembuf=fo.b)
                S_.wait_all("sp", BdT + (B_vd if tok_cols is not None else []))
                flush()
                free_dsem([scp, sh, tmp, stg2] + xts + fmo + vo)

        def attn_phase():
            NMS = S // 2048
            with contextlib.ExitStack() as es:
                maskb = sbt(es, "maskb", [128, 4, 256], F32)
                maskb0 = sbt(es, "maskb0", [128, 4, 256], F32)
                for hh in range(4):
                    op("dve", lambda e, hh=hh: e.tensor_copy(out=maskb[:, hh, 0:128], in_=cst_f[:, 3, :]), reads=[cst_f.b], writes=[maskb.b])
                    op("dve", lambda e, hh=hh: e.tensor_copy(out=maskb[:, hh, 128:256], in_=cst_f[:, 4, :]), reads=[cst_f.b], writes=[maskb.b])
                    op("dve", lambda e, hh=hh: e.tensor_copy(out=maskb0[:, hh, 0:128], in_=cst_f[:, 5, :]), reads=[cst_f.b], writes=[maskb0.b])
                    op("dve", lambda e, hh=hh: e.tensor_copy(out=maskb0[:, hh, 128:256], in_=cst_f[:, 4, :]), reads=[cst_f.b], writes=[maskb0.b])
                kTb = [sbt(es, f"kTb{i}", [128, 8, 2048], BF16) for i in range(2)]
                qTb = [sbt(es, f"qTb{i}", [128, 8, 2048], BF16) for i in range(2)]
                vvs = [sbt(es, f"vv{i}", [128, 2, D], BF16) for i in range(3)]
                sm = [sbt(es, f"sm{i}", [128, 4, 256], F32) for i in range(2)]
                pbf = [sbt(es, f"pbf{i}", [128, 4, 256], BF16) for i in range(2)]
                pTs = [sbt(es, f"pTs{i}", [128, 8, 128], BF16) for i in range(2)]
                mdt = [sbt(es, f"mdt{i}", [128, 32], F32) for i in range(2)]
                negm = [sbt(es, f"negm{i}", [128, 8], F32) for i in range(2)]
                rden = sbt(es, "rden", [128, 16], F32)
                ogt = [sbt(es, f"ogt{i}", [128, 16, 64], BF16) for i in range(2)]
                qcnt = 0
                ucnt = 0
                hcnt = 0
                for ms in range(NMS):
                    base = ms * 2048
                    kc = kTb[ms % 2]
                    kp = kTb[(ms + 1) % 2]
                    dma("sp", lambda e, kc=kc, base=base: e.dma_start(
                        out=kc[:], in_=kT_d[:, base:base + 2048].rearrange("(m p) n -> p m n", p=128)),
                        reads=B_kT, writes=[kc.b], sembuf=kc.b)
                    for g, d in enumerate(DILS):
                        if ('g0' in DBG and g != 0) or ('g1' in DBG and g != 1) or ('g2' in DBG and g != 2):
                            continue
                        qb = qTb[qcnt % 2]
                        qcnt += 1
                        dma("sp", lambda e, qb=qb, g=g, base=base: e.dma_start(
                            out=qb[:], in_=qT_d[g * D:(g + 1) * D, base:base + 2048].rearrange("(m p) n -> p m n", p=128)),
                            reads=B_qT, writes=[qb.b], sembuf=qb.b)
                        nbk = 16 // d
                        vview = v_d.rearrange("(n dd) c -> dd n c", dd=d)
                        ogview = og_d[g].rearrange("(n dd) c -> dd n c", dd=d)
                        mdview = md_d[g].rearrange("(n dd) c -> dd n c", dd=d)
                        for r in range(d if 'nounits' not in DBG else 0):
                            for b in range(nbk):
                                gb0 = (ms == 0 and b == 0)
                                n0 = base // d + b * 128
                                vv = vvs[ucnt % 3]
                                md = mdt[ucnt % 2]
                                og_t = ogt[ucnt % 2]
                                ucnt += 1
                                if 'nov' in DBG:
                                    pass
                                elif gb0:
                                    dma("sp", lambda e, vv=vv, r=r, n0=n0, vview=vview: e.dma_start(out=vv[:, 1, :], in_=vview[r, n0:n0 + 128, :]),
                                        reads=B_vd, writes=[vv.b], sembuf=vv.b)
                                else:
                                    dma("sp", lambda e, vv=vv, r=r, n0=n0, vview=vview: e.dma_start(
                                        out=vv[:], in_=vview[r, n0 - 128:n0 + 128, :].rearrange("(two p) c -> p two c", p=128)),
                                        reads=B_vd, writes=[vv.b], sembuf=vv.b)
                                q0 = r + b * 128 * d
                                if b >= 1:
                                    ksrc, kp0 = kc, r + (b - 1) * 128 * d
                                elif not gb0:
                                    ksrc, kp0 = kp, r + (nbk - 1) * 128 * d
                                else:
                                    ksrc, kp0 = kc, q0
                                for hg in range(4):
                                    sb0 = 2 * (hcnt % 2)
                                    tb = 4 + (hcnt % 2)
                                    smt = sm[hcnt % 2]
                                    pb_ = pbf[hcnt % 2]
                                    pT = pTs[hcnt % 2]
                                    ng = negm[hcnt % 2]
                                    hcnt += 1
                                    for hh in range(4):
                                        hd = hg * 4 + hh
                                        c = hd // 2
                                        pr = (hd % 2) * 64
                                        bank = sb0 + hh % 2
                                        co = (hh // 2) * 256
                                        op("pe", lambda e, bank=bank, co=co, pr=pr, c=c, ksrc=ksrc, kp0=kp0, qb=qb, q0=q0, d=d: e.matmul(
                                            PS[:, bank, co:co + 128], lhsT=qb[pr:pr + 64, c, q0:q0 + 127 * d + 1:d],
                                            rhs=ksrc[pr:pr + 64, c, kp0:kp0 + 127 * d + 1:d], start=True, stop=True),
                                           reads=[qb.b, ksrc.b], writes=[PB[bank]])
                                        op("pe", lambda e, bank=bank, co=co, pr=pr, c=c, kc=kc, qb=qb, q0=q0, d=d: e.matmul(
                                            PS[:, bank, co + 128:co + 256], lhsT=qb[pr:pr + 64, c, q0:q0 + 127 * d + 1:d],
                                            rhs=kc[pr:pr + 64, c, q0:q0 + 127 * d + 1:d], start=True, stop=True),
                                           reads=[qb.b, kc.b], writes=[PB[bank]])
                                    mk = maskb0 if gb0 else maskb
                                    op("dve", lambda e, sb0=sb0, smt=smt, mk=mk: e.tensor_tensor(
                                        out=smt[:], in0=PS[:, sb0:sb0 + 2, :].rearrange("p b (h n) -> p (b h) n", h=2), in1=mk[:], op=ALU.add),
                                       reads=[PB[sb0], PB[sb0 + 1], mk.b], writes=[smt.b])
                                    HS = (0, 2, 1, 3)
                                    op("dve", lambda e, smt=smt, ng=ng: e.tensor_reduce(out=ng[:, 4:8], in_=smt[:], axis=AX.X, op=ALU.max),
                                       reads=[smt.b], writes=[ng.b])
                                    op("dve", lambda e, md=md, hg=hg, ng=ng: e.tensor_copy(out=md[:, hg * 4:hg * 4 + 3:2], in_=ng[:, 4:6]),
                                       reads=[ng.b], writes=[md.b])
                                    op("dve", lambda e, md=md, hg=hg, ng=ng: e.tensor_copy(out=md[:, hg * 4 + 1:hg * 4 + 4:2], in_=ng[:, 6:8]),
                                       reads=[ng.b], writes=[md.b])
                                    op("dve", lambda e, ng=ng: e.tensor_scalar_mul(out=ng[:, 0:4], in0=ng[:, 4:8], scalar1=-1.0),
                                       reads=[ng.b], writes=[ng.b])
                                    if 'stopA' in DBG:
                                        continue
                                    for hh in range(4):
                                        hd = hg * 4 + HS[hh]
                                        op("act", lambda e, hh=hh, hd=hd, smt=smt, pb_=pb_, ng=ng, md=md: e.activation(
                                            out=pb_[:, hh, :], in_=smt[:, hh, :], func=ACT.Exp, bias=ng[:, hh:hh + 1], accum_out=md[:, 16 + hd:17 + hd]),
                                           reads=[smt.b, ng.b], writes=[pb_.b, md.b])
                                    pv = PS[:, tb, :].bitcast(BF16)
                                    for hh in range(4):
                                        for half in range(2):
                                            i8 = hh * 2 + half
                                            op("pe", lambda e, i8=i8, hh=hh, half=half, pb_=pb_, pv=pv: e.transpose(
                                                out=pv[:, i8 * 128:(i8 + 1) * 128], in_=pb_[:, hh, half * 128:(half + 1) * 128], identity=ident_b[:]),
                                               reads=[pb_.b, ident_b.b], writes=[PB[tb]])
                                    op("act", lambda e, pT=pT, pv=pv: e.copy(out=pT[:], in_=pv.rearrange("p (k n) -> p k n", k=8)),
                                       reads=[PB[tb]], writes=[pT.b])
                                    if 'stopB' in DBG:
                                        continue
                                    for hh in range(4):
                                        hd = hg * 4 + HS[hh]
                                        ob = 6 + hd // 8
                                        oc = (hd % 8) * 64
                                        if not gb0:
                                            op("pe", lambda e, ob=ob, oc=oc, hh=hh, hd=hd, pT=pT, vv=vv: e.matmul(
                                                PS[:, ob, oc:oc + 64], lhsT=pT[:, hh * 2, :], rhs=vv[:, 0, hd * 64:(hd + 1) * 64], start=True, stop=False),
                                               reads=[pT.b, vv.b], writes=[PB[ob]])
                                        op("pe", lambda e, ob=ob, oc=oc, hh=hh, hd=hd, pT=pT, vv=vv, gb0=gb0: e.matmul(
                                            PS[:, ob, oc:oc + 64], lhsT=pT[:, hh * 2 + 1, :], rhs=vv[:, 1, hd * 64:(hd + 1) * 64], start=gb0, stop=True),
                                           reads=[pT.b, vv.b], writes=[PB[ob]])
                                if 'stopA' in DBG or 'stopB' in DBG or 'stopC' in DBG:
                                    continue
                                op("dve", lambda e, md=md: e.reciprocal(out=rden[:], in_=md[:, 16:32]), reads=[md.b], writes=[rden.b])
                                op("dve", lambda e, og_t=og_t: e.tensor_tensor(
                                    out=og_t[:], in0=PS[:, 6:8, :].rearrange("p b (h n) -> p (b h) n", h=8),
                                    in1=rden[:].unsqueeze(2).to_broadcast([128, 16, 64]), op=ALU.mult),
                                   reads=[PB[6], PB[7], rden.b], writes=[og_t.b])
                                if 'noog' not in DBG:
                                    dma("sp", lambda e, og_t=og_t, ogview=ogview, r=r, n0=n0: e.dma_start(
                                        out=ogview[r, n0:n0 + 128, :], in_=og_t[:].rearrange("p h n -> p (h n)")),
                                        reads=[og_t.b], writes=[B_og], sembuf=og_t.b)
                                if 'nomd' not in DBG:
                                    dma("sp", lambda e, md=md, mdview=mdview, r=r, n0=n0: e.dma_start(out=mdview[r, n0:n0 + 128, :], in_=md[:]),
                                        reads=[md.b], writes=[B_og], sembuf=md.b)
                S_.wait_all("sp", [B_og])
                flush()
                free_dsem(kTb + qTb + vvs + mdt + ogt)

        def comb_phase(l, last=False):
            li = l - 2
            with contextlib.ExitStack() as es:
                gp = load_bc(es, "gp", ada_d[l:l + 1, 2048:3072])
                lng = load_bc(es, "lng", ln_g[l, 0:1, :])
                lnb = load_bc(es, "lnb", ln_b[l, 0:1, :])
                xts = [sbt(es, f"xt{i}", [128, D], F32) for i in range(2)]
                xos = [sbt(es, f"xo{i}", [128, D], F32) for i in range(2)]
                wout, Bwout = load_w(es, "dwout", dil_w_out[li], 8, D, stg=xos)
                ogs = [[sbt(es, f"ogl{i}_{g}", [128, 16, 64], BF16) for g in range(3)] for i in range(2)]
                mds = [[sbt(es, f"mdl{i}_{g}", [128, 32], F32) for g in range(3)] for i in range(2)]
                tmp = sbt(es, "tmp", [128, D], F32)
                z = sbt(es, "z", [128, D], F32)
                o_ = sbt(es, "o", [128, D], BF16)
                oT = sbt(es, "oT", [128, 8, 128], BF16)
                M = sbt(es, "M", [128, 16], F32)
                ew = sbt(es, "ew", [128, 3, 16], F32)
                W = sbt(es, "W", [128, 16], F32)
                stt = sbt(es, "stt", [128, 2, 6], F32)
                mv = sbt(es, "mv", [128, 8], F32)

                def load(t):
                    i = t % 2
                    dma("sp", lambda e: e.dma_start(out=xts[i][:], in_=xs[t * 128:(t + 1) * 128, :]), reads=[DX[t]],
                        writes=[xts[i].b], sembuf=xts[i].b)
                    for g in range(3):
                        dma("sp", lambda e, g=g: e.dma_start(out=ogs[i][g][:].rearrange("p h n -> p (h n)"), in_=og_d[g][t * 128:(t + 1) * 128, :]),
                            reads=[B_og], writes=[ogs[i][g].b], sembuf=ogs[i][g].b)
                        dma("sp", lambda e, g=g: e.dma_start(out=mds[i][g][:], in_=md_d[g][t * 128:(t + 1) * 128, :]),
                            reads=[B_og], writes=[mds[i][g].b], sembuf=mds[i][g].b)
                load(0)
                for t in range(NT):
                    if t + 1 < NT:
                        load(t + 1)
                    i = t % 2
                    xt, xo, og3, md3 = xts[i], xos[i], ogs[i], mds[i]
                    op("dve", lambda e, md3=md3: e.tensor_tensor(out=M[:], in0=md3[0][:, 0:16], in1=md3[1][:, 0:16], op=ALU.max),
                       reads=[md3[0].b, md3[1].b], writes=[M.b])
                    op("dve", lambda e, md3=md3: e.tensor_tensor(out=M[:], in0=M[:], in1=md3[2][:, 0:16], op=ALU.max),
                       reads=[M.b, md3[2].b], writes=[M.b])
                    for g in range(3):
                        op("dve", lambda e, g=g, md3=md3: e.tensor_tensor(out=ew[:, g, :], in0=md3[g][:, 0:16], in1=M[:], op=ALU.subtract),
                           reads=[md3[g].b, M.b], writes=[ew.b])
                    op("act", lambda e: e.activation(out=ew[:], in_=ew[:], func=ACT.Exp), reads=[ew.b], writes=[ew.b])
                    for g in range(3):
                        op("dve", lambda e, g=g, md3=md3: e.tensor_tensor(out=ew[:, g, :], in0=ew[:, g, :], in1=md3[g][:, 16:32], op=ALU.mult),
                           reads=[ew.b, md3[g].b], writes=[ew.b])
                    op("dve", lambda e: e.tensor_tensor(out=W[:], in0=ew[:, 0, :], in1=ew[:, 1, :], op=ALU.add), reads=[ew.b], writes=[W.b])
                    op("dve", lambda e: e.tensor_tensor(out=W[:], in0=W[:], in1=ew[:, 2, :], op=ALU.add), reads=[ew.b, W.b], writes=[W.b])
                    op("dve", lambda e: e.reciprocal(out=W[:], in_=W[:]), reads=[W.b], writes=[W.b])
                    for g in range(3):
                        op("dve", lambda e, g=g: e.tensor_tensor(out=ew[:, g, :], in0=ew[:, g, :], in1=W[:], op=ALU.mult),
                           reads=[ew.b, W.b], writes=[ew.b])
                    tv = tmp[:].rearrange("p (h n) -> p h n", h=16)
                    zv = z[:].rearrange("p (h n) -> p h n", h=16)
                    op("dve", lambda e, og3=og3: e.tensor_tensor(out=tv, in0=og3[0][:], in1=ew[:, 0, :].unsqueeze(2).to_broadcast([128, 16, 64]), op=ALU.mult),
                       reads=[og3[0].b, ew.b], writes=[tmp.b])
                    op("pool", lambda e, og3=og3: e.tensor_tensor(out=zv, in0=og3[1][:], in1=ew[:, 1, :].unsqueeze(2).to_broadcast([128, 16, 64]), op=ALU.mult),
                       reads=[og3[1].b, ew.b], writes=[z.b])
                    op("dve", lambda e: e.tensor_tensor(out=tmp[:], in0=tmp[:], in1=z[:], op=ALU.add), reads=[tmp.b, z.b], writes=[tmp.b])
                    op("pool", lambda e, og3=og3: e.tensor_tensor(out=zv, in0=og3[2][:], in1=ew[:, 2, :].unsqueeze(2).to_broadcast([128, 16, 64]), op=ALU.mult),
                       reads=[og3[2].b, ew.b], writes=[z.b])
                    op("dve", lambda e: e.tensor_tensor(out=o_[:], in0=tmp[:], in1=z[:], op=ALU.add), reads=[tmp.b, z.b], writes=[o_.b])
                    pv = PS[:, 0, :].bitcast(BF16)
                    for k in range(8):
                        op("pe", lambda e, k=k: e.transpose(out=pv[:, k * 128:(k + 1) * 128], in_=o_[:, k * 128:(k + 1) * 128], identity=ident_b[:]),
                           reads=[o_.b, ident_b.b], writes=[PB[0]])
                    op("act", lambda e: e.copy(out=oT[:], in_=pv.rearrange("p (k n) -> p k n", k=8)), reads=[PB[0]], writes=[oT.b])
                    yb = 1 + 2 * (t % 2)
                    for cb in range(2):
                        for k in range(8):
                            op("pe", lambda e, cb=cb, k=k, yb=yb: e.matmul(PS[:, yb + cb, :], lhsT=oT[:, k, :], rhs=wout[:, k, cb * 512:(cb + 1) * 512],
                                                                           start=(k == 0), stop=(k == 7)), reads=[oT.b, Bwout[k]], writes=[PB[yb + cb]])
                    resid_ln(xt, yb, gp, lng, lnb, tmp, z, xo, stt, mv)
                    dma("sp", lambda e, t=t, xo=xo: e.dma_start(out=x_dst(last)[t * 128:(t + 1) * 128, :], in_=xo[:]),
                        reads=[xo.b], writes=[DX[t]], sembuf=xo.b)
                S_.wait_all("sp", DX)
                flush()
                free_dsem([gp, lng, lnb] + xts + xos + [a for b_ in ogs for a in b_] + [a for b_ in mds for a in b_])

        phases = []
        for l in range(2):
            phases.append(("gla%d" % l, lambda last, l=l: gla_phase(l, last)))
            phases.append(("ffn%d" % l, lambda last, l=l: ffn_phase(l, last)))
        for l in (2, 3):
            def dil(last, l=l):
                if l == 2:
                    proj_phase(4, w_kv, 8, 0, kT_d, B_kT, 1.0, tok_cols=(1024, 1024))
                if sub == "kv":
                    return
                proj_phase(l, dil_w_q[l - 2], 24, 0, qT_d, B_qT, 0.125)
                if sub == "q":
                    return
                attn_phase()
                if sub == "attn":
                    return
                comb_phase(l, last)
            phases.append(("dil%d" % l, dil))
            phases.append(("ffn%d" % l, lambda last, l=l: ffn_phase(l, last)))
        names = [p[0] for p in phases]
        sub = None
        if stop_after is not None and ":" in stop_after:
            stop_after, sub = stop_after.split(":")
        stop_idx = len(phases) - 1 if stop_after is None else names.index(stop_after)
        for i, (nm, fn) in enumerate(phases[:stop_idx + 1]):
            fn(i == stop_idx)
        print("instructions:", S_.ninst)
    return nc


def make_consts():
    cst = np.zeros((128, 6, 128), np.float32)
    j = np.arange(128)[:, None]
    i = np.arange(128)[None, :]
    cst[:, 0, :] = np.eye(128)
    cst[:, 1, :] = (j <= i)
    cst[:, 2, :] = (j > i)
    cst[:, 3, :] = np.where(i >= j, 0.0, NEG)
    cst[:, 4, :] = np.where(i <= j, 0.0, NEG)
    cst[:, 5, :] = NEG
    return cst


def make_in_maps(inputs, nb, S):
    cst = make_consts()
    shared = {k: np.ascontiguousarray(v) for k, v in inputs.items() if k not in ("x", "c")}
    shared["kv_ada_b"] = shared["kv_ada_b"].reshape(1, -1)
    shared["cst"] = cst
    maps = []
    for b in range(nb):
        m = dict(shared)
        m["x"] = np.ascontiguousarray(inputs["x"][b, :S])
        m["c_t"] = np.ascontiguousarray(inputs["c"][b].reshape(8, 128).T)
        maps.append(m)
    return maps


def kernel(**inputs):
    S = inputs["x"].shape[1]
    nc = build_nc(S)
    maps = make_in_maps(inputs, 8, S)
    res = run_bass_kernel_spmd(nc, maps, core_ids=list(range(8)))
    return np.stack([r["out"] for r in res.results], axis=0)
```

```python
import contextlib
import os
DBG = os.environ.get('KDBG', '')
import numpy as np
import concourse.bass as bass
import concourse.mybir as mybir
from concourse.bass_utils import run_bass_kernel_spmd

F32 = mybir.dt.float32
BF16 = mybir.dt.bfloat16
ACT = mybir.ActivationFunctionType
ALU = mybir.AluOpType
AX = mybir.AxisListType

D = 1024
DEPTH = 4
FH = 2816
ALPHA = (2.0 * DEPTH) ** 0.25
LN_EPS = 1e-5
RMS_EPS = 1e-5
GW = 3088
DILS = (1, 4, 16)
NEG = -30000.0

COMPUTE = ("pe", "act", "dve", "pool")
ALL = COMPUTE + ("sp",)


class Buf:
    __slots__ = ("name", "w", "r", "dsem", "dcnt")

    def __init__(self, name):
        self.name = name
        self.w = None
        self.r = {}
        self.dsem = None
        self.dcnt = 0


class Sched:
    def __init__(self, nc, esems, dma_sems):
        self.nc = nc
        self.ops = {e: [] for e in ALL}
        self.seq = {e: 0 for e in COMPUTE}
        self.esem = esems
        self.free_dsems = [(s_, 0) for s_ in dma_sems]
        self.known = {e: {} for e in ALL}
        self.semobj = dict(esems)
        self.ninst = 0

    def _need(self, eng, tok, acc):
        if tok is None:
            return
        k, v = tok
        if k == eng and eng == "pe":
            return
        if acc.get(k, 0) < v:
            acc[k] = v

    @staticmethod
    def _flat(lst):
        out = []
        for b in lst:
            if isinstance(b, (list, tuple)):
                out.extend(b)
            else:
                out.append(b)
        return out

    def _deps(self, eng, reads, writes):
        acc = {}
        for b in reads:
            self._need(eng, b.w, acc)
        for b in writes:
            self._need(eng, b.w, acc)
            for k, v in b.r.items():
                self._need(eng, (k, v), acc)
        kn = self.known[eng]
        for k, v in acc.items():
            if kn.get(k, 0) < v:
                kn[k] = v
                self.ops[eng].append(("wait", self.semobj[k], v))
                self.ninst += 1

    def _mark(self, tok, reads, writes):
        k, v = tok
        for b in reads:
            if b.r.get(k, 0) < v:
                b.r[k] = v
        for b in writes:
            b.w = tok
            b.r = {}

    def op(self, eng, fn, reads=(), writes=()):
        reads = self._flat(reads)
        writes = self._flat(writes)
        self._deps(eng, reads, writes)
        self.seq[eng] += 1
        tok = (eng, self.seq[eng])
        self.ops[eng].append(("op", fn, self.esem[eng]))
        self.ninst += 1
        self._mark(tok, reads, writes)
        return tok

    def dma(self, q, fn, reads=(), writes=(), sembuf=None):
        reads = self._flat(reads)
        writes = self._flat(writes)
        self._deps(q, reads, writes)
        b = sembuf
        if b.dsem is None:
            b.dsem, b.dcnt = self.free_dsems.pop()
            self.semobj[("d", b.name)] = b.dsem
        b.dcnt += 16
        tok = (("d", b.name), b.dcnt)
        self.ops[q].append(("dma", fn, b.dsem))
        self.ninst += 1
        self._mark(tok, reads, writes)
        return tok

    def wait_all(self, eng, bufs):
        self._deps(eng, (), bufs)

    def emit(self, block):
        amap = {"pe": block.tensor, "act": block.scalar, "dve": block.vector,
                "pool": block.gpsimd, "sp": block.sync}
        for e in ALL:
            lst = self.ops[e]

            def body(engobj, lst=lst):
                for item in lst:
                    if item[0] == "wait":
                        engobj.wait_ge(item[1], item[2])
                    elif item[0] == "op":
                        item[1](engobj).then_inc(item[2], 1)
                    else:
                        item[1](engobj).then_inc(item[2], 16)
            if lst:
                amap[e](body)
            self.ops[e] = []


class T:
    def __init__(self, t, name):
        self.t = t
        self.b = Buf(name)

    def __getitem__(self, k):
        return self.t[k]


def build_nc(S, stop_after=None, dbg=False, attn_only=False):
    NT = S // 128
    nc = bass.Bass("TRN2", target_bir_lowering=False)

    def din(name, shape):
        if attn_only and name != "cst":
            return nc.dram_tensor(name, list(shape), F32).ap()
        return nc.dram_tensor(name, list(shape), F32, kind="ExternalInput").ap()

    x_in = din("x", [S, D])
    c_t = din("c_t", [128, 8])
    gla_w_in = din("gla_w_in", [2, D, GW])
    gla_wgu = din("gla_w_gate_up", [2, 16, 512])
    gla_bg = din("gla_b_gate", [2, 512])
    gla_ng = din("gla_norm_g", [2, 256])
    gla_w_out = din("gla_w_out", [2, D, D])
    dil_w_q = din("dil_w_q", [2, D, 3 * D])
    dil_w_out = din("dil_w_out", [2, D, D])
    kv_ada_w = din("kv_ada_w", [D, 2 * D])
    kv_ada_b = din("kv_ada_b", [1, 2 * D])
    w_kv = din("w_kv", [D, 2 * D])
    ffn_w_in = din("ffn_w_in", [4, D, 2 * FH])
    ffn_w_out = din("ffn_w_out", [4, FH, D])
    ada_w = din("ada_w", [4, D, 6 * D])
    ada_b = din("ada_b", [4, 6 * D])
    ln_g = din("ln_g", [4, 2, D])
    ln_b = din("ln_b", [4, 2, D])
    cst = din("cst", [128, 6, 128])
    out = nc.dram_tensor("out", [S, D], F32, kind="ExternalOutput").ap()

    xs = nc.dram_tensor("xs", [S, D], F32).ap()
    ada_d = nc.dram_tensor("ada_d", [5, 6 * D], F32).ap()
    kin_ = {"kind": "ExternalInput"} if attn_only else {}
    kout_ = {"kind": "ExternalOutput"} if attn_only else {}
    kT_d = nc.dram_tensor("kT_d", [D, S], BF16, **kin_).ap()
    v_d = nc.dram_tensor("v_d", [S, D], BF16, **kin_).ap()
    qT_d = nc.dram_tensor("qT_d", [3 * D, S], BF16, **kin_).ap()
    og_d = [nc.dram_tensor(f"og_d{g}", [S, D], BF16, **kout_).ap() for g in range(3)]
    md_d = [nc.dram_tensor(f"md_d{g}", [S, 32], F32, **kout_).ap() for g in range(3)]

    DX = [Buf(f"dx{t}") for t in range(NT)]
    B_ada = Buf("ada_d")
    B_kT = [Buf(f"dkT{i}") for i in range(max(1, S // 512))]
    B_vd = [Buf(f"dv{t}") for t in range(NT)]
    B_qT = [Buf(f"dqT{i}") for i in range(max(1, S // 512))]
    B_og = Buf("dog")

    with contextlib.ExitStack() as es0:
        esems = {e: es0.enter_context(nc.semaphore("s_" + e)) for e in COMPUTE}
        dsems = [es0.enter_context(nc.semaphore(f"d{i}")) for i in range(92)]
        S_ = Sched(nc, esems, dsems)
        op = S_.op
        dma = S_.dma
        uid = [0]

        def flush():
            for e_ in ALL:
                for k_ in COMPUTE:
                    if k_ != e_ and S_.known[e_].get(k_, 0) < S_.seq[k_]:
                        S_.known[e_][k_] = S_.seq[k_]
                        S_.ops[e_].append(("wait", S_.esem[k_], S_.seq[k_]))
            with nc.Block() as block:
                S_.emit(block)

        def free_dsem(tiles):
            for tt in tiles:
                b = tt.b if isinstance(tt, T) else tt
                if b.dsem is not None:
                    S_.free_dsems.append((b.dsem, b.dcnt))
                    b.dsem = None

        PS = es0.enter_context(nc.psum_tensor("PS", [128, 8, 512], F32))
        PB = [Buf(f"ps{i}") for i in range(8)]

        def sbt(es, name, shape, dt):
            uid[0] += 1
            nm = f"{name}_{uid[0]}"
            return T(es.enter_context(nc.sbuf_tensor(nm, list(shape), dt)), nm)

        cst_f = sbt(es0, "cst_f", [128, 6, 128], F32)
        ident_b = sbt(es0, "ident_b", [128, 128], BF16)
        mask4 = sbt(es0, "mask4", [128, 4, 128], F32)
        ones_b = sbt(es0, "ones_b", [1, 128], BF16)
        dma("sp", lambda e: e.dma_start(out=cst_f[:], in_=cst), writes=[cst_f.b], sembuf=cst_f.b)
        op("dve", lambda e: e.tensor_copy(out=ident_b[:], in_=cst_f[:, 0, :]), reads=[cst_f.b], writes=[ident_b.b])
        for h in range(4):
            op("dve", lambda e, h=h: e.tensor_copy(out=mask4[:, h, :], in_=cst_f[:, 1, :]), reads=[cst_f.b], writes=[mask4.b])
        op("dve", lambda e: e.memset(ones_b[:], 1.0), writes=[ones_b.b])
        tri_incl = cst_f[:, 1, :]
        tri_after = cst_f[:, 2, :]

        with contextlib.ExitStack() as es:
          if not attn_only:
            ct = sbt(es, "ct", [128, 8], F32)
            sc = sbt(es, "sc", [128, 8], F32)
            dma("sp", lambda e: e.dma_start(out=ct[:], in_=c_t), writes=[ct.b], sembuf=ct.b)
            op("act", lambda e: e.activation(out=sc[:], in_=ct[:], func=ACT.Silu), reads=[ct.b], writes=[sc.b])
            wa = [sbt(es, f"wa{i}", [128, 8, 512], F32) for i in range(3)]
            brow = sbt(es, "brow", [1, 6 * D], F32)
            arow = [sbt(es, f"arow{i}", [1, 6 * D], F32) for i in range(2)]
            cnt = 0
            for l in range(5):
                ncb = 12 if l < 4 else 4
                width = ncb * 512
                wsrc = ada_w[l] if l < 4 else kv_ada_w
                bsrc = ada_b[l:l + 1, :] if l < 4 else kv_ada_b
                ar = arow[l % 2]
                dma("sp", lambda e, bsrc=bsrc, width=width: e.dma_start(out=brow[:, 0:width], in_=bsrc),
                    writes=[brow.b], sembuf=brow.b)
                for cb in range(ncb):
                    w = wa[cnt % 3]
                    cnt += 1
                    dma("sp", lambda e, w=w, wsrc=wsrc, cb=cb: e.dma_start(
                        out=w[:], in_=wsrc[:, cb * 512:(cb + 1) * 512].rearrange("(k p) n -> p k n", p=128)),
                        writes=[w.b], sembuf=w.b)
                    pb = cnt % 2
                    for k in range(8):
                        op("pe", lambda e, w=w, k=k, pb=pb: e.matmul(PS[0:1, pb, :], lhsT=sc[:, k:k + 1], rhs=w[:, k, :],
                                                                      start=(k == 0), stop=(k == 7)),
                           reads=[sc.b, w.b], writes=[PB[pb]])
                    op("dve", lambda e, ar=ar, cb=cb, pb=pb: e.tensor_tensor(
                        out=ar[:, cb * 512:(cb + 1) * 512], in0=PS[0:1, pb, :], in1=brow[:, cb * 512:(cb + 1) * 512], op=ALU.add),
                       reads=[PB[pb], brow.b], writes=[ar.b])
                if l < 4:
                    for a0 in (1024, 4096):
                        op("dve", lambda e, ar=ar, a0=a0: e.tensor_scalar_add(out=ar[:, a0:a0 + 2048], in0=ar[:, a0:a0 + 2048], scalar1=1.0),
                           reads=[ar.b], writes=[ar.b])
                else:
                    op("dve", lambda e, ar=ar: e.tensor_scalar_add(out=ar[:, 1024:2048], in0=ar[:, 1024:2048], scalar1=1.0),
                       reads=[ar.b], writes=[ar.b])
                dma("sp", lambda e, ar=ar, l=l, width=width: e.dma_start(out=ada_d[l:l + 1, 0:width], in_=ar[:, 0:width]),
                    reads=[ar.b], writes=[B_ada], sembuf=ar.b)
            S_.wait_all("sp", [B_ada])
            flush()
            free_dsem([ct, brow] + wa + arow)

        def load_bc(es, name, src_row):
            t = sbt(es, name, [128, D], F32)
            dma("sp", lambda e: e.dma_start(out=t[:], in_=src_row.partition_broadcast(128)),
                reads=[B_ada], writes=[t.b], sembuf=t.b)
            return t

        cast_rr = [0]

        def load_w(es, name, src, kch, n, col0=0, stg=None):
            t = es.enter_context(nc.sbuf_tensor(f"{name}_{uid[0]}", [128, kch, n], BF16))
            uid[0] += 1
            bufs = []
            engs = ("dve", "pool", "act")
            for k in range(kch):
                kb = []
                for c0 in range(0, n, 1024):
                    wd = min(1024, n - c0)
                    b = Buf(f"{name}k{k}c{c0}_{uid[0]}")
                    uid[0] += 1
                    sg_ = stg[cast_rr[0] % len(stg)]
                    eng = engs[cast_rr[0] % 3]
                    cast_rr[0] += 1
                    dma("sp", lambda e, k=k, c0=c0, wd=wd, sg_=sg_: e.dma_start(
                        out=sg_[:, 0:wd], in_=src[k * 128:(k + 1) * 128, col0 + c0:col0 + c0 + wd]),
                        writes=[sg_.b], sembuf=sg_.b)
                    if eng == "act":
                        op("act", lambda e, k=k, c0=c0, wd=wd, sg_=sg_: e.copy(out=t[:, k, c0:c0 + wd], in_=sg_[:, 0:wd]),
                           reads=[sg_.b], writes=[b])
                    else:
                        op(eng, lambda e, k=k, c0=c0, wd=wd, sg_=sg_: e.tensor_copy(out=t[:, k, c0:c0 + wd], in_=sg_[:, 0:wd]),
                           reads=[sg_.b], writes=[b])
                    kb.append(b)
                bufs.append(kb)
            return t, bufs

        def src_tile(first, t):
            base = x_in if first else xs
            return base[t * 128:(t + 1) * 128, :]

        def modulate_T(xt, scp, sh, h, hT, pbank, tmp, ncol=128, col0=0):
            op("dve", lambda e: e.tensor_tensor(out=tmp[:], in0=xt[:], in1=scp[:], op=ALU.mult),
               reads=[xt.b, scp.b], writes=[tmp.b])
            op("pool", lambda e: e.tensor_tensor(out=h[:], in0=tmp[:], in1=sh[:], op=ALU.add),
               reads=[tmp.b, sh.b], writes=[h.b])
            pv = PS[:, pbank, :].bitcast(BF16)
            for k in range(8):
                op("pe", lambda e, k=k: e.transpose(out=pv[:, k * 128:(k + 1) * 128], in_=h[:, k * 128:(k + 1) * 128], identity=ident_b[:]),
                   reads=[h.b, ident_b.b], writes=[PB[pbank]])
            op("act", lambda e: e.copy(out=hT[:, :, col0:col0 + 128], in_=pv.rearrange("p (k n) -> p k n", k=8)),
               reads=[PB[pbank]], writes=[hT.b])

        def resid_ln(xt, yb0, gp, lng, lnb, tmp, z, xo, stt, mv):
            yv = PS[:, yb0:yb0 + 2, :].rearrange("p b n -> p (b n)")
            op("dve", lambda e: e.tensor_tensor(out=tmp[:], in0=yv, in1=gp[:], op=ALU.mult),
               reads=[PB[yb0], PB[yb0 + 1], gp.b], writes=[tmp.b])
            op("dve", lambda e: e.scalar_tensor_tensor(out=z[:], in0=xt[:], scalar=ALPHA, in1=tmp[:], op0=ALU.mult, op1=ALU.add),
               reads=[xt.b, tmp.b], writes=[z.b])
            for c in range(2):
                op("dve", lambda e, c=c: e.bn_stats(out=stt[:, c, :], in_=z[:, c * 512:(c + 1) * 512]), reads=[z.b], writes=[stt.b])
            op("dve", lambda e: e.bn_aggr(out=mv[:, 0:2], in_=stt[:]), reads=[stt.b], writes=[mv.b])
            op("act", lambda e: e.activation(out=mv[:, 2:3], in_=mv[:, 1:2], func=ACT.Ln, bias=LN_EPS), reads=[mv.b], writes=[mv.b])
            op("act", lambda e: e.activation(out=mv[:, 3:4], in_=mv[:, 2:3], func=ACT.Exp, scale=-0.5), reads=[mv.b], writes=[mv.b])
            op("dve", lambda e: e.scalar_tensor_tensor(out=mv[:, 4:5], in0=mv[:, 0:1], scalar=-1.0, in1=mv[:, 3:4], op0=ALU.mult, op1=ALU.mult),
               reads=[mv.b], writes=[mv.b])
            op("act", lambda e: e.activation(out=tmp[:], in_=z[:], func=ACT.Identity, scale=mv[:, 3:4], bias=mv[:, 4:5]),
               reads=[z.b, mv.b], writes=[tmp.b])
            op("dve", lambda e: e.tensor_tensor(out=z[:], in0=tmp[:], in1=lng[:], op=ALU.mult), reads=[tmp.b, lng.b], writes=[z.b])
            op("pool", lambda e: e.tensor_tensor(out=xo[:], in0=z[:], in1=lnb[:], op=ALU.add), reads=[z.b, lnb.b], writes=[xo.b])

        state = {"first": True}

        def x_dst(last):
            return out if last else xs

        def gla_phase(l, last=False):
            first = state["first"]
            state["first"] = False
            with contextlib.ExitStack() as es:
                scp = load_bc(es, "scp", ada_d[l:l + 1, 1024:2048])
                sh = load_bc(es, "sh", ada_d[l:l + 1, 0:1024])
                gp = load_bc(es, "gp", ada_d[l:l + 1, 2048:3072])
                lng = load_bc(es, "lng", ln_g[l, 0:1, :])
                lnb = load_bc(es, "lnb", ln_b[l, 0:1, :])
                ngb = sbt(es, "ngb", [128, 4, 256], F32)
                for h in range(4):
                    dma("sp", lambda e, h=h: e.dma_start(out=ngb[:, h, :], in_=gla_ng[l:l + 1, :].partition_broadcast(128)),
                        writes=[ngb.b], sembuf=ngb.b)
                xts = [sbt(es, f"xt{i}", [128, D], F32) for i in range(3)]
                xos = [sbt(es, f"xo{i}", [128, D], F32) for i in range(2)]
                tmp = sbt(es, "tmp", [128, D], F32)
                z = sbt(es, "z", [128, D], F32)
                stg = [tmp, z] + xos
                win, Bwin = load_w(es, "win", gla_w_in[l], 8, GW, stg=stg)
                wout, Bwout = load_w(es, "wout", gla_w_out[l], 8, D, stg=stg)
                wgu = sbt(es, "wgu", [16, 512], BF16)
                bgr = sbt(es, "bgr", [1, 512], BF16)
                wgu_f = sbt(es, "wgu_f", [16, 512], F32)
                bgr_f = sbt(es, "bgr_f", [1, 512], F32)
                dma("sp", lambda e: e.dma_start(out=wgu_f[:], in_=gla_wgu[l]), writes=[wgu_f.b], sembuf=wgu_f.b)
                dma("sp", lambda e: e.dma_start(out=bgr_f[:], in_=gla_bg[l:l + 1, :]), writes=[bgr_f.b], sembuf=bgr_f.b)
                op("dve", lambda e: e.tensor_copy(out=wgu[:], in_=wgu_f[:]), reads=[wgu_f.b], writes=[wgu.b])
                op("dve", lambda e: e.tensor_copy(out=bgr[:], in_=bgr_f[:]), reads=[bgr_f.b], writes=[bgr.b])
                h_ = sbt(es, "h", [128, D], BF16)
                hT = sbt(es, "hT", [128, 8, 128], BF16)
                glrT = sbt(es, "glrT", [16, 128], BF16)
                e1 = sbt(es, "e1", [128, 512], F32)
                sp_ = sbt(es, "sp", [128, 512], F32)
                E = sbt(es, "E", [128, 4, 128], F32)
                Einv = sbt(es, "Einv", [128, 4, 128], F32)
                Ea = sbt(es, "Ea", [128, 512], F32)
                qeT = sbt(es, "qeT", [128, 4, 128], BF16)
                keT = sbt(es, "keT", [128, 4, 128], BF16)
                kd = sbt(es, "kd", [128, 512], BF16)
                v_ = sbt(es, "v", [128, D], BF16)
                sr = sbt(es, "sr", [128, 4, 256], F32)
                gr = sbt(es, "gr", [128, 4, 256], F32)
                attnT = sbt(es, "attnT", [128, 4, 128], BF16)
                S32 = sbt(es, "S32", [128, 4, 256], F32)
                S16 = sbt(es, "S16", [128, 4, 256], BF16)
                osq = sbt(es, "osq", [128, 4, 256], F32)
                ss = sbt(es, "ss", [128, 8], F32)
                og = sbt(es, "og", [128, D], BF16)
                ogT = sbt(es, "ogT", [128, 8, 128], BF16)
                stt = sbt(es, "stt", [128, 2, 6], F32)
                mv = sbt(es, "mv", [128, 8], F32)
                op("dve", lambda e: e.memset(S32[:], 0.0), writes=[S32.b])
                op("dve", lambda e: e.memset(S16[:], 0.0), writes=[S16.b])

                def load_x(t):
                    xt = xts[t % 3]
                    dma("sp", lambda e: e.dma_start(out=xt[:], in_=src_tile(first, t)), reads=[DX[t]] if not first else [],
                        writes=[xt.b], sembuf=xt.b)

                load_x(0)
                if NT > 1:
                    load_x(1)
                for t in range(NT):
                    if t + 2 < NT:
                        load_x(t + 2)
                    xt = xts[t % 3]
                    xo = xos[t % 2]
                    modulate_T(xt, scp, sh, h_, hT, 0, tmp)
                    for (bank, c0) in ((1, 0), (2, 512)):
                        for m in range(4):
                            for k in range(8):
                                op("pe", lambda e, bank=bank, c0=c0, m=m, k=k: e.matmul(
                                    PS[:, bank, m * 128:(m + 1) * 128], lhsT=win[:, k, c0 + m * 128:c0 + (m + 1) * 128], rhs=hT[:, k, :],
                                    start=(k == 0), stop=(k == 7)), reads=[Bwin[k], hT.b], writes=[PB[bank]])
                    for k in range(8):
                        op("pe", lambda e, k=k: e.matmul(PS[0:16, 3, 0:128], lhsT=win[:, k, 3072:3088], rhs=hT[:, k, :],
                                                         start=(k == 0), stop=(k == 7)), reads=[Bwin[k], hT.b], writes=[PB[3]])
                    op("act", lambda e: e.copy(out=glrT[:], in_=PS[0:16, 3, 0:128]), reads=[PB[3]], writes=[glrT.b])
                    for (bank, c0) in ((4, 512), (5, 1024), (6, 1536), (7, 2048), (0, 2560)):
                        for k in range(8):
                            op("pe", lambda e, bank=bank, c0=c0, k=k: e.matmul(
                                PS[:, bank, :], lhsT=hT[:, k, :], rhs=win[:, k, c0:c0 + 512], start=(k == 0), stop=(k == 7)),
                               reads=[Bwin[k], hT.b], writes=[PB[bank]])
                    op("dve", lambda e: e.tensor_copy(out=v_[:, 0:512], in_=PS[:, 5, :]), reads=[PB[5]], writes=[v_.b])
                    op("dve", lambda e: e.tensor_copy(out=v_[:, 512:1024], in_=PS[:, 6, :]), reads=[PB[6]], writes=[v_.b])
                    op("act", lambda e: e.activation(out=sr[:, 0:2, :], in_=PS[:, 7, :].rearrange("p (h d) -> p h d", h=2), func=ACT.Silu),
                       reads=[PB[7]], writes=[sr.b])
                    op("act", lambda e: e.activation(out=sr[:, 2:4, :], in_=PS[:, 0, :].rearrange("p (h d) -> p h d", h=2), func=ACT.Silu),
                       reads=[PB[0]], writes=[sr.b])
                    op("pool", lambda e: e.tensor_tensor(out=gr[:], in0=sr[:], in1=ngb[:], op=ALU.mult), reads=[sr.b, ngb.b], writes=[gr.b])
                    op("pe", lambda e: e.matmul(PS[:, 3, :], lhsT=glrT[:], rhs=wgu[:], start=True, stop=False),
                       reads=[glrT.b, wgu.b], writes=[PB[3]])
                    op("pe", lambda e: e.matmul(PS[:, 3, :], lhsT=ones_b[:], rhs=bgr[:], start=False, stop=True),
                       reads=[ones_b.b, bgr.b], writes=[PB[3]])
                    op("act", lambda e: e.activation(out=e1[:], in_=PS[:, 3, :], func=ACT.Exp, scale=-1.0), reads=[PB[3]], writes=[e1.b])
                    op("act", lambda e: e.activation(out=sp_[:], in_=e1[:], func=ACT.Ln, bias=1.0), reads=[e1.b], writes=[sp_.b])
                    for hh in range(4):
                        op("pe", lambda e, hh=hh: e.matmul(PS[:, 5, hh * 128:(hh + 1) * 128], lhsT=sp_[:, hh * 128:(hh + 1) * 128],
                                                           rhs=tri_incl, start=True, stop=True),
                           reads=[sp_.b, cst_f.b], writes=[PB[5]])
                    op("pe", lambda e: e.matmul(PS[:, 6, :], lhsT=tri_after, rhs=sp_[:], start=True, stop=True),
                       reads=[sp_.b, cst_f.b], writes=[PB[6]])
                    p5 = PS[:, 5, :].rearrange("p (h n) -> p h n", h=4)
                    op("act", lambda e: e.activation(out=E[:], in_=p5, func=ACT.Exp, scale=-1.0 / 16), reads=[PB[5]], writes=[E.b])
                    op("act", lambda e: e.activation(out=Einv[:], in_=p5, func=ACT.Exp, scale=1.0 / 16), reads=[PB[5]], writes=[Einv.b])
                    op("act", lambda e: e.activation(out=Ea[:], in_=PS[:, 6, :], func=ACT.Exp, scale=-1.0 / 16), reads=[PB[6]], writes=[Ea.b])
                    op("dve", lambda e: e.scalar_tensor_tensor(out=qeT[:], in0=PS[:, 1, :].rearrange("p (h n) -> p h n", h=4), scalar=128.0 ** -0.5,
                                                               in1=E[:], op0=ALU.mult, op1=ALU.mult), reads=[PB[1], E.b], writes=[qeT.b])
                    op("dve", lambda e: e.tensor_tensor(out=keT[:], in0=PS[:, 2, :].rearrange("p (h n) -> p h n", h=4), in1=Einv[:], op=ALU.mult),
                       reads=[PB[2], Einv.b], writes=[keT.b])
                    op("dve", lambda e: e.tensor_tensor(out=kd[:], in0=PS[:, 4, :], in1=Ea[:], op=ALU.mult), reads=[PB[4], Ea.b], writes=[kd.b])
                    for hh in range(4):
                        op("pe", lambda e, hh=hh: e.matmul(PS[:, 7, hh * 128:(hh + 1) * 128], lhsT=keT[:, hh, :], rhs=qeT[:, hh, :], start=True, stop=True),
                           reads=[keT.b, qeT.b], writes=[PB[7]])
                    op("dve", lambda e: e.tensor_tensor(out=attnT[:], in0=PS[:, 7, :].rearrange("p (h n) -> p h n", h=4), in1=mask4[:], op=ALU.mult),
                       reads=[PB[7], mask4.b], writes=[attnT.b])
                    for hh in range(4):
                        ob = hh // 2
                        oc = (hh % 2) * 256
                        op("pe", lambda e, hh=hh, ob=ob, oc=oc: e.matmul(PS[:, ob, oc:oc + 256], lhsT=attnT[:, hh, :], rhs=v_[:, hh * 256:(hh + 1) * 256],
                                                                         start=True, stop=False), reads=[attnT.b, v_.b], writes=[PB[ob]])
                        op("pe", lambda e, hh=hh, ob=ob, oc=oc: e.matmul(PS[:, ob, oc:oc + 256], lhsT=qeT[:, hh, :], rhs=S16[:, hh, :],
                                                                         start=False, stop=True), reads=[qeT.b, S16.b], writes=[PB[ob]])
                    for hh in range(4):
                        ob = 2 + hh // 2
                        oc = (hh % 2) * 256
                        op("pe", lambda e, hh=hh, ob=ob, oc=oc: e.matmul(PS[:, ob, oc:oc + 256], lhsT=kd[:, hh * 128:(hh + 1) * 128],
                                                                         rhs=v_[:, hh * 256:(hh + 1) * 256], start=True, stop=True),
                           reads=[kd.b, v_.b], writes=[PB[ob]])
                    for hh in range(4):
                        ob = 2 + hh // 2
                        oc = (hh % 2) * 256
                        op("dve", lambda e, hh=hh, ob=ob, oc=oc: e.scalar_tensor_tensor(
                            out=S32[:, hh, :], in0=S32[:, hh, :], scalar=E[:, hh, 127:128], in1=PS[:, ob, oc:oc + 256], op0=ALU.mult, op1=ALU.add),
                           reads=[S32.b, E.b, PB[ob]], writes=[S32.b])
                    op("pool", lambda e: e.tensor_copy(out=S16[:], in_=S32[:]), reads=[S32.b], writes=[S16.b])
                    ov = PS[:, 0:2, :].rearrange("p b (h d) -> p (b h) d", h=2)
                    op("act", lambda e: e.activation(out=osq[:], in_=ov, func=ACT.Square), reads=[PB[0], PB[1]], writes=[osq.b])
                    op("dve", lambda e: e.tensor_reduce(out=ss[:, 0:4], in_=osq[:], axis=AX.X, op=ALU.add), reads=[osq.b], writes=[ss.b])
                    op("act", lambda e: e.activation(out=ss[:, 4:8], in_=ss[:, 0:4], func=ACT.Ln, scale=1.0 / 256, bias=RMS_EPS),
                       reads=[ss.b], writes=[ss.b])
                    op("act", lambda e: e.activation(out=ss[:, 0:4], in_=ss[:, 4:8], func=ACT.Exp, scale=-0.5), reads=[ss.b], writes=[ss.b])
                    for hh in range(4):
                        ob = hh // 2
                        oc = (hh % 2) * 256
                        op("dve", lambda e, hh=hh, ob=ob, oc=oc: e.scalar_tensor_tensor(
                            out=og[:, hh * 256:(hh + 1) * 256], in0=PS[:, ob, oc:oc + 256], scalar=ss[:, hh:hh + 1], in1=gr[:, hh, :],
                            op0=ALU.mult, op1=ALU.mult), reads=[PB[ob], ss.b, gr.b], writes=[og.b])
                    pv = PS[:, 4, :].bitcast(BF16)
                    for k in range(8):
                        op("pe", lambda e, k=k: e.transpose(out=pv[:, k * 128:(k + 1) * 128], in_=og[:, k * 128:(k + 1) * 128], identity=ident_b[:]),
                           reads=[og.b, ident_b.b], writes=[PB[4]])
                    op("act", lambda e: e.copy(out=ogT[:], in_=pv.rearrange("p (k n) -> p k n", k=8)), reads=[PB[4]], writes=[ogT.b])
                    for cb in range(2):
                        for k in range(8):
                            op("pe", lambda e, cb=cb, k=k: e.matmul(PS[:, 5 + cb, :], lhsT=ogT[:, k, :], rhs=wout[:, k, cb * 512:(cb + 1) * 512],
                                                                    start=(k == 0), stop=(k == 7)), reads=[ogT.b, Bwout[k]], writes=[PB[5 + cb]])
                    resid_ln(xt, 5, gp, lng, lnb, tmp, z, xo, stt, mv)
                    dma("sp", lambda e, t=t, xo=xo: e.dma_start(out=x_dst(last)[t * 128:(t + 1) * 128, :], in_=xo[:]),
                        reads=[xo.b], writes=[DX[t]], sembuf=xo.b)
                S_.wait_all("sp", DX)
                flush()
                free_dsem([scp, sh, gp, lng, lnb, ngb, wgu_f, bgr_f, tmp, z] + xts + xos)

        def ffn_phase(l, last=False):
            first = state["first"]
            state["first"] = False
            with contextlib.ExitStack() as es:
                scp = load_bc(es, "scp", ada_d[l:l + 1, 4096:5120])
                sh = load_bc(es, "sh", ada_d[l:l + 1, 3072:4096])
                gp = load_bc(es, "gp", ada_d[l:l + 1, 5120:6144])
                lng = load_bc(es, "lng", ln_g[l, 1:2, :])
                lnb = load_bc(es, "lnb", ln_b[l, 1:2, :])
                xts = [sbt(es, f"xt{i}", [128, D], F32) for i in range(2)]
                xos = [sbt(es, f"xo{i}", [128, D], F32) for i in range(2)]
                tmp = sbt(es, "tmp", [128, D], F32)
                z = sbt(es, "z", [128, D], F32)
                stg = [tmp, z] + xos
                win, Bwin = load_w(es, "fwin", ffn_w_in[l], 8, 2 * FH, stg=stg)
                wout, Bwout = load_w(es, "fwout", ffn_w_out[l], 22, D, stg=stg)
                h_ = sbt(es, "h", [128, D], BF16)
                hT = sbt(es, "hT", [128, 8, 128], BF16)
                sg = sbt(es, "sg", [128, 512], F32)
                a_ = sbt(es, "a", [128, FH], BF16)
                aT = sbt(es, "aT", [128, 22, 128], BF16)
                stt = sbt(es, "stt", [128, 2, 6], F32)
                mv = sbt(es, "mv", [128, 8], F32)

                def load_x(t):
                    xt = xts[t % 2]
                    dma("sp", lambda e: e.dma_start(out=xt[:], in_=src_tile(first, t)), reads=[DX[t]] if not first else [],
                        writes=[xt.b], sembuf=xt.b)

                load_x(0)
                for t in range(NT):
                    if t + 1 < NT:
                        load_x(t + 1)
                    xt = xts[t % 2]
                    xo = xos[t % 2]
                    modulate_T(xt, scp, sh, h_, hT, 0, tmp)
                    for j in range(6):
                        wd = 512 if j < 5 else 256
                        gb = 1 + 2 * (j % 2)
                        ub = gb + 1
                        for (bank, c0) in ((gb, j * 512), (ub, FH + j * 512)):
                            for k in range(8):
                                op("pe", lambda e, bank=bank, c0=c0, k=k, wd=wd: e.matmul(
                                    PS[:, bank, 0:wd], lhsT=hT[:, k, :], rhs=win[:, k, c0:c0 + wd], start=(k == 0), stop=(k == 7)),
                                   reads=[hT.b, Bwin[k]], writes=[PB[bank]])
                        op("act", lambda e, gb=gb, wd=wd: e.activation(out=sg[:, 0:wd], in_=PS[:, gb, 0:wd], func=ACT.Silu),
                           reads=[PB[gb]], writes=[sg.b])
                        op("dve", lambda e, ub=ub, wd=wd, j=j: e.tensor_tensor(out=a_[:, j * 512:j * 512 + wd], in0=PS[:, ub, 0:wd], in1=sg[:, 0:wd], op=ALU.mult),
                           reads=[PB[ub], sg.b], writes=[a_.b])
                    for rnd in range(3):
                        bank = 5 if rnd % 2 == 0 else 0
                        n = 8 if rnd < 2 else 6
                        pv = PS[:, bank, :].bitcast(BF16)
                        for i in range(n):
                            kk = rnd * 8 + i
                            op("pe", lambda e, i=i, kk=kk, pv=pv: e.transpose(out=pv[:, i * 128:(i + 1) * 128], in_=a_[:, kk * 128:(kk + 1) * 128], identity=ident_b[:]),
                               reads=[a_.b, ident_b.b], writes=[PB[bank]])
                        op("act", lambda e, rnd=rnd, n=n, pv=pv: e.copy(out=aT[:, rnd * 8:rnd * 8 + n, :], in_=pv[:, 0:n * 128].rearrange("p (k n) -> p k n", k=n)),
                           reads=[PB[bank]], writes=[aT.b])
                    for cb in range(2):
                        for k in range(22):
                            op("pe", lambda e, cb=cb, k=k: e.matmul(PS[:, 6 + cb, :], lhsT=aT[:, k, :], rhs=wout[:, k, cb * 512:(cb + 1) * 512],
                                                                    start=(k == 0), stop=(k == 21)), reads=[aT.b, Bwout[k]], writes=[PB[6 + cb]])
                    resid_ln(xt, 6, gp, lng, lnb, tmp, z, xo, stt, mv)
                    dma("sp", lambda e, t=t, xo=xo: e.dma_start(out=x_dst(last)[t * 128:(t + 1) * 128, :], in_=xo[:]),
                        reads=[xo.b], writes=[DX[t]], sembuf=xo.b)
                S_.wait_all("sp", DX)
                flush()
                free_dsem([scp, sh, gp, lng, lnb, tmp, z] + xts + xos)


        def proj_phase(row, wsrc, nfm, fm_col0, dstT, BdT, fm_scale, tok_cols=None):
            with contextlib.ExitStack() as es:
                scp = load_bc(es, "scp", ada_d[row:row + 1, 1024:2048])
                sh = load_bc(es, "sh", ada_d[row:row + 1, 0:1024])
                ncols = fm_col0 + nfm * 128 if tok_cols is None else max(fm_col0 + nfm * 128, tok_cols[0] + tok_cols[1])
                xts = [sbt(es, f"xt{i}", [128, D], F32) for i in range(3)]
                tmp = sbt(es, "tmp", [128, D], F32)
                stg2 = sbt(es, "stg2", [128, D], F32)
                w, Bw = load_w(es, "pw", wsrc, 8, ncols, stg=[tmp, stg2])
                h_ = sbt(es, "h", [128, D], BF16)
                hT4 = [sbt(es, f"hT4{i}", [128, 8, 512], BF16) for i in range(2)]
                fmo = [sbt(es, f"fmo{i}", [128, nfm, 512], BF16) for i in range(2)]
                vo = [sbt(es, f"vo{i}", [128, D], BF16) for i in range(2)]

                def load_x(t):
                    xt = xts[t % 3]
                    dma("sp", lambda e: e.dma_start(out=xt[:], in_=xs[t * 128:(t + 1) * 128, :]), reads=[DX[t]],
                        writes=[xt.b], sembuf=xt.b)
                load_x(0)
                load_x(1)
                for blk in range(S // 512):
                    hT = hT4[blk % 2]
                    fo = fmo[blk % 2]
                    for tt in range(4):
                        t = blk * 4 + tt
                        if t + 2 < NT:
                            load_x(t + 2)
                        modulate_T(xts[t % 3], scp, sh, h_, hT, 0, tmp, col0=tt * 128)
                        if tok_cols is not None:
                            v16 = vo[t % 2]
                            for cb in range(2):
                                for k in range(8):
                                    op("pe", lambda e, cb=cb, k=k, hT=hT, tt=tt: e.matmul(
                                        PS[:, 1 + cb, :], lhsT=hT[:, k, tt * 128:(tt + 1) * 128],
                                        rhs=w[:, k, tok_cols[0] + cb * 512:tok_cols[0] + (cb + 1) * 512], start=(k == 0), stop=(k == 7)),
                                       reads=[hT.b, Bw[k]], writes=[PB[1 + cb]])
                            op("dve", lambda e, v16=v16: e.tensor_copy(out=v16[:], in_=PS[:, 1:3, :].rearrange("p b n -> p (b n)")),
                               reads=[PB[1], PB[2]], writes=[v16.b])
                            dma("sp", lambda e, v16=v16, t=t: e.dma_start(out=v_d[t * 128:(t + 1) * 128, :], in_=v16[:]),
                                reads=[v16.b], writes=[B_vd[t]], sembuf=v16.b)
                    for m in range(nfm):
                        bank = 3 + (m % 4)
                        for k in range(8):
                            op("pe", lambda e, bank=bank, m=m, k=k, hT=hT: e.matmul(
                                PS[:, bank, :], lhsT=w[:, k, fm_col0 + m * 128:fm_col0 + (m + 1) * 128], rhs=hT[:, k, :],
                                start=(k == 0), stop=(k == 7)), reads=[hT.b, Bw[k]], writes=[PB[bank]])
                        eng = "act" if m % 2 == 0 else "dve"
                        if eng == "act":
                            op("act", lambda e, bank=bank, m=m, fo=fo: e.activation(out=fo[:, m, :], in_=PS[:, bank, :], func=ACT.Identity, scale=fm_scale),
                               reads=[PB[bank]], writes=[fo.b])
                        else:
                            op("dve", lambda e, bank=bank, m=m, fo=fo: e.tensor_scalar_mul(out=fo[:, m, :], in0=PS[:, bank, :], scalar1=fm_scale),
                               reads=[PB[bank]], writes=[fo.b])
                    dma("sp", lambda e, fo=fo, blk=blk: e.dma_start(
                        out=dstT[:, blk * 512:(blk + 1) * 512].rearrange("(m p) n -> p m n", p=128), in_=fo[:]),
                        reads=[fo.b], writes=[BdT[blk]], sembuf=fo.b)
                S_.wait_all("sp", BdT + (B_vd if tok_cols is not None else []))
                flush()
                free_dsem([scp, sh, tmp, stg2] + xts + fmo + vo)

        def attn_phase():
            NMS = S // 2048
            with contextlib.ExitStack() as es:
                maskb = sbt(es, "maskb", [128, 4, 256], F32)
                maskb0 = sbt(es, "maskb0", [128, 4, 256], F32)
                for hh in range(4):
                    op("dve", lambda e, hh=hh: e.tensor_copy(out=maskb[:, hh, 0:128], in_=cst_f[:, 3, :]), reads=[cst_f.b], writes=[maskb.b])
                    op("dve", lambda e, hh=hh: e.tensor_copy(out=maskb[:, hh, 128:256], in_=cst_f[:, 4, :]), reads=[cst_f.b], writes=[maskb.b])
                    op("dve", lambda e, hh=hh: e.tensor_copy(out=maskb0[:, hh, 0:128], in_=cst_f[:, 5, :]), reads=[cst_f.b], writes=[maskb0.b])
                    op("dve", lambda e, hh=hh: e.tensor_copy(out=maskb0[:, hh, 128:256], in_=cst_f[:, 4, :]), reads=[cst_f.b], writes=[maskb0.b])
                kTb = [sbt(es, f"kTb{i}", [128, 8, 2048], BF16) for i in range(2)]
                qz = sbt(es, "qz", [128, 16, 2048], BF16)
                qzE = Buf("qzE_%d" % uid[0])
                qzO = Buf("qzO_%d" % uid[0])
                uid[0] += 1
                op("pool", lambda e: e.memset(qz[:], 0.0), writes=[qz.b, qzE, qzO])
                vvs = [sbt(es, f"vv{i}", [128, 2, D], BF16) for i in range(3)]
                sm = [sbt(es, f"sm{i}", [128, 4, 256], F32) for i in range(2)]
                pbf = [sbt(es, f"pbf{i}", [128, 4, 256], BF16) for i in range(2)]
                pTs = [sbt(es, f"pTs{i}", [128, 8, 128], BF16) for i in range(2)]
                mdt = [sbt(es, f"mdt{i}", [128, 32], F32) for i in range(2)]
                negm = [sbt(es, f"negm{i}", [128, 8], F32) for i in range(2)]
                rden = sbt(es, "rden", [128, 16], F32)
                ogt = [sbt(es, f"ogt{i}", [128, 16, 64], BF16) for i in range(2)]
                qcnt = 0
                ucnt = 0
                hcnt = 0
                for ms in range(NMS):
                    base = ms * 2048
                    kc = kTb[ms % 2]
                    kp = kTb[(ms + 1) % 2]
                    dma("sp", lambda e, kc=kc, base=base: e.dma_start(
                        out=kc[:], in_=kT_d[:, base:base + 2048].rearrange("(m p) n -> p m n", p=128)),
                        reads=B_kT, writes=[kc.b], sembuf=kc.b)
                    for g, d in enumerate(DILS):
                        if ('g0' in DBG and g != 0) or ('g1' in DBG and g != 1) or ('g2' in DBG and g != 2):
                            continue
                        qsrc = qT_d[g * D:(g + 1) * D, base:base + 2048].rearrange("(m two p) n -> two p m n", two=2, p=64)
                        dma("sp", lambda e, qsrc=qsrc: e.dma_start(out=qz[0:64, 0:16:2, :], in_=qsrc[0]),
                            reads=B_qT + [qz.b], writes=[qzE], sembuf=qzE)
                        dma("sp", lambda e, qsrc=qsrc: e.dma_start(out=qz[64:128, 1:16:2, :], in_=qsrc[1]),
                            reads=B_qT + [qz.b], writes=[qzO], sembuf=qzO)
                        nbk = 16 // d
                        vview = v_d.rearrange("(n dd) c -> dd n c", dd=d)
                        ogview = og_d[g].rearrange("(n dd) c -> dd n c", dd=d)
                        mdview = md_d[g].rearrange("(n dd) c -> dd n c", dd=d)
                        for r in range(d if 'nounits' not in DBG else 0):
                            for b in range(nbk):
                                gb0 = (ms == 0 and b == 0)
                                n0 = base // d + b * 128
                                vv = vvs[ucnt % 3]
                                md = mdt[ucnt % 2]
                                og_t = ogt[ucnt % 2]
                                ucnt += 1
                                if 'nov' in DBG:
                                    pass
                                elif gb0:
                                    dma("sp", lambda e, vv=vv, r=r, n0=n0, vview=vview: e.dma_start(out=vv[:, 1, :], in_=vview[r, n0:n0 + 128, :]),
                                        reads=B_vd, writes=[vv.b], sembuf=vv.b)
                                else:
                                    dma("sp", lambda e, vv=vv, r=r, n0=n0, vview=vview: e.dma_start(
                                        out=vv[:], in_=vview[r, n0 - 128:n0 + 128, :].rearrange("(two p) c -> p two c", p=128)),
                                        reads=B_vd, writes=[vv.b], sembuf=vv.b)
                                q0 = r + b * 128 * d
                                if b >= 1:
                                    ksrc, kp0 = kc, r + (b - 1) * 128 * d
                                elif not gb0:
                                    ksrc, kp0 = kp, r + (nbk - 1) * 128 * d
                                else:
                                    ksrc, kp0 = kc, q0
                                for hg in range(4):
                                    sb0 = 2 * (hcnt % 2)
                                    tb = 4 + (hcnt % 2)
                                    smt = sm[hcnt % 2]
                                    pb_ = pbf[hcnt % 2]
                                    pT = pTs[hcnt % 2]
                                    ng = negm[hcnt % 2]
                                    hcnt += 1
                                    for hh in range(4):
                                        hd = hg * 4 + hh
                                        c = hd // 2
                                        pr = (hd % 2) * 64
                                        bank = sb0 + hh // 2
                                        co = (hh % 2) * 256
                                        qB = qzE if hd % 2 == 0 else qzO
                                        op("pe", lambda e, bank=bank, co=co, hd=hd, c=c, ksrc=ksrc, kp0=kp0, q0=q0, d=d: e.matmul(
                                            PS[:, bank, co:co + 128], lhsT=qz[:, hd, q0:q0 + 127 * d + 1:d],
                                            rhs=ksrc[:, c, kp0:kp0 + 127 * d + 1:d], start=True, stop=True),
                                           reads=[qB, ksrc.b], writes=[PB[bank]])
                                        op("pe", lambda e, bank=bank, co=co, hd=hd, c=c, kc=kc, q0=q0, d=d: e.matmul(
                                            PS[:, bank, co + 128:co + 256], lhsT=qz[:, hd, q0:q0 + 127 * d + 1:d],
                                            rhs=kc[:, c, q0:q0 + 127 * d + 1:d], start=True, stop=True),
                                           reads=[qB, kc.b], writes=[PB[bank]])
                                    mk = maskb0 if gb0 else maskb
                                    op("dve", lambda e, sb0=sb0, smt=smt, mk=mk: e.tensor_tensor(
                                        out=smt[:], in0=PS[:, sb0:sb0 + 2, :].rearrange("p b (h n) -> p (b h) n", h=2), in1=mk[:], op=ALU.add),
                                       reads=[PB[sb0], PB[sb0 + 1], mk.b], writes=[smt.b])
                                    op("dve", lambda e, smt=smt, md=md, hg=hg: e.tensor_reduce(out=md[:, hg * 4:(hg + 1) * 4], in_=smt[:], axis=AX.X, op=ALU.max),
                                       reads=[smt.b], writes=[md.b])
                                    op("dve", lambda e, md=md, hg=hg, ng=ng: e.tensor_scalar_mul(out=ng[:, 0:4], in0=md[:, hg * 4:(hg + 1) * 4], scalar1=-1.0),
                                       reads=[md.b], writes=[ng.b])
                                    if 'stopA' in DBG:
                                        continue
                                    for hh in range(4):
                                        hd = hg * 4 + hh
                                        op("act", lambda e, hh=hh, hd=hd, smt=smt, pb_=pb_, ng=ng, md=md: e.activation(
                                            out=pb_[:, hh, :], in_=smt[:, hh, :], func=ACT.Exp, bias=ng[:, hh:hh + 1], accum_out=md[:, 16 + hd:17 + hd]),
                                           reads=[smt.b, ng.b], writes=[pb_.b, md.b])
                                    pv = PS[:, tb, :].bitcast(BF16)
                                    for hh in range(4):
                                        for half in range(2):
                                            i8 = hh * 2 + half
                                            op("pe", lambda e, i8=i8, hh=hh, half=half, pb_=pb_, pv=pv: e.transpose(
                                                out=pv[:, i8 * 128:(i8 + 1) * 128], in_=pb_[:, hh, half * 128:(half + 1) * 128], identity=ident_b[:]),
                                               reads=[pb_.b, ident_b.b], writes=[PB[tb]])
                                    op("act", lambda e, pT=pT, pv=pv: e.copy(out=pT[:], in_=pv.rearrange("p (k n) -> p k n", k=8)),
                                       reads=[PB[tb]], writes=[pT.b])
                                    if 'stopB' in DBG:
                                        continue
                                    for hh in range(4):
                                        hd = hg * 4 + hh
                                        ob = 6 + hd // 8
                                        oc = (hd % 8) * 64
                                        if not gb0:
                                            op("pe", lambda e, ob=ob, oc=oc, hh=hh, hd=hd, pT=pT, vv=vv: e.matmul(
                                                PS[:, ob, oc:oc + 64], lhsT=pT[:, hh * 2, :], rhs=vv[:, 0, hd * 64:(hd + 1) * 64], start=True, stop=False),
                                               reads=[pT.b, vv.b], writes=[PB[ob]])
                                        op("pe", lambda e, ob=ob, oc=oc, hh=hh, hd=hd, pT=pT, vv=vv, gb0=gb0: e.matmul(
                                            PS[:, ob, oc:oc + 64], lhsT=pT[:, hh * 2 + 1, :], rhs=vv[:, 1, hd * 64:(hd + 1) * 64], start=gb0, stop=True),
                                           reads=[pT.b, vv.b], writes=[PB[ob]])
                                if 'stopA' in DBG or 'stopB' in DBG or 'stopC' in DBG:
                                    continue
                                op("dve", lambda e, md=md: e.reciprocal(out=rden[:], in_=md[:, 16:32]), reads=[md.b], writes=[rden.b])
                                op("dve", lambda e, og_t=og_t: e.tensor_tensor(
                                    out=og_t[:], in0=PS[:, 6:8, :].rearrange("p b (h n) -> p (b h) n", h=8),
                                    in1=rden[:].unsqueeze(2).to_broadcast([128, 16, 64]), op=ALU.mult),
                                   reads=[PB[6], PB[7], rden.b], writes=[og_t.b])
                                if 'noog' not in DBG:
                                    dma("sp", lambda e, og_t=og_t, ogview=ogview, r=r, n0=n0: e.dma_start(
                                        out=ogview[r, n0:n0 + 128, :], in_=og_t[:].rearrange("p h n -> p (h n)")),
                                        reads=[og_t.b], writes=[B_og], sembuf=og_t.b)
                                if 'nomd' not in DBG:
                                    dma("sp", lambda e, md=md, mdview=mdview, r=r, n0=n0: e.dma_start(out=mdview[r, n0:n0 + 128, :], in_=md[:]),
                                        reads=[md.b], writes=[B_og], sembuf=md.b)
                S_.wait_all("sp", [B_og])
                flush()
                free_dsem(kTb + [qzE, qzO] + vvs + mdt + ogt)

        def comb_phase(l, last=False):
            li = l - 2
            with contextlib.ExitStack() as es:
                gp = load_bc(es, "gp", ada_d[l:l + 1, 2048:3072])
                lng = load_bc(es, "lng", ln_g[l, 0:1, :])
                lnb = load_bc(es, "lnb", ln_b[l, 0:1, :])
                xts = [sbt(es, f"xt{i}", [128, D], F32) for i in range(2)]
                xos = [sbt(es, f"xo{i}", [128, D], F32) for i in range(2)]
                wout, Bwout = load_w(es, "dwout", dil_w_out[li], 8, D, stg=xos)
                ogs = [[sbt(es, f"ogl{i}_{g}", [128, 16, 64], BF16) for g in range(3)] for i in range(2)]
                mds = [[sbt(es, f"mdl{i}_{g}", [128, 32], F32) for g in range(3)] for i in range(2)]
                tmp = sbt(es, "tmp", [128, D], F32)
                z = sbt(es, "z", [128, D], F32)
                o_ = sbt(es, "o", [128, D], BF16)
                oT = sbt(es, "oT", [128, 8, 128], BF16)
                M = sbt(es, "M", [128, 16], F32)
                ew = sbt(es, "ew", [128, 3, 16], F32)
                W = sbt(es, "W", [128, 16], F32)
                stt = sbt(es, "stt", [128, 2, 6], F32)
                mv = sbt(es, "mv", [128, 8], F32)

                def load(t):
                    i = t % 2
                    dma("sp", lambda e: e.dma_start(out=xts[i][:], in_=xs[t * 128:(t + 1) * 128, :]), reads=[DX[t]],
                        writes=[xts[i].b], sembuf=xts[i].b)
                    for g in range(3):
                        dma("sp", lambda e, g=g: e.dma_start(out=ogs[i][g][:].rearrange("p h n -> p (h n)"), in_=og_d[g][t * 128:(t + 1) * 128, :]),
                            reads=[B_og], writes=[ogs[i][g].b], sembuf=ogs[i][g].b)
                        dma("sp", lambda e, g=g: e.dma_start(out=mds[i][g][:], in_=md_d[g][t * 128:(t + 1) * 128, :]),
                            reads=[B_og], writes=[mds[i][g].b], sembuf=mds[i][g].b)
                load(0)
                for t in range(NT):
                    if t + 1 < NT:
                        load(t + 1)
                    i = t % 2
                    xt, xo, og3, md3 = xts[i], xos[i], ogs[i], mds[i]
                    op("dve", lambda e, md3=md3: e.tensor_tensor(out=M[:], in0=md3[0][:, 0:16], in1=md3[1][:, 0:16], op=ALU.max),
                       reads=[md3[0].b, md3[1].b], writes=[M.b])
                    op("dve", lambda e, md3=md3: e.tensor_tensor(out=M[:], in0=M[:], in1=md3[2][:, 0:16], op=ALU.max),
                       reads=[M.b, md3[2].b], writes=[M.b])
                    for g in range(3):
                        op("dve", lambda e, g=g, md3=md3: e.tensor_tensor(out=ew[:, g, :], in0=md3[g][:, 0:16], in1=M[:], op=ALU.subtract),
                           reads=[md3[g].b, M.b], writes=[ew.b])
                    op("act", lambda e: e.activation(out=ew[:], in_=ew[:], func=ACT.Exp), reads=[ew.b], writes=[ew.b])
                    for g in range(3):
                        op("dve", lambda e, g=g, md3=md3: e.tensor_tensor(out=ew[:, g, :], in0=ew[:, g, :], in1=md3[g][:, 16:32], op=ALU.mult),
                           reads=[ew.b, md3[g].b], writes=[ew.b])
                    op("dve", lambda e: e.tensor_tensor(out=W[:], in0=ew[:, 0, :], in1=ew[:, 1, :], op=ALU.add), reads=[ew.b], writes=[W.b])
                    op("dve", lambda e: e.tensor_tensor(out=W[:], in0=W[:], in1=ew[:, 2, :], op=ALU.add), reads=[ew.b, W.b], writes=[W.b])
                    op("dve", lambda e: e.reciprocal(out=W[:], in_=W[:]), reads=[W.b], writes=[W.b])
                    for g in range(3):
                        op("dve", lambda e, g=g: e.tensor_tensor(out=ew[:, g, :], in0=ew[:, g, :], in1=W[:], op=ALU.mult),
                           reads=[ew.b, W.b], writes=[ew.b])
                    tv = tmp[:].rearrange("p (h n) -> p h n", h=16)
                    zv = z[:].rearrange("p (h n) -> p h n", h=16)
                    op("dve", lambda e, og3=og3: e.tensor_tensor(out=tv, in0=og3[0][:], in1=ew[:, 0, :].unsqueeze(2).to_broadcast([128, 16, 64]), op=ALU.mult),
                       reads=[og3[0].b, ew.b], writes=[tmp.b])
                    op("dve", lambda e, og3=og3: e.tensor_tensor(out=zv, in0=og3[1][:], in1=ew[:, 1, :].unsqueeze(2).to_broadcast([128, 16, 64]), op=ALU.mult),
                       reads=[og3[1].b, ew.b], writes=[z.b])
                    op("dve", lambda e: e.tensor_tensor(out=tmp[:], in0=tmp[:], in1=z[:], op=ALU.add), reads=[tmp.b, z.b], writes=[tmp.b])
                    op("dve", lambda e, og3=og3: e.tensor_tensor(out=zv, in0=og3[2][:], in1=ew[:, 2, :].unsqueeze(2).to_broadcast([128, 16, 64]), op=ALU.mult),
                       reads=[og3[2].b, ew.b], writes=[z.b])
                    op("dve", lambda e: e.tensor_tensor(out=o_[:], in0=tmp[:], in1=z[:], op=ALU.add), reads=[tmp.b, z.b], writes=[o_.b])
                    pv = PS[:, 0, :].bitcast(BF16)
                    for k in range(8):
                        op("pe", lambda e, k=k: e.transpose(out=pv[:, k * 128:(k + 1) * 128], in_=o_[:, k * 128:(k + 1) * 128], identity=ident_b[:]),
                           reads=[o_.b, ident_b.b], writes=[PB[0]])
                    op("act", lambda e: e.copy(out=oT[:], in_=pv.rearrange("p (k n) -> p k n", k=8)), reads=[PB[0]], writes=[oT.b])
                    yb = 1 + 2 * (t % 2)
                    for cb in range(2):
                        for k in range(8):
                            op("pe", lambda e, cb=cb, k=k, yb=yb: e.matmul(PS[:, yb + cb, :], lhsT=oT[:, k, :], rhs=wout[:, k, cb * 512:(cb + 1) * 512],
                                                                           start=(k == 0), stop=(k == 7)), reads=[oT.b, Bwout[k]], writes=[PB[yb + cb]])
                    resid_ln(xt, yb, gp, lng, lnb, tmp, z, xo, stt, mv)
                    dma("sp", lambda e, t=t, xo=xo: e.dma_start(out=x_dst(last)[t * 128:(t + 1) * 128, :], in_=xo[:]),
                        reads=[xo.b], writes=[DX[t]], sembuf=xo.b)
                S_.wait_all("sp", DX)
                flush()
                free_dsem([gp, lng, lnb] + xts + xos + [a for b_ in ogs for a in b_] + [a for b_ in mds for a in b_])

        phases = []
        for l in range(2):
            phases.append(("gla%d" % l, lambda last, l=l: gla_phase(l, last)))
            phases.append(("ffn%d" % l, lambda last, l=l: ffn_phase(l, last)))
        for l in (2, 3):
            def dil(last, l=l):
                if l == 2:
                    proj_phase(4, w_kv, 8, 0, kT_d, B_kT, 1.0, tok_cols=(1024, 1024))
                if sub == "kv":
                    return
                proj_phase(l, dil_w_q[l - 2], 24, 0, qT_d, B_qT, 0.125)
                if sub == "q":
                    return
                attn_phase()
                if sub == "attn":
                    return
                comb_phase(l, last)
            phases.append(("dil%d" % l, dil))
            phases.append(("ffn%d" % l, lambda last, l=l: ffn_phase(l, last)))
        if attn_only:
            attn_phase()
            phases = []
            stop_after = None
        names = [p[0] for p in phases]
        sub = None
        if stop_after is not None and ":" in stop_after:
            stop_after, sub = stop_after.split(":")
        stop_idx = len(phases) - 1 if stop_after is None else names.index(stop_after)
        if attn_only:
            stop_idx = -1
        for i, (nm, fn) in enumerate(phases[:stop_idx + 1]):
            fn(i == stop_idx)
        print("instructions:", S_.ninst)
    return nc


def make_consts():
    cst = np.zeros((128, 6, 128), np.float32)
    j = np.arange(128)[:, None]
    i = np.arange(128)[None, :]
    cst[:, 0, :] = np.eye(128)
    cst[:, 1, :] = (j <= i)
    cst[:, 2, :] = (j > i)
    cst[:, 3, :] = np.where(i >= j, 0.0, NEG)
    cst[:, 4, :] = np.where(i <= j, 0.0, NEG)
    cst[:, 5, :] = NEG
    return cst


def make_in_maps(inputs, nb, S):
    cst = make_consts()
    shared = {k: np.ascontiguousarray(v) for k, v in inputs.items() if k not in ("x", "c")}
    shared["kv_ada_b"] = shared["kv_ada_b"].reshape(1, -1)
    shared["cst"] = cst
    maps = []
    for b in range(nb):
        m = dict(shared)
        m["x"] = np.ascontiguousarray(inputs["x"][b, :S])
        m["c_t"] = np.ascontiguousarray(inputs["c"][b].reshape(8, 128).T)
        maps.append(m)
    return maps


def kernel(**inputs):
    S = inputs["x"].shape[1]
    nc = build_nc(S)
    maps = make_in_maps(inputs, 8, S)
    res = run_bass_kernel_spmd(nc, maps, core_ids=list(range(8)))
    return np.stack([r["out"] for r in res.results], axis=0)
```

```python
import contextlib
import os
DBG = os.environ.get('KDBG', '')
import numpy as np
import concourse.bass as bass
import concourse.mybir as mybir
from concourse.bass_utils import run_bass_kernel_spmd

F32 = mybir.dt.float32
BF16 = mybir.dt.bfloat16
ACT = mybir.ActivationFunctionType
ALU = mybir.AluOpType
AX = mybir.AxisListType

D = 1024
DEPTH = 4
FH = 2816
ALPHA = (2.0 * DEPTH) ** 0.25
LN_EPS = 1e-5
RMS_EPS = 1e-5
GW = 3088
DILS = (1, 4, 16)
NEG = -30000.0

COMPUTE = ("pe", "act", "dve", "pool")
ALL = COMPUTE + ("sp",)


class Buf:
    __slots__ = ("name", "w", "r", "dsem", "dcnt")

    def __init__(self, name):
        self.name = name
        self.w = None
        self.r = {}
        self.dsem = None
        self.dcnt = 0


class Sched:
    def __init__(self, nc, esems, dma_sems):
        self.nc = nc
        self.ops = {e: [] for e in ALL}
        self.seq = {e: 0 for e in COMPUTE}
        self.esem = esems
        self.free_dsems = [(s_, 0) for s_ in dma_sems]
        self.known = {e: {} for e in ALL}
        self.semobj = dict(esems)
        self.ninst = 0

    def _need(self, eng, tok, acc):
        if tok is None:
            return
        k, v = tok
        if k == eng and eng == "pe":
            return
        if acc.get(k, 0) < v:
            acc[k] = v

    @staticmethod
    def _flat(lst):
        out = []
        for b in lst:
            if isinstance(b, (list, tuple)):
                out.extend(b)
            else:
                out.append(b)
        return out

    def _deps(self, eng, reads, writes):
        acc = {}
        for b in reads:
            self._need(eng, b.w, acc)
        for b in writes:
            self._need(eng, b.w, acc)
            for k, v in b.r.items():
                self._need(eng, (k, v), acc)
        kn = self.known[eng]
        for k, v in acc.items():
            if kn.get(k, 0) < v:
                kn[k] = v
                self.ops[eng].append(("wait", self.semobj[k], v))
                self.ninst += 1

    def _mark(self, tok, reads, writes):
        k, v = tok
        for b in reads:
            if b.r.get(k, 0) < v:
                b.r[k] = v
        for b in writes:
            b.w = tok
            b.r = {}

    def op(self, eng, fn, reads=(), writes=()):
        reads = self._flat(reads)
        writes = self._flat(writes)
        self._deps(eng, reads, writes)
        self.seq[eng] += 1
        tok = (eng, self.seq[eng])
        self.ops[eng].append(("op", fn, self.esem[eng]))
        self.ninst += 1
        self._mark(tok, reads, writes)
        return tok

    def dma(self, q, fn, reads=(), writes=(), sembuf=None):
        reads = self._flat(reads)
        writes = self._flat(writes)
        self._deps(q, reads, writes)
        b = sembuf
        if b.dsem is None:
            b.dsem, b.dcnt = self.free_dsems.pop()
            self.semobj[("d", b.name)] = b.dsem
        b.dcnt += 16
        tok = (("d", b.name), b.dcnt)
        self.ops[q].append(("dma", fn, b.dsem))
        self.ninst += 1
        self._mark(tok, reads, writes)
        return tok

    def wait_all(self, eng, bufs):
        self._deps(eng, (), bufs)

    def emit(self, block):
        amap = {"pe": block.tensor, "act": block.scalar, "dve": block.vector,
                "pool": block.gpsimd, "sp": block.sync}
        for e in ALL:
            lst = self.ops[e]

            def body(engobj, lst=lst):
                for item in lst:
                    if item[0] == "wait":
                        engobj.wait_ge(item[1], item[2])
                    elif item[0] == "op":
                        item[1](engobj).then_inc(item[2], 1)
                    else:
                        item[1](engobj).then_inc(item[2], 16)
            if lst:
                amap[e](body)
            self.ops[e] = []


class T:
    def __init__(self, t, name):
        self.t = t
        self.b = Buf(name)

    def __getitem__(self, k):
        return self.t[k]


def build_nc(S, stop_after=None, dbg=False, attn_only=False):
    NT = S // 128
    nc = bass.Bass("TRN2", target_bir_lowering=False)

    def din(name, shape):
        if attn_only and name != "cst":
            return nc.dram_tensor(name, list(shape), F32).ap()
        return nc.dram_tensor(name, list(shape), F32, kind="ExternalInput").ap()

    x_in = din("x", [S, D])
    c_t = din("c_t", [128, 8])
    gla_w_in = din("gla_w_in", [2, D, GW])
    gla_wgu = din("gla_w_gate_up", [2, 16, 512])
    gla_bg = din("gla_b_gate", [2, 512])
    gla_ng = din("gla_norm_g", [2, 256])
    gla_w_out = din("gla_w_out", [2, D, D])
    dil_w_q = din("dil_w_q", [2, D, 3 * D])
    dil_w_out = din("dil_w_out", [2, D, D])
    kv_ada_w = din("kv_ada_w", [D, 2 * D])
    kv_ada_b = din("kv_ada_b", [1, 2 * D])
    w_kv = din("w_kv", [D, 2 * D])
    ffn_w_in = din("ffn_w_in", [4, D, 2 * FH])
    ffn_w_out = din("ffn_w_out", [4, FH, D])
    ada_w = din("ada_w", [4, D, 6 * D])
    ada_b = din("ada_b", [4, 6 * D])
    ln_g = din("ln_g", [4, 2, D])
    ln_b = din("ln_b", [4, 2, D])
    cst = din("cst", [128, 6, 128])
    out = nc.dram_tensor("out", [S, D], F32, kind="ExternalOutput").ap()

    xs = nc.dram_tensor("xs", [S, D], F32).ap()
    ada_d = nc.dram_tensor("ada_d", [5, 6 * D], F32).ap()
    kin_ = {"kind": "ExternalInput"} if attn_only else {}
    kout_ = {"kind": "ExternalOutput"} if attn_only else {}
    kT_d = nc.dram_tensor("kT_d", [D, S], BF16, **kin_).ap()
    v_d = nc.dram_tensor("v_d", [S, D], BF16, **kin_).ap()
    qT_d = nc.dram_tensor("qT_d", [3 * D, S], BF16, **kin_).ap()
    og_d = [nc.dram_tensor(f"og_d{g}", [S, D], BF16, **kout_).ap() for g in range(3)]
    md_d = [nc.dram_tensor(f"md_d{g}", [S, 32], F32, **kout_).ap() for g in range(3)]

    DX = [Buf(f"dx{t}") for t in range(NT)]
    B_ada = Buf("ada_d")
    B_kT = [Buf(f"dkT{i}") for i in range(max(1, S // 512))]
    B_vd = [Buf(f"dv{t}") for t in range(NT)]
    B_qT = [Buf(f"dqT{i}") for i in range(max(1, S // 512))]
    B_og = Buf("dog")

    with contextlib.ExitStack() as es0:
        esems = {e: es0.enter_context(nc.semaphore("s_" + e)) for e in COMPUTE}
        dsems = [es0.enter_context(nc.semaphore(f"d{i}")) for i in range(92)]
        S_ = Sched(nc, esems, dsems)
        op = S_.op
        dma = S_.dma
        uid = [0]

        def flush():
            for e_ in ALL:
                for k_ in COMPUTE:
                    if k_ != e_ and S_.known[e_].get(k_, 0) < S_.seq[k_]:
                        S_.known[e_][k_] = S_.seq[k_]
                        S_.ops[e_].append(("wait", S_.esem[k_], S_.seq[k_]))
            with nc.Block() as block:
                S_.emit(block)

        def free_dsem(tiles):
            for tt in tiles:
                b = tt.b if isinstance(tt, T) else tt
                if b.dsem is not None:
                    S_.free_dsems.append((b.dsem, b.dcnt))
                    b.dsem = None

        PS = es0.enter_context(nc.psum_tensor("PS", [128, 8, 512], F32))
        PB = [Buf(f"ps{i}") for i in range(8)]

        def sbt(es, name, shape, dt):
            uid[0] += 1
            nm = f"{name}_{uid[0]}"
            return T(es.enter_context(nc.sbuf_tensor(nm, list(shape), dt)), nm)

        cst_f = sbt(es0, "cst_f", [128, 6, 128], F32)
        ident_b = sbt(es0, "ident_b", [128, 128], BF16)
        mask4 = sbt(es0, "mask4", [128, 4, 128], F32)
        ones_b = sbt(es0, "ones_b", [1, 128], BF16)
        dma("sp", lambda e: e.dma_start(out=cst_f[:], in_=cst), writes=[cst_f.b], sembuf=cst_f.b)
        op("dve", lambda e: e.tensor_copy(out=ident_b[:], in_=cst_f[:, 0, :]), reads=[cst_f.b], writes=[ident_b.b])
        for h in range(4):
            op("dve", lambda e, h=h: e.tensor_copy(out=mask4[:, h, :], in_=cst_f[:, 1, :]), reads=[cst_f.b], writes=[mask4.b])
        op("dve", lambda e: e.memset(ones_b[:], 1.0), writes=[ones_b.b])
        tri_incl = cst_f[:, 1, :]
        tri_after = cst_f[:, 2, :]

        with contextlib.ExitStack() as es:
          if not attn_only:
            ct = sbt(es, "ct", [128, 8], F32)
            sc = sbt(es, "sc", [128, 8], F32)
            dma("sp", lambda e: e.dma_start(out=ct[:], in_=c_t), writes=[ct.b], sembuf=ct.b)
            op("act", lambda e: e.activation(out=sc[:], in_=ct[:], func=ACT.Silu), reads=[ct.b], writes=[sc.b])
            wa = [sbt(es, f"wa{i}", [128, 8, 512], F32) for i in range(3)]
            brow = sbt(es, "brow", [1, 6 * D], F32)
            arow = [sbt(es, f"arow{i}", [1, 6 * D], F32) for i in range(2)]
            cnt = 0
            for l in range(5):
                ncb = 12 if l < 4 else 4
                width = ncb * 512
                wsrc = ada_w[l] if l < 4 else kv_ada_w
                bsrc = ada_b[l:l + 1, :] if l < 4 else kv_ada_b
                ar = arow[l % 2]
                dma("sp", lambda e, bsrc=bsrc, width=width: e.dma_start(out=brow[:, 0:width], in_=bsrc),
                    writes=[brow.b], sembuf=brow.b)
                for cb in range(ncb):
                    w = wa[cnt % 3]
                    cnt += 1
                    dma("sp", lambda e, w=w, wsrc=wsrc, cb=cb: e.dma_start(
                        out=w[:], in_=wsrc[:, cb * 512:(cb + 1) * 512].rearrange("(k p) n -> p k n", p=128)),
                        writes=[w.b], sembuf=w.b)
                    pb = cnt % 2
                    for k in range(8):
                        op("pe", lambda e, w=w, k=k, pb=pb: e.matmul(PS[0:1, pb, :], lhsT=sc[:, k:k + 1], rhs=w[:, k, :],
                                                                      start=(k == 0), stop=(k == 7)),
                           reads=[sc.b, w.b], writes=[PB[pb]])
                    op("dve", lambda e, ar=ar, cb=cb, pb=pb: e.tensor_tensor(
                        out=ar[:, cb * 512:(cb + 1) * 512], in0=PS[0:1, pb, :], in1=brow[:, cb * 512:(cb + 1) * 512], op=ALU.add),
                       reads=[PB[pb], brow.b], writes=[ar.b])
                if l < 4:
                    for a0 in (1024, 4096):
                        op("dve", lambda e, ar=ar, a0=a0: e.tensor_scalar_add(out=ar[:, a0:a0 + 2048], in0=ar[:, a0:a0 + 2048], scalar1=1.0),
                           reads=[ar.b], writes=[ar.b])
                else:
                    op("dve", lambda e, ar=ar: e.tensor_scalar_add(out=ar[:, 1024:2048], in0=ar[:, 1024:2048], scalar1=1.0),
                       reads=[ar.b], writes=[ar.b])
                dma("sp", lambda e, ar=ar, l=l, width=width: e.dma_start(out=ada_d[l:l + 1, 0:width], in_=ar[:, 0:width]),
                    reads=[ar.b], writes=[B_ada], sembuf=ar.b)
            S_.wait_all("sp", [B_ada])
            flush()
            free_dsem([ct, brow] + wa + arow)

        def load_bc(es, name, src_row):
            t = sbt(es, name, [128, D], F32)
            dma("sp", lambda e: e.dma_start(out=t[:], in_=src_row.partition_broadcast(128)),
                reads=[B_ada], writes=[t.b], sembuf=t.b)
            return t

        cast_rr = [0]

        def load_w(es, name, src, kch, n, col0=0, stg=None):
            t = es.enter_context(nc.sbuf_tensor(f"{name}_{uid[0]}", [128, kch, n], BF16))
            uid[0] += 1
            bufs = []
            engs = ("dve", "pool", "act")
            for k in range(kch):
                kb = []
                for c0 in range(0, n, 1024):
                    wd = min(1024, n - c0)
                    b = Buf(f"{name}k{k}c{c0}_{uid[0]}")
                    uid[0] += 1
                    sg_ = stg[cast_rr[0] % len(stg)]
                    eng = engs[cast_rr[0] % 3]
                    cast_rr[0] += 1
                    dma("sp", lambda e, k=k, c0=c0, wd=wd, sg_=sg_: e.dma_start(
                        out=sg_[:, 0:wd], in_=src[k * 128:(k + 1) * 128, col0 + c0:col0 + c0 + wd]),
                        writes=[sg_.b], sembuf=sg_.b)
                    if eng == "act":
                        op("act", lambda e, k=k, c0=c0, wd=wd, sg_=sg_: e.copy(out=t[:, k, c0:c0 + wd], in_=sg_[:, 0:wd]),
                           reads=[sg_.b], writes=[b])
                    else:
                        op(eng, lambda e, k=k, c0=c0, wd=wd, sg_=sg_: e.tensor_copy(out=t[:, k, c0:c0 + wd], in_=sg_[:, 0:wd]),
                           reads=[sg_.b], writes=[b])
                    kb.append(b)
                bufs.append(kb)
            return t, bufs

        def src_tile(first, t):
            base = x_in if first else xs
            return base[t * 128:(t + 1) * 128, :]

        def modulate_T(xt, scp, sh, h, hT, pbank, tmp, ncol=128, col0=0):
            op("dve", lambda e: e.tensor_tensor(out=tmp[:], in0=xt[:], in1=scp[:], op=ALU.mult),
               reads=[xt.b, scp.b], writes=[tmp.b])
            op("pool", lambda e: e.tensor_tensor(out=h[:], in0=tmp[:], in1=sh[:], op=ALU.add),
               reads=[tmp.b, sh.b], writes=[h.b])
            pv = PS[:, pbank, :].bitcast(BF16)
            for k in range(8):
                op("pe", lambda e, k=k: e.transpose(out=pv[:, k * 128:(k + 1) * 128], in_=h[:, k * 128:(k + 1) * 128], identity=ident_b[:]),
                   reads=[h.b, ident_b.b], writes=[PB[pbank]])
            op("act", lambda e: e.copy(out=hT[:, :, col0:col0 + 128], in_=pv.rearrange("p (k n) -> p k n", k=8)),
               reads=[PB[pbank]], writes=[hT.b])

        def resid_ln(xt, yb0, gp, lng, lnb, tmp, z, xo, stt, mv):
            yv = PS[:, yb0:yb0 + 2, :].rearrange("p b n -> p (b n)")
            op("dve", lambda e: e.tensor_tensor(out=tmp[:], in0=yv, in1=gp[:], op=ALU.mult),
               reads=[PB[yb0], PB[yb0 + 1], gp.b], writes=[tmp.b])
            op("dve", lambda e: e.scalar_tensor_tensor(out=z[:], in0=xt[:], scalar=ALPHA, in1=tmp[:], op0=ALU.mult, op1=ALU.add),
               reads=[xt.b, tmp.b], writes=[z.b])
            for c in range(2):
                op("dve", lambda e, c=c: e.bn_stats(out=stt[:, c, :], in_=z[:, c * 512:(c + 1) * 512]), reads=[z.b], writes=[stt.b])
            op("dve", lambda e: e.bn_aggr(out=mv[:, 0:2], in_=stt[:]), reads=[stt.b], writes=[mv.b])
            op("act", lambda e: e.activation(out=mv[:, 2:3], in_=mv[:, 1:2], func=ACT.Ln, bias=LN_EPS), reads=[mv.b], writes=[mv.b])
            op("act", lambda e: e.activation(out=mv[:, 3:4], in_=mv[:, 2:3], func=ACT.Exp, scale=-0.5), reads=[mv.b], writes=[mv.b])
            op("dve", lambda e: e.scalar_tensor_tensor(out=mv[:, 4:5], in0=mv[:, 0:1], scalar=-1.0, in1=mv[:, 3:4], op0=ALU.mult, op1=ALU.mult),
               reads=[mv.b], writes=[mv.b])
            op("act", lambda e: e.activation(out=tmp[:], in_=z[:], func=ACT.Identity, scale=mv[:, 3:4], bias=mv[:, 4:5]),
               reads=[z.b, mv.b], writes=[tmp.b])
            op("dve", lambda e: e.tensor_tensor(out=z[:], in0=tmp[:], in1=lng[:], op=ALU.mult), reads=[tmp.b, lng.b], writes=[z.b])
            op("pool", lambda e: e.tensor_tensor(out=xo[:], in0=z[:], in1=lnb[:], op=ALU.add), reads=[z.b, lnb.b], writes=[xo.b])

        state = {"first": True}

        def x_dst(last):
            return out if last else xs

        def gla_phase(l, last=False):
            first = state["first"]
            state["first"] = False
            with contextlib.ExitStack() as es:
                scp = load_bc(es, "scp", ada_d[l:l + 1, 1024:2048])
                sh = load_bc(es, "sh", ada_d[l:l + 1, 0:1024])
                gp = load_bc(es, "gp", ada_d[l:l + 1, 2048:3072])
                lng = load_bc(es, "lng", ln_g[l, 0:1, :])
                lnb = load_bc(es, "lnb", ln_b[l, 0:1, :])
                ngb = sbt(es, "ngb", [128, 4, 256], F32)
                for h in range(4):
                    dma("sp", lambda e, h=h: e.dma_start(out=ngb[:, h, :], in_=gla_ng[l:l + 1, :].partition_broadcast(128)),
                        writes=[ngb.b], sembuf=ngb.b)
                xts = [sbt(es, f"xt{i}", [128, D], F32) for i in range(3)]
                xos = [sbt(es, f"xo{i}", [128, D], F32) for i in range(2)]
                tmp = sbt(es, "tmp", [128, D], F32)
                z = sbt(es, "z", [128, D], F32)
                stg = [tmp, z] + xos
                win, Bwin = load_w(es, "win", gla_w_in[l], 8, GW, stg=stg)
                wout, Bwout = load_w(es, "wout", gla_w_out[l], 8, D, stg=stg)
                wgu = sbt(es, "wgu", [16, 512], BF16)
                bgr = sbt(es, "bgr", [1, 512], BF16)
                wgu_f = sbt(es, "wgu_f", [16, 512], F32)
                bgr_f = sbt(es, "bgr_f", [1, 512], F32)
                dma("sp", lambda e: e.dma_start(out=wgu_f[:], in_=gla_wgu[l]), writes=[wgu_f.b], sembuf=wgu_f.b)
                dma("sp", lambda e: e.dma_start(out=bgr_f[:], in_=gla_bg[l:l + 1, :]), writes=[bgr_f.b], sembuf=bgr_f.b)
                op("dve", lambda e: e.tensor_copy(out=wgu[:], in_=wgu_f[:]), reads=[wgu_f.b], writes=[wgu.b])
                op("dve", lambda e: e.tensor_copy(out=bgr[:], in_=bgr_f[:]), reads=[bgr_f.b], writes=[bgr.b])
                h_ = sbt(es, "h", [128, D], BF16)
                hT = sbt(es, "hT", [128, 8, 128], BF16)
                glrT = sbt(es, "glrT", [16, 128], BF16)
                e1 = sbt(es, "e1", [128, 512], F32)
                sp_ = sbt(es, "sp", [128, 512], F32)
                E = sbt(es, "E", [128, 4, 128], F32)
                Einv = sbt(es, "Einv", [128, 4, 128], F32)
                Ea = sbt(es, "Ea", [128, 512], F32)
                qeT = sbt(es, "qeT", [128, 4, 128], BF16)
                keT = sbt(es, "keT", [128, 4, 128], BF16)
                kd = sbt(es, "kd", [128, 512], BF16)
                v_ = sbt(es, "v", [128, D], BF16)
                sr = sbt(es, "sr", [128, 4, 256], F32)
                gr = sbt(es, "gr", [128, 4, 256], F32)
                attnT = sbt(es, "attnT", [128, 4, 128], BF16)
                S32 = sbt(es, "S32", [128, 4, 256], F32)
                S16 = sbt(es, "S16", [128, 4, 256], BF16)
                osq = sbt(es, "osq", [128, 4, 256], F32)
                ss = sbt(es, "ss", [128, 8], F32)
                og = sbt(es, "og", [128, D], BF16)
                ogT = sbt(es, "ogT", [128, 8, 128], BF16)
                stt = sbt(es, "stt", [128, 2, 6], F32)
                mv = sbt(es, "mv", [128, 8], F32)
                op("dve", lambda e: e.memset(S32[:], 0.0), writes=[S32.b])
                op("dve", lambda e: e.memset(S16[:], 0.0), writes=[S16.b])

                def load_x(t):
                    xt = xts[t % 3]
                    dma("sp", lambda e: e.dma_start(out=xt[:], in_=src_tile(first, t)), reads=[DX[t]] if not first else [],
                        writes=[xt.b], sembuf=xt.b)

                load_x(0)
                if NT > 1:
                    load_x(1)
                for t in range(NT):
                    if t + 2 < NT:
                        load_x(t + 2)
                    xt = xts[t % 3]
                    xo = xos[t % 2]
                    modulate_T(xt, scp, sh, h_, hT, 0, tmp)
                    for (bank, c0) in ((1, 0), (2, 512)):
                        for m in range(4):
                            for k in range(8):
                                op("pe", lambda e, bank=bank, c0=c0, m=m, k=k: e.matmul(
                                    PS[:, bank, m * 128:(m + 1) * 128], lhsT=win[:, k, c0 + m * 128:c0 + (m + 1) * 128], rhs=hT[:, k, :],
                                    start=(k == 0), stop=(k == 7)), reads=[Bwin[k], hT.b], writes=[PB[bank]])
                    for k in range(8):
                        op("pe", lambda e, k=k: e.matmul(PS[0:16, 3, 0:128], lhsT=win[:, k, 3072:3088], rhs=hT[:, k, :],
                                                         start=(k == 0), stop=(k == 7)), reads=[Bwin[k], hT.b], writes=[PB[3]])
                    op("act", lambda e: e.copy(out=glrT[:], in_=PS[0:16, 3, 0:128]), reads=[PB[3]], writes=[glrT.b])
                    for (bank, c0) in ((4, 512), (5, 1024), (6, 1536), (7, 2048), (0, 2560)):
                        for k in range(8):
                            op("pe", lambda e, bank=bank, c0=c0, k=k: e.matmul(
                                PS[:, bank, :], lhsT=hT[:, k, :], rhs=win[:, k, c0:c0 + 512], start=(k == 0), stop=(k == 7)),
                               reads=[Bwin[k], hT.b], writes=[PB[bank]])
                    op("dve", lambda e: e.tensor_copy(out=v_[:, 0:512], in_=PS[:, 5, :]), reads=[PB[5]], writes=[v_.b])
                    op("dve", lambda e: e.tensor_copy(out=v_[:, 512:1024], in_=PS[:, 6, :]), reads=[PB[6]], writes=[v_.b])
                    op("act", lambda e: e.activation(out=sr[:, 0:2, :], in_=PS[:, 7, :].rearrange("p (h d) -> p h d", h=2), func=ACT.Silu),
                       reads=[PB[7]], writes=[sr.b])
                    op("act", lambda e: e.activation(out=sr[:, 2:4, :], in_=PS[:, 0, :].rearrange("p (h d) -> p h d", h=2), func=ACT.Silu),
                       reads=[PB[0]], writes=[sr.b])
                    op("pool", lambda e: e.tensor_tensor(out=gr[:], in0=sr[:], in1=ngb[:], op=ALU.mult), reads=[sr.b, ngb.b], writes=[gr.b])
                    op("pe", lambda e: e.matmul(PS[:, 3, :], lhsT=glrT[:], rhs=wgu[:], start=True, stop=False),
                       reads=[glrT.b, wgu.b], writes=[PB[3]])
                    op("pe", lambda e: e.matmul(PS[:, 3, :], lhsT=ones_b[:], rhs=bgr[:], start=False, stop=True),
                       reads=[ones_b.b, bgr.b], writes=[PB[3]])
                    op("act", lambda e: e.activation(out=e1[:], in_=PS[:, 3, :], func=ACT.Exp, scale=-1.0), reads=[PB[3]], writes=[e1.b])
                    op("act", lambda e: e.activation(out=sp_[:], in_=e1[:], func=ACT.Ln, bias=1.0), reads=[e1.b], writes=[sp_.b])
                    for hh in range(4):
                        op("pe", lambda e, hh=hh: e.matmul(PS[:, 5, hh * 128:(hh + 1) * 128], lhsT=sp_[:, hh * 128:(hh + 1) * 128],
                                                           rhs=tri_incl, start=True, stop=True),
                           reads=[sp_.b, cst_f.b], writes=[PB[5]])
                    op("pe", lambda e: e.matmul(PS[:, 6, :], lhsT=tri_after, rhs=sp_[:], start=True, stop=True),
                       reads=[sp_.b, cst_f.b], writes=[PB[6]])
                    p5 = PS[:, 5, :].rearrange("p (h n) -> p h n", h=4)
                    op("act", lambda e: e.activation(out=E[:], in_=p5, func=ACT.Exp, scale=-1.0 / 16), reads=[PB[5]], writes=[E.b])
                    op("act", lambda e: e.activation(out=Einv[:], in_=p5, func=ACT.Exp, scale=1.0 / 16), reads=[PB[5]], writes=[Einv.b])
                    op("act", lambda e: e.activation(out=Ea[:], in_=PS[:, 6, :], func=ACT.Exp, scale=-1.0 / 16), reads=[PB[6]], writes=[Ea.b])
                    op("dve", lambda e: e.scalar_tensor_tensor(out=qeT[:], in0=PS[:, 1, :].rearrange("p (h n) -> p h n", h=4), scalar=128.0 ** -0.5,
                                                               in1=E[:], op0=ALU.mult, op1=ALU.mult), reads=[PB[1], E.b], writes=[qeT.b])
                    op("dve", lambda e: e.tensor_tensor(out=keT[:], in0=PS[:, 2, :].rearrange("p (h n) -> p h n", h=4), in1=Einv[:], op=ALU.mult),
                       reads=[PB[2], Einv.b], writes=[keT.b])
                    op("dve", lambda e: e.tensor_tensor(out=kd[:], in0=PS[:, 4, :], in1=Ea[:], op=ALU.mult), reads=[PB[4], Ea.b], writes=[kd.b])
                    for hh in range(4):
                        op("pe", lambda e, hh=hh: e.matmul(PS[:, 7, hh * 128:(hh + 1) * 128], lhsT=keT[:, hh, :], rhs=qeT[:, hh, :], start=True, stop=True),
                           reads=[keT.b, qeT.b], writes=[PB[7]])
                    op("dve", lambda e: e.tensor_tensor(out=attnT[:], in0=PS[:, 7, :].rearrange("p (h n) -> p h n", h=4), in1=mask4[:], op=ALU.mult),
                       reads=[PB[7], mask4.b], writes=[attnT.b])
                    for hh in range(4):
                        ob = hh // 2
                        oc = (hh % 2) * 256
                        op("pe", lambda e, hh=hh, ob=ob, oc=oc: e.matmul(PS[:, ob, oc:oc + 256], lhsT=attnT[:, hh, :], rhs=v_[:, hh * 256:(hh + 1) * 256],
                                                                         start=True, stop=False), reads=[attnT.b, v_.b], writes=[PB[ob]])
                        op("pe", lambda e, hh=hh, ob=ob, oc=oc: e.matmul(PS[:, ob, oc:oc + 256], lhsT=qeT[:, hh, :], rhs=S16[:, hh, :],
                                                                         start=False, stop=True), reads=[qeT.b, S16.b], writes=[PB[ob]])
                    for hh in range(4):
                        ob = 2 + hh // 2
                        oc = (hh % 2) * 256
                        op("pe", lambda e, hh=hh, ob=ob, oc=oc: e.matmul(PS[:, ob, oc:oc + 256], lhsT=kd[:, hh * 128:(hh + 1) * 128],
                                                                         rhs=v_[:, hh * 256:(hh + 1) * 256], start=True, stop=True),
                           reads=[kd.b, v_.b], writes=[PB[ob]])
                    for hh in range(4):
                        ob = 2 + hh // 2
                        oc = (hh % 2) * 256
                        op("dve", lambda e, hh=hh, ob=ob, oc=oc: e.scalar_tensor_tensor(
                            out=S32[:, hh, :], in0=S32[:, hh, :], scalar=E[:, hh, 127:128], in1=PS[:, ob, oc:oc + 256], op0=ALU.mult, op1=ALU.add),
                           reads=[S32.b, E.b, PB[ob]], writes=[S32.b])
                    op("pool", lambda e: e.tensor_copy(out=S16[:], in_=S32[:]), reads=[S32.b], writes=[S16.b])
                    ov = PS[:, 0:2, :].rearrange("p b (h d) -> p (b h) d", h=2)
                    op("act", lambda e: e.activation(out=osq[:], in_=ov, func=ACT.Square), reads=[PB[0], PB[1]], writes=[osq.b])
                    op("dve", lambda e: e.tensor_reduce(out=ss[:, 0:4], in_=osq[:], axis=AX.X, op=ALU.add), reads=[osq.b], writes=[ss.b])
                    op("act", lambda e: e.activation(out=ss[:, 4:8], in_=ss[:, 0:4], func=ACT.Ln, scale=1.0 / 256, bias=RMS_EPS),
                       reads=[ss.b], writes=[ss.b])
                    op("act", lambda e: e.activation(out=ss[:, 0:4], in_=ss[:, 4:8], func=ACT.Exp, scale=-0.5), reads=[ss.b], writes=[ss.b])
                    for hh in range(4):
                        ob = hh // 2
                        oc = (hh % 2) * 256
                        op("dve", lambda e, hh=hh, ob=ob, oc=oc: e.scalar_tensor_tensor(
                            out=og[:, hh * 256:(hh + 1) * 256], in0=PS[:, ob, oc:oc + 256], scalar=ss[:, hh:hh + 1], in1=gr[:, hh, :],
                            op0=ALU.mult, op1=ALU.mult), reads=[PB[ob], ss.b, gr.b], writes=[og.b])
                    pv = PS[:, 4, :].bitcast(BF16)
                    for k in range(8):
                        op("pe", lambda e, k=k: e.transpose(out=pv[:, k * 128:(k + 1) * 128], in_=og[:, k * 128:(k + 1) * 128], identity=ident_b[:]),
                           reads=[og.b, ident_b.b], writes=[PB[4]])
                    op("act", lambda e: e.copy(out=ogT[:], in_=pv.rearrange("p (k n) -> p k n", k=8)), reads=[PB[4]], writes=[ogT.b])
                    for cb in range(2):
                        for k in range(8):
                            op("pe", lambda e, cb=cb, k=k: e.matmul(PS[:, 5 + cb, :], lhsT=ogT[:, k, :], rhs=wout[:, k, cb * 512:(cb + 1) * 512],
                                                                    start=(k == 0), stop=(k == 7)), reads=[ogT.b, Bwout[k]], writes=[PB[5 + cb]])
                    resid_ln(xt, 5, gp, lng, lnb, tmp, z, xo, stt, mv)
                    dma("sp", lambda e, t=t, xo=xo: e.dma_start(out=x_dst(last)[t * 128:(t + 1) * 128, :], in_=xo[:]),
                        reads=[xo.b], writes=[DX[t]], sembuf=xo.b)
                S_.wait_all("sp", DX)
                flush()
                free_dsem([scp, sh, gp, lng, lnb, ngb, wgu_f, bgr_f, tmp, z] + xts + xos)

        def ffn_phase(l, last=False):
            first = state["first"]
            state["first"] = False
            with contextlib.ExitStack() as es:
                scp = load_bc(es, "scp", ada_d[l:l + 1, 4096:5120])
                sh = load_bc(es, "sh", ada_d[l:l + 1, 3072:4096])
                gp = load_bc(es, "gp", ada_d[l:l + 1, 5120:6144])
                lng = load_bc(es, "lng", ln_g[l, 1:2, :])
                lnb = load_bc(es, "lnb", ln_b[l, 1:2, :])
                xts = [sbt(es, f"xt{i}", [128, D], F32) for i in range(2)]
                xos = [sbt(es, f"xo{i}", [128, D], F32) for i in range(2)]
                tmp = sbt(es, "tmp", [128, D], F32)
                z = sbt(es, "z", [128, D], F32)
                stg = [tmp, z] + xos
                win, Bwin = load_w(es, "fwin", ffn_w_in[l], 8, 2 * FH, stg=stg)
                wout, Bwout = load_w(es, "fwout", ffn_w_out[l], 22, D, stg=stg)
                hs = [sbt(es, f"h{i}", [128, D], BF16) for i in range(2)]
                hTs = [sbt(es, f"hT{i}", [128, 8, 128], BF16) for i in range(2)]
                tmpm = sbt(es, "tmpm", [128, D], F32)
                sgs = [sbt(es, f"sg{i}", [128, 512], F32) for i in range(1)]
                a_ = sbt(es, "a", [128, FH], BF16)
                aT = sbt(es, "aT", [128, 22, 128], BF16)
                stt = sbt(es, "stt", [128, 2, 6], F32)
                mv = sbt(es, "mv", [128, 8], F32)

                def load_x(t):
                    xt = xts[t % 2]
                    dma("sp", lambda e: e.dma_start(out=xt[:], in_=src_tile(first, t)), reads=[DX[t]] if not first else [],
                        writes=[xt.b], sembuf=xt.b)

                def front(t):
                    modulate_T(xts[t % 2], scp, sh, hs[t % 2], hTs[t % 2], 0, tmpm)

                load_x(0)
                front(0)
                for t in range(NT):
                    if t + 1 < NT:
                        load_x(t + 1)
                    xt = xts[t % 2]
                    xo = xos[t % 2]
                    hT = hTs[t % 2]
                    for j in range(6):
                        wd = 512 if j < 5 else 256
                        gb = 1 + 2 * (j % 2)
                        ub = gb + 1
                        sg = sgs[0]
                        for (bank, c0) in ((gb, j * 512), (ub, FH + j * 512)):
                            for k in range(8):
                                op("pe", lambda e, bank=bank, c0=c0, k=k, wd=wd, hT=hT: e.matmul(
                                    PS[:, bank, 0:wd], lhsT=hT[:, k, :], rhs=win[:, k, c0:c0 + wd], start=(k == 0), stop=(k == 7)),
                                   reads=[hT.b, Bwin[k]], writes=[PB[bank]])
                        op("act", lambda e, gb=gb, wd=wd, sg=sg: e.activation(out=sg[:, 0:wd], in_=PS[:, gb, 0:wd], func=ACT.Silu),
                           reads=[PB[gb]], writes=[sg.b])
                        op("dve", lambda e, ub=ub, wd=wd, j=j, sg=sg: e.tensor_tensor(out=a_[:, j * 512:j * 512 + wd], in0=PS[:, ub, 0:wd], in1=sg[:, 0:wd], op=ALU.mult),
                           reads=[PB[ub], sg.b], writes=[a_.b])
                    if t + 1 < NT:
                        front(t + 1)
                    for rnd in range(3):
                        bank = 5 if rnd % 2 == 0 else 0
                        n = 8 if rnd < 2 else 6
                        pv = PS[:, bank, :].bitcast(BF16)
                        for i in range(n):
                            kk = rnd * 8 + i
                            op("pe", lambda e, i=i, kk=kk, pv=pv: e.transpose(out=pv[:, i * 128:(i + 1) * 128], in_=a_[:, kk * 128:(kk + 1) * 128], identity=ident_b[:]),
                               reads=[a_.b, ident_b.b], writes=[PB[bank]])
                        op("act", lambda e, rnd=rnd, n=n, pv=pv: e.copy(out=aT[:, rnd * 8:rnd * 8 + n, :], in_=pv[:, 0:n * 128].rearrange("p (k n) -> p k n", k=n)),
                           reads=[PB[bank]], writes=[aT.b])
                    for cb in range(2):
                        for k in range(22):
                            op("pe", lambda e, cb=cb, k=k: e.matmul(PS[:, 6 + cb, :], lhsT=aT[:, k, :], rhs=wout[:, k, cb * 512:(cb + 1) * 512],
                                                                    start=(k == 0), stop=(k == 21)), reads=[aT.b, Bwout[k]], writes=[PB[6 + cb]])
                    resid_ln(xt, 6, gp, lng, lnb, tmp, z, xo, stt, mv)
                    dma("sp", lambda e, t=t, xo=xo: e.dma_start(out=x_dst(last)[t * 128:(t + 1) * 128, :], in_=xo[:]),
                        reads=[xo.b], writes=[DX[t]], sembuf=xo.b)
                S_.wait_all("sp", DX)
                flush()
                free_dsem([scp, sh, gp, lng, lnb, tmp, z] + xts + xos)


        def proj_phase(row, wsrc, nfm, fm_col0, dstT, BdT, fm_scale, tok_cols=None):
            with contextlib.ExitStack() as es:
                scp = load_bc(es, "scp", ada_d[row:row + 1, 1024:2048])
                sh = load_bc(es, "sh", ada_d[row:row + 1, 0:1024])
                ncols = fm_col0 + nfm * 128 if tok_cols is None else max(fm_col0 + nfm * 128, tok_cols[0] + tok_cols[1])
                xts = [sbt(es, f"xt{i}", [128, D], F32) for i in range(3)]
                tmp = sbt(es, "tmp", [128, D], F32)
                stg2 = sbt(es, "stg2", [128, D], F32)
                w, Bw = load_w(es, "pw", wsrc, 8, ncols, stg=[tmp, stg2])
                h_ = sbt(es, "h", [128, D], BF16)
                hT4 = [sbt(es, f"hT4{i}", [128, 8, 512], BF16) for i in range(2)]
                fmo = [sbt(es, f"fmo{i}", [128, nfm, 512], BF16) for i in range(2)]
                vo = [sbt(es, f"vo{i}", [128, D], BF16) for i in range(2)]

                def load_x(t):
                    xt = xts[t % 3]
                    dma("sp", lambda e: e.dma_start(out=xt[:], in_=xs[t * 128:(t + 1) * 128, :]), reads=[DX[t]],
                        writes=[xt.b], sembuf=xt.b)
                load_x(0)
                load_x(1)
                for blk in range(S // 512):
                    hT = hT4[blk % 2]
                    fo = fmo[blk % 2]
                    for tt in range(4):
                        t = blk * 4 + tt
                        if t + 2 < NT:
                            load_x(t + 2)
                        modulate_T(xts[t % 3], scp, sh, h_, hT, 0, tmp, col0=tt * 128)
                        if tok_cols is not None:
                            v16 = vo[t % 2]
                            for cb in range(2):
                                for k in range(8):
                                    op("pe", lambda e, cb=cb, k=k, hT=hT, tt=tt: e.matmul(
                                        PS[:, 1 + cb, :], lhsT=hT[:, k, tt * 128:(tt + 1) * 128],
                                        rhs=w[:, k, tok_cols[0] + cb * 512:tok_cols[0] + (cb + 1) * 512], start=(k == 0), stop=(k == 7)),
                                       reads=[hT.b, Bw[k]], writes=[PB[1 + cb]])
                            op("dve", lambda e, v16=v16: e.tensor_copy(out=v16[:], in_=PS[:, 1:3, :].rearrange("p b n -> p (b n)")),
                               reads=[PB[1], PB[2]], writes=[v16.b])
                            dma("sp", lambda e, v16=v16, t=t: e.dma_start(out=v_d[t * 128:(t + 1) * 128, :], in_=v16[:]),
                                reads=[v16.b], writes=[B_vd[t]], sembuf=v16.b)
                    for m in range(nfm):
                        bank = 3 + (m % 4)
                        for k in range(8):
                            op("pe", lambda e, bank=bank, m=m, k=k, hT=hT: e.matmul(
                                PS[:, bank, :], lhsT=w[:, k, fm_col0 + m * 128:fm_col0 + (m + 1) * 128], rhs=hT[:, k, :],
                                start=(k == 0), stop=(k == 7)), reads=[hT.b, Bw[k]], writes=[PB[bank]])
                        eng = "act" if m % 2 == 0 else "dve"
                        if eng == "act":
                            op("act", lambda e, bank=bank, m=m, fo=fo: e.activation(out=fo[:, m, :], in_=PS[:, bank, :], func=ACT.Identity, scale=fm_scale),
                               reads=[PB[bank]], writes=[fo.b])
                        else:
                            op("dve", lambda e, bank=bank, m=m, fo=fo: e.tensor_scalar_mul(out=fo[:, m, :], in0=PS[:, bank, :], scalar1=fm_scale),
                               reads=[PB[bank]], writes=[fo.b])
                    dma("sp", lambda e, fo=fo, blk=blk: e.dma_start(
                        out=dstT[:, blk * 512:(blk + 1) * 512].rearrange("(m p) n -> p m n", p=128), in_=fo[:]),
                        reads=[fo.b], writes=[BdT[blk]], sembuf=fo.b)
                S_.wait_all("sp", BdT + (B_vd if tok_cols is not None else []))
                flush()
                free_dsem([scp, sh, tmp, stg2] + xts + fmo + vo)

        def attn_phase():
            NMS = S // 2048
            with contextlib.ExitStack() as es:
                maskb = sbt(es, "maskb", [128, 4, 256], F32)
                maskb0 = sbt(es, "maskb0", [128, 4, 256], F32)
                for hh in range(4):
                    op("dve", lambda e, hh=hh: e.tensor_copy(out=maskb[:, hh, 0:128], in_=cst_f[:, 3, :]), reads=[cst_f.b], writes=[maskb.b])
                    op("dve", lambda e, hh=hh: e.tensor_copy(out=maskb[:, hh, 128:256], in_=cst_f[:, 4, :]), reads=[cst_f.b], writes=[maskb.b])
                    op("dve", lambda e, hh=hh: e.tensor_copy(out=maskb0[:, hh, 0:128], in_=cst_f[:, 5, :]), reads=[cst_f.b], writes=[maskb0.b])
                    op("dve", lambda e, hh=hh: e.tensor_copy(out=maskb0[:, hh, 128:256], in_=cst_f[:, 4, :]), reads=[cst_f.b], writes=[maskb0.b])
                kTb = [sbt(es, f"kTb{i}", [128, 8, 2048], BF16) for i in range(2)]
                qz = sbt(es, "qz", [128, 16, 2048], BF16)
                qzE = Buf("qzE_%d" % uid[0])
                qzO = Buf("qzO_%d" % uid[0])
                uid[0] += 1
                op("pool", lambda e: e.memset(qz[:], 0.0), writes=[qz.b, qzE, qzO])
                vvs = [sbt(es, f"vv{i}", [128, 2, D], BF16) for i in range(3)]
                sm = [sbt(es, f"sm{i}", [128, 4, 256], F32) for i in range(2)]
                pbf = [sbt(es, f"pbf{i}", [128, 4, 256], BF16) for i in range(2)]
                pTs = [sbt(es, f"pTs{i}", [128, 8, 128], BF16) for i in range(2)]
                mdt = [sbt(es, f"mdt{i}", [128, 32], F32) for i in range(2)]
                negm = [sbt(es, f"negm{i}", [128, 8], F32) for i in range(2)]
                rden = sbt(es, "rden", [128, 16], F32)
                ogt = [sbt(es, f"ogt{i}", [128, 16, 64], BF16) for i in range(2)]
                qcnt = 0
                ucnt = 0
                hcnt = 0
                for ms in range(NMS):
                    base = ms * 2048
                    kc = kTb[ms % 2]
                    kp = kTb[(ms + 1) % 2]
                    dma("sp", lambda e, kc=kc, base=base: e.dma_start(
                        out=kc[:], in_=kT_d[:, base:base + 2048].rearrange("(m p) n -> p m n", p=128)),
                        reads=B_kT, writes=[kc.b], sembuf=kc.b)
                    for g, d in enumerate(DILS):
                        if ('g0' in DBG and g != 0) or ('g1' in DBG and g != 1) or ('g2' in DBG and g != 2):
                            continue
                        qsrc = qT_d[g * D:(g + 1) * D, base:base + 2048].rearrange("(m two p) n -> two p m n", two=2, p=64)
                        dma("sp", lambda e, qsrc=qsrc: e.dma_start(out=qz[0:64, 0:16:2, :], in_=qsrc[0]),
                            reads=B_qT + [qz.b], writes=[qzE], sembuf=qzE)
                        dma("sp", lambda e, qsrc=qsrc: e.dma_start(out=qz[64:128, 1:16:2, :], in_=qsrc[1]),
                            reads=B_qT + [qz.b], writes=[qzO], sembuf=qzO)
                        nbk = 16 // d
                        vview = v_d.rearrange("(n dd) c -> dd n c", dd=d)
                        ogview = og_d[g].rearrange("(n dd) c -> dd n c", dd=d)
                        mdview = md_d[g].rearrange("(n dd) c -> dd n c", dd=d)
                        for r in range(d if 'nounits' not in DBG else 0):
                            for b in range(nbk):
                                gb0 = (ms == 0 and b == 0)
                                n0 = base // d + b * 128
                                vv = vvs[ucnt % 3]
                                md = mdt[ucnt % 2]
                                og_t = ogt[ucnt % 2]
                                ucnt += 1
                                if 'nov' in DBG:
                                    pass
                                elif gb0:
                                    dma("sp", lambda e, vv=vv, r=r, n0=n0, vview=vview: e.dma_start(out=vv[:, 1, :], in_=vview[r, n0:n0 + 128, :]),
                                        reads=B_vd, writes=[vv.b], sembuf=vv.b)
                                else:
                                    dma("sp", lambda e, vv=vv, r=r, n0=n0, vview=vview: e.dma_start(
                                        out=vv[:], in_=vview[r, n0 - 128:n0 + 128, :].rearrange("(two p) c -> p two c", p=128)),
                                        reads=B_vd, writes=[vv.b], sembuf=vv.b)
                                q0 = r + b * 128 * d
                                if b >= 1:
                                    ksrc, kp0 = kc, r + (b - 1) * 128 * d
                                elif not gb0:
                                    ksrc, kp0 = kp, r + (nbk - 1) * 128 * d
                                else:
                                    ksrc, kp0 = kc, q0
                                for hg in range(4):
                                    sb0 = 2 * (hcnt % 2)
                                    tb = 4 + (hcnt % 2)
                                    smt = sm[hcnt % 2]
                                    pb_ = pbf[hcnt % 2]
                                    pT = pTs[hcnt % 2]
                                    ng = negm[hcnt % 2]
                                    hcnt += 1
                                    for hh in range(4):
                                        hd = hg * 4 + hh
                                        c = hd // 2
                                        pr = (hd % 2) * 64
                                        bank = sb0 + hh // 2
                                        co = (hh % 2) * 256
                                        qB = qzE if hd % 2 == 0 else qzO
                                        op("pe", lambda e, bank=bank, co=co, hd=hd, c=c, ksrc=ksrc, kp0=kp0, q0=q0, d=d: e.matmul(
                                            PS[:, bank, co:co + 128], lhsT=qz[:, hd, q0:q0 + 127 * d + 1:d],
                                            rhs=ksrc[:, c, kp0:kp0 + 127 * d + 1:d], start=True, stop=True),
                                           reads=[qB, ksrc.b], writes=[PB[bank]])
                                        op("pe", lambda e, bank=bank, co=co, hd=hd, c=c, kc=kc, q0=q0, d=d: e.matmul(
                                            PS[:, bank, co + 128:co + 256], lhsT=qz[:, hd, q0:q0 + 127 * d + 1:d],
                                            rhs=kc[:, c, q0:q0 + 127 * d + 1:d], start=True, stop=True),
                                           reads=[qB, kc.b], writes=[PB[bank]])
                                    mk = maskb0 if gb0 else maskb
                                    op("dve", lambda e, sb0=sb0, smt=smt, mk=mk: e.tensor_tensor(
                                        out=smt[:], in0=PS[:, sb0:sb0 + 2, :].rearrange("p b (h n) -> p (b h) n", h=2), in1=mk[:], op=ALU.add),
                                       reads=[PB[sb0], PB[sb0 + 1], mk.b], writes=[smt.b])
                                    op("dve", lambda e, smt=smt, md=md, hg=hg: e.tensor_reduce(out=md[:, hg * 4:(hg + 1) * 4], in_=smt[:], axis=AX.X, op=ALU.max),
                                       reads=[smt.b], writes=[md.b])
                                    op("dve", lambda e, md=md, hg=hg, ng=ng: e.tensor_scalar_mul(out=ng[:, 0:4], in0=md[:, hg * 4:(hg + 1) * 4], scalar1=-1.0),
                                       reads=[md.b], writes=[ng.b])
                                    if 'stopA' in DBG:
                                        continue
                                    for hh in range(4):
                                        hd = hg * 4 + hh
                                        op("act", lambda e, hh=hh, hd=hd, smt=smt, pb_=pb_, ng=ng, md=md: e.activation(
                                            out=pb_[:, hh, :], in_=smt[:, hh, :], func=ACT.Exp, bias=ng[:, hh:hh + 1], accum_out=md[:, 16 + hd:17 + hd]),
                                           reads=[smt.b, ng.b], writes=[pb_.b, md.b])
                                    pv = PS[:, tb, :].bitcast(BF16)
                                    for hh in range(4):
                                        for half in range(2):
                                            i8 = hh * 2 + half
                                            op("pe", lambda e, i8=i8, hh=hh, half=half, pb_=pb_, pv=pv: e.transpose(
                                                out=pv[:, i8 * 128:(i8 + 1) * 128], in_=pb_[:, hh, half * 128:(half + 1) * 128], identity=ident_b[:]),
                                               reads=[pb_.b, ident_b.b], writes=[PB[tb]])
                                    op("act", lambda e, pT=pT, pv=pv: e.copy(out=pT[:], in_=pv.rearrange("p (k n) -> p k n", k=8)),
                                       reads=[PB[tb]], writes=[pT.b])
                                    if 'stopB' in DBG:
                                        continue
                                    for hh in range(4):
                                        hd = hg * 4 + hh
                                        ob = 6 + hd // 8
                                        oc = (hd % 8) * 64
                                        if not gb0:
                                            op("pe", lambda e, ob=ob, oc=oc, hh=hh, hd=hd, pT=pT, vv=vv: e.matmul(
                                                PS[:, ob, oc:oc + 64], lhsT=pT[:, hh * 2, :], rhs=vv[:, 0, hd * 64:(hd + 1) * 64], start=True, stop=False),
                                               reads=[pT.b, vv.b], writes=[PB[ob]])
                                        op("pe", lambda e, ob=ob, oc=oc, hh=hh, hd=hd, pT=pT, vv=vv, gb0=gb0: e.matmul(
                                            PS[:, ob, oc:oc + 64], lhsT=pT[:, hh * 2 + 1, :], rhs=vv[:, 1, hd * 64:(hd + 1) * 64], start=gb0, stop=True),
                                           reads=[pT.b, vv.b], writes=[PB[ob]])
                                if 'stopA' in DBG or 'stopB' in DBG or 'stopC' in DBG:
                                    continue
                                op("dve", lambda e, md=md: e.reciprocal(out=rden[:], in_=md[:, 16:32]), reads=[md.b], writes=[rden.b])
                                op("dve", lambda e, og_t=og_t: e.tensor_tensor(
                                    out=og_t[:], in0=PS[:, 6:8, :].rearrange("p b (h n) -> p (b h) n", h=8),
                                    in1=rden[:].unsqueeze(2).to_broadcast([128, 16, 64]), op=ALU.mult),
                                   reads=[PB[6], PB[7], rden.b], writes=[og_t.b])
                                if 'noog' not in DBG:
                                    dma("sp", lambda e, og_t=og_t, ogview=ogview, r=r, n0=n0: e.dma_start(
                                        out=ogview[r, n0:n0 + 128, :], in_=og_t[:].rearrange("p h n -> p (h n)")),
                                        reads=[og_t.b], writes=[B_og], sembuf=og_t.b)
                                if 'nomd' not in DBG:
                                    dma("sp", lambda e, md=md, mdview=mdview, r=r, n0=n0: e.dma_start(out=mdview[r, n0:n0 + 128, :], in_=md[:]),
                                        reads=[md.b], writes=[B_og], sembuf=md.b)
                S_.wait_all("sp", [B_og])
                flush()
                free_dsem(kTb + [qzE, qzO] + vvs + mdt + ogt)

        def comb_phase(l, last=False):
            li = l - 2
            with contextlib.ExitStack() as es:
                gp = load_bc(es, "gp", ada_d[l:l + 1, 2048:3072])
                lng = load_bc(es, "lng", ln_g[l, 0:1, :])
                lnb = load_bc(es, "lnb", ln_b[l, 0:1, :])
                xts = [sbt(es, f"xt{i}", [128, D], F32) for i in range(2)]
                xos = [sbt(es, f"xo{i}", [128, D], F32) for i in range(2)]
                wout, Bwout = load_w(es, "dwout", dil_w_out[li], 8, D, stg=xos)
                ogs = [[sbt(es, f"ogl{i}_{g}", [128, 16, 64], BF16) for g in range(3)] for i in range(2)]
                mds = [[sbt(es, f"mdl{i}_{g}", [128, 32], F32) for g in range(3)] for i in range(2)]
                tmp = sbt(es, "tmp", [128, D], F32)
                z = sbt(es, "z", [128, D], F32)
                o_ = sbt(es, "o", [128, D], BF16)
                oT = sbt(es, "oT", [128, 8, 128], BF16)
                M = sbt(es, "M", [128, 16], F32)
                ew = sbt(es, "ew", [128, 3, 16], F32)
                W = sbt(es, "W", [128, 16], F32)
                stt = sbt(es, "stt", [128, 2, 6], F32)
                mv = sbt(es, "mv", [128, 8], F32)

                def load(t):
                    i = t % 2
                    dma("sp", lambda e: e.dma_start(out=xts[i][:], in_=xs[t * 128:(t + 1) * 128, :]), reads=[DX[t]],
                        writes=[xts[i].b], sembuf=xts[i].b)
                    for g in range(3):
                        dma("sp", lambda e, g=g: e.dma_start(out=ogs[i][g][:].rearrange("p h n -> p (h n)"), in_=og_d[g][t * 128:(t + 1) * 128, :]),
                            reads=[B_og], writes=[ogs[i][g].b], sembuf=ogs[i][g].b)
                        dma("sp", lambda e, g=g: e.dma_start(out=mds[i][g][:], in_=md_d[g][t * 128:(t + 1) * 128, :]),
                            reads=[B_og], writes=[mds[i][g].b], sembuf=mds[i][g].b)
                load(0)
                for t in range(NT):
                    if t + 1 < NT:
                        load(t + 1)
                    i = t % 2
                    xt, xo, og3, md3 = xts[i], xos[i], ogs[i], mds[i]
                    op("dve", lambda e, md3=md3: e.tensor_tensor(out=M[:], in0=md3[0][:, 0:16], in1=md3[1][:, 0:16], op=ALU.max),
                       reads=[md3[0].b, md3[1].b], writes=[M.b])
                    op("dve", lambda e, md3=md3: e.tensor_tensor(out=M[:], in0=M[:], in1=md3[2][:, 0:16], op=ALU.max),
                       reads=[M.b, md3[2].b], writes=[M.b])
                    for g in range(3):
                        op("dve", lambda e, g=g, md3=md3: e.tensor_tensor(out=ew[:, g, :], in0=md3[g][:, 0:16], in1=M[:], op=ALU.subtract),
                           reads=[md3[g].b, M.b], writes=[ew.b])
                    op("act", lambda e: e.activation(out=ew[:], in_=ew[:], func=ACT.Exp), reads=[ew.b], writes=[ew.b])
                    for g in range(3):
                        op("dve", lambda e, g=g, md3=md3: e.tensor_tensor(out=ew[:, g, :], in0=ew[:, g, :], in1=md3[g][:, 16:32], op=ALU.mult),
                           reads=[ew.b, md3[g].b], writes=[ew.b])
                    op("dve", lambda e: e.tensor_tensor(out=W[:], in0=ew[:, 0, :], in1=ew[:, 1, :], op=ALU.add), reads=[ew.b], writes=[W.b])
                    op("dve", lambda e: e.tensor_tensor(out=W[:], in0=W[:], in1=ew[:, 2, :], op=ALU.add), reads=[ew.b, W.b], writes=[W.b])
                    op("dve", lambda e: e.reciprocal(out=W[:], in_=W[:]), reads=[W.b], writes=[W.b])
                    for g in range(3):
                        op("dve", lambda e, g=g: e.tensor_tensor(out=ew[:, g, :], in0=ew[:, g, :], in1=W[:], op=ALU.mult),
                           reads=[ew.b, W.b], writes=[ew.b])
                    tv = tmp[:].rearrange("p (h n) -> p h n", h=16)
                    zv = z[:].rearrange("p (h n) -> p h n", h=16)
                    op("dve", lambda e, og3=og3: e.tensor_tensor(out=tv, in0=og3[0][:], in1=ew[:, 0, :].unsqueeze(2).to_broadcast([128, 16, 64]), op=ALU.mult),
                       reads=[og3[0].b, ew.b], writes=[tmp.b])
                    op("dve", lambda e, og3=og3: e.tensor_tensor(out=zv, in0=og3[1][:], in1=ew[:, 1, :].unsqueeze(2).to_broadcast([128, 16, 64]), op=ALU.mult),
                       reads=[og3[1].b, ew.b], writes=[z.b])
                    op("dve", lambda e: e.tensor_tensor(out=tmp[:], in0=tmp[:], in1=z[:], op=ALU.add), reads=[tmp.b, z.b], writes=[tmp.b])
                    op("dve", lambda e, og3=og3: e.tensor_tensor(out=zv, in0=og3[2][:], in1=ew[:, 2, :].unsqueeze(2).to_broadcast([128, 16, 64]), op=ALU.mult),
                       reads=[og3[2].b, ew.b], writes=[z.b])
                    op("dve", lambda e: e.tensor_tensor(out=o_[:], in0=tmp[:], in1=z[:], op=ALU.add), reads=[tmp.b, z.b], writes=[o_.b])
                    pv = PS[:, 0, :].bitcast(BF16)
                    for k in range(8):
                        op("pe", lambda e, k=k: e.transpose(out=pv[:, k * 128:(k + 1) * 128], in_=o_[:, k * 128:(k + 1) * 128], identity=ident_b[:]),
                           reads=[o_.b, ident_b.b], writes=[PB[0]])
                    op("act", lambda e: e.copy(out=oT[:], in_=pv.rearrange("p (k n) -> p k n", k=8)), reads=[PB[0]], writes=[oT.b])
                    yb = 1 + 2 * (t % 2)
                    for cb in range(2):
                        for k in range(8):
                            op("pe", lambda e, cb=cb, k=k, yb=yb: e.matmul(PS[:, yb + cb, :], lhsT=oT[:, k, :], rhs=wout[:, k, cb * 512:(cb + 1) * 512],
                                                                           start=(k == 0), stop=(k == 7)), reads=[oT.b, Bwout[k]], writes=[PB[yb + cb]])
                    resid_ln(xt, yb, gp, lng, lnb, tmp, z, xo, stt, mv)
                    dma("sp", lambda e, t=t, xo=xo: e.dma_start(out=x_dst(last)[t * 128:(t + 1) * 128, :], in_=xo[:]),
                        reads=[xo.b], writes=[DX[t]], sembuf=xo.b)
                S_.wait_all("sp", DX)
                flush()
                free_dsem([gp, lng, lnb] + xts + xos + [a for b_ in ogs for a in b_] + [a for b_ in mds for a in b_])

        phases = []
        for l in range(2):
            phases.append(("gla%d" % l, lambda last, l=l: gla_phase(l, last)))
            phases.append(("ffn%d" % l, lambda last, l=l: ffn_phase(l, last)))
        for l in (2, 3):
            def dil(last, l=l):
                if l == 2:
                    proj_phase(4, w_kv, 8, 0, kT_d, B_kT, 1.0, tok_cols=(1024, 1024))
                if sub == "kv":
                    return
                proj_phase(l, dil_w_q[l - 2], 24, 0, qT_d, B_qT, 0.125)
                if sub == "q":
                    return
                attn_phase()
                if sub == "attn":
                    return
                comb_phase(l, last)
            phases.append(("dil%d" % l, dil))
            phases.append(("ffn%d" % l, lambda last, l=l: ffn_phase(l, last)))
        if attn_only:
            attn_phase()
            phases = []
            stop_after = None
        names = [p[0] for p in phases]
        sub = None
        if stop_after is not None and ":" in stop_after:
            stop_after, sub = stop_after.split(":")
        stop_idx = len(phases) - 1 if stop_after is None else names.index(stop_after)
        if attn_only:
            stop_idx = -1
        for i, (nm, fn) in enumerate(phases[:stop_idx + 1]):
            fn(i == stop_idx)
        print("instructions:", S_.ninst)
    return nc


def make_consts():
    cst = np.zeros((128, 6, 128), np.float32)
    j = np.arange(128)[:, None]
    i = np.arange(128)[None, :]
    cst[:, 0, :] = np.eye(128)
    cst[:, 1, :] = (j <= i)
    cst[:, 2, :] = (j > i)
    cst[:, 3, :] = np.where(i >= j, 0.0, NEG)
    cst[:, 4, :] = np.where(i <= j, 0.0, NEG)
    cst[:, 5, :] = NEG
    return cst


def make_in_maps(inputs, nb, S):
    cst = make_consts()
    shared = {k: np.ascontiguousarray(v) for k, v in inputs.items() if k not in ("x", "c")}
    shared["kv_ada_b"] = shared["kv_ada_b"].reshape(1, -1)
    shared["cst"] = cst
    maps = []
    for b in range(nb):
        m = dict(shared)
        m["x"] = np.ascontiguousarray(inputs["x"][b, :S])
        m["c_t"] = np.ascontiguousarray(inputs["c"][b].reshape(8, 128).T)
        maps.append(m)
    return maps


def kernel(**inputs):
    S = inputs["x"].shape[1]
    nc = build_nc(S)
    maps = make_in_maps(inputs, 8, S)
    res = run_bass_kernel_spmd(nc, maps, core_ids=list(range(8)))
    return np.stack([r["out"] for r in res.results], axis=0)
```

```python
import contextlib
import os
DBG = os.environ.get('KDBG', '')
import numpy as np
import concourse.bass as bass
import concourse.mybir as mybir
from concourse.bass_utils import run_bass_kernel_spmd

F32 = mybir.dt.float32
BF16 = mybir.dt.bfloat16
ACT = mybir.ActivationFunctionType
ALU = mybir.AluOpType
AX = mybir.AxisListType

D = 1024
DEPTH = 4
FH = 2816
ALPHA = (2.0 * DEPTH) ** 0.25
LN_EPS = 1e-5
RMS_EPS = 1e-5
GW = 3088
DILS = (1, 4, 16)
NEG = -30000.0

COMPUTE = ("pe", "act", "dve", "pool")
ALL = COMPUTE + ("sp",)


class Buf:
    __slots__ = ("name", "w", "r", "dsem", "dcnt")

    def __init__(self, name):
        self.name = name
        self.w = None
        self.r = {}
        self.dsem = None
        self.dcnt = 0


class Sched:
    def __init__(self, nc, esems, dma_sems):
        self.nc = nc
        self.ops = {e: [] for e in ALL}
        self.seq = {e: 0 for e in COMPUTE}
        self.esem = esems
        self.free_dsems = [(s_, 0) for s_ in dma_sems]
        self.known = {e: {} for e in ALL}
        self.semobj = dict(esems)
        self.ninst = 0

    def _need(self, eng, tok, acc):
        if tok is None:
            return
        k, v = tok
        if k == eng and eng == "pe":
            return
        if acc.get(k, 0) < v:
            acc[k] = v

    @staticmethod
    def _flat(lst):
        out = []
        for b in lst:
            if isinstance(b, (list, tuple)):
                out.extend(b)
            else:
                out.append(b)
        return out

    def _deps(self, eng, reads, writes):
        acc = {}
        for b in reads:
            self._need(eng, b.w, acc)
        for b in writes:
            self._need(eng, b.w, acc)
            for k, v in b.r.items():
                self._need(eng, (k, v), acc)
        kn = self.known[eng]
        for k, v in acc.items():
            if kn.get(k, 0) < v:
                kn[k] = v
                self.ops[eng].append(("wait", self.semobj[k], v))
                self.ninst += 1

    def _mark(self, tok, reads, writes):
        k, v = tok
        for b in reads:
            if b.r.get(k, 0) < v:
                b.r[k] = v
        for b in writes:
            b.w = tok
            b.r = {}

    def op(self, eng, fn, reads=(), writes=()):
        reads = self._flat(reads)
        writes = self._flat(writes)
        self._deps(eng, reads, writes)
        self.seq[eng] += 1
        tok = (eng, self.seq[eng])
        self.ops[eng].append(("op", fn, self.esem[eng]))
        self.ninst += 1
        self._mark(tok, reads, writes)
        return tok

    def dma(self, q, fn, reads=(), writes=(), sembuf=None):
        reads = self._flat(reads)
        writes = self._flat(writes)
        self._deps(q, reads, writes)
        b = sembuf
        if b.dsem is None:
            b.dsem, b.dcnt = self.free_dsems.pop()
            self.semobj[("d", b.name)] = b.dsem
        b.dcnt += 16
        tok = (("d", b.name), b.dcnt)
        self.ops[q].append(("dma", fn, b.dsem))
        self.ninst += 1
        self._mark(tok, reads, writes)
        return tok

    def wait_all(self, eng, bufs):
        self._deps(eng, (), bufs)

    def emit(self, block):
        amap = {"pe": block.tensor, "act": block.scalar, "dve": block.vector,
                "pool": block.gpsimd, "sp": block.sync}
        for e in ALL:
            lst = self.ops[e]

            def body(engobj, lst=lst):
                for item in lst:
                    if item[0] == "wait":
                        engobj.wait_ge(item[1], item[2])
                    elif item[0] == "op":
                        item[1](engobj).then_inc(item[2], 1)
                    else:
                        item[1](engobj).then_inc(item[2], 16)
            if lst:
                amap[e](body)
            self.ops[e] = []


class T:
    def __init__(self, t, name):
        self.t = t
        self.b = Buf(name)

    def __getitem__(self, k):
        return self.t[k]


def build_nc(S, stop_after=None, dbg=False, attn_only=False):
    NT = S // 128
    nc = bass.Bass("TRN2", target_bir_lowering=False)

    def din(name, shape):
        if attn_only and name != "cst":
            return nc.dram_tensor(name, list(shape), F32).ap()
        return nc.dram_tensor(name, list(shape), F32, kind="ExternalInput").ap()

    x_in = din("x", [S, D])
    c_t = din("c_t", [128, 8])
    gla_w_in = din("gla_w_in", [2, D, GW])
    gla_wgu = din("gla_w_gate_up", [2, 16, 512])
    gla_bg = din("gla_b_gate", [2, 512])
    gla_ng = din("gla_norm_g", [2, 256])
    gla_w_out = din("gla_w_out", [2, D, D])
    dil_w_q = din("dil_w_q", [2, D, 3 * D])
    dil_w_out = din("dil_w_out", [2, D, D])
    kv_ada_w = din("kv_ada_w", [D, 2 * D])
    kv_ada_b = din("kv_ada_b", [1, 2 * D])
    w_kv = din("w_kv", [D, 2 * D])
    ffn_w_in = din("ffn_w_in", [4, D, 2 * FH])
    ffn_w_out = din("ffn_w_out", [4, FH, D])
    ada_w = din("ada_w", [4, D, 6 * D])
    ada_b = din("ada_b", [4, 6 * D])
    ln_g = din("ln_g", [4, 2, D])
    ln_b = din("ln_b", [4, 2, D])
    cst = din("cst", [128, 6, 128])
    out = nc.dram_tensor("out", [S, D], F32, kind="ExternalOutput").ap()

    xs = nc.dram_tensor("xs", [S, D], F32).ap()
    ada_d = nc.dram_tensor("ada_d", [5, 6 * D], F32).ap()
    kin_ = {"kind": "ExternalInput"} if attn_only else {}
    kout_ = {"kind": "ExternalOutput"} if attn_only else {}
    kT_d = nc.dram_tensor("kT_d", [D, S], BF16, **kin_).ap()
    v_d = nc.dram_tensor("v_d", [S, D], BF16, **kin_).ap()
    qT_d = nc.dram_tensor("qT_d", [3 * D, S], BF16, **kin_).ap()
    og_d = [nc.dram_tensor(f"og_d{g}", [S, D], BF16, **kout_).ap() for g in range(3)]
    md_d = [nc.dram_tensor(f"md_d{g}", [S, 32], F32, **kout_).ap() for g in range(3)]

    DX = [Buf(f"dx{t}") for t in range(NT)]
    B_ada = Buf("ada_d")
    B_kT = [Buf(f"dkT{i}") for i in range(max(1, S // 512))]
    B_vd = [Buf(f"dv{t}") for t in range(NT)]
    B_qT = [Buf(f"dqT{i}") for i in range(max(1, S // 512))]
    B_og = Buf("dog")

    with contextlib.ExitStack() as es0:
        esems = {e: es0.enter_context(nc.semaphore("s_" + e)) for e in COMPUTE}
        dsems = [es0.enter_context(nc.semaphore(f"d{i}")) for i in range(92)]
        S_ = Sched(nc, esems, dsems)
        op = S_.op
        dma = S_.dma
        uid = [0]

        def flush():
            for e_ in ALL:
                for k_ in COMPUTE:
                    if k_ != e_ and S_.known[e_].get(k_, 0) < S_.seq[k_]:
                        S_.known[e_][k_] = S_.seq[k_]
                        S_.ops[e_].append(("wait", S_.esem[k_], S_.seq[k_]))
            with nc.Block() as block:
                S_.emit(block)

        def free_dsem(tiles):
            for tt in tiles:
                b = tt.b if isinstance(tt, T) else tt
                if b.dsem is not None:
                    S_.free_dsems.append((b.dsem, b.dcnt))
                    b.dsem = None

        PS = es0.enter_context(nc.psum_tensor("PS", [128, 8, 512], F32))
        PB = [Buf(f"ps{i}") for i in range(8)]

        def sbt(es, name, shape, dt):
            uid[0] += 1
            nm = f"{name}_{uid[0]}"
            return T(es.enter_context(nc.sbuf_tensor(nm, list(shape), dt)), nm)

        cst_f = sbt(es0, "cst_f", [128, 6, 128], F32)
        ident_b = sbt(es0, "ident_b", [128, 128], BF16)
        mask4 = sbt(es0, "mask4", [128, 4, 128], F32)
        ones_b = sbt(es0, "ones_b", [1, 128], BF16)
        dma("sp", lambda e: e.dma_start(out=cst_f[:], in_=cst), writes=[cst_f.b], sembuf=cst_f.b)
        op("dve", lambda e: e.tensor_copy(out=ident_b[:], in_=cst_f[:, 0, :]), reads=[cst_f.b], writes=[ident_b.b])
        for h in range(4):
            op("dve", lambda e, h=h: e.tensor_copy(out=mask4[:, h, :], in_=cst_f[:, 1, :]), reads=[cst_f.b], writes=[mask4.b])
        op("dve", lambda e: e.memset(ones_b[:], 1.0), writes=[ones_b.b])
        tri_incl = cst_f[:, 1, :]
        tri_after = cst_f[:, 2, :]

        with contextlib.ExitStack() as es:
          if not attn_only:
            ct = sbt(es, "ct", [128, 8], F32)
            sc = sbt(es, "sc", [128, 8], F32)
            dma("sp", lambda e: e.dma_start(out=ct[:], in_=c_t), writes=[ct.b], sembuf=ct.b)
            op("act", lambda e: e.activation(out=sc[:], in_=ct[:], func=ACT.Silu), reads=[ct.b], writes=[sc.b])
            wa = [sbt(es, f"wa{i}", [128, 8, 512], F32) for i in range(3)]
            brow = sbt(es, "brow", [1, 6 * D], F32)
            arow = [sbt(es, f"arow{i}", [1, 6 * D], F32) for i in range(2)]
            cnt = 0
            for l in range(5):
                ncb = 12 if l < 4 else 4
                width = ncb * 512
                wsrc = ada_w[l] if l < 4 else kv_ada_w
                bsrc = ada_b[l:l + 1, :] if l < 4 else kv_ada_b
                ar = arow[l % 2]
                dma("sp", lambda e, bsrc=bsrc, width=width: e.dma_start(out=brow[:, 0:width], in_=bsrc),
                    writes=[brow.b], sembuf=brow.b)
                for cb in range(ncb):
                    w = wa[cnt % 3]
                    cnt += 1
                    dma("sp", lambda e, w=w, wsrc=wsrc, cb=cb: e.dma_start(
                        out=w[:], in_=wsrc[:, cb * 512:(cb + 1) * 512].rearrange("(k p) n -> p k n", p=128)),
                        writes=[w.b], sembuf=w.b)
                    pb = cnt % 2
                    for k in range(8):
                        op("pe", lambda e, w=w, k=k, pb=pb: e.matmul(PS[0:1, pb, :], lhsT=sc[:, k:k + 1], rhs=w[:, k, :],
                                                                      start=(k == 0), stop=(k == 7)),
                           reads=[sc.b, w.b], writes=[PB[pb]])
                    op("dve", lambda e, ar=ar, cb=cb, pb=pb: e.tensor_tensor(
                        out=ar[:, cb * 512:(cb + 1) * 512], in0=PS[0:1, pb, :], in1=brow[:, cb * 512:(cb + 1) * 512], op=ALU.add),
                       reads=[PB[pb], brow.b], writes=[ar.b])
                if l < 4:
                    for a0 in (1024, 4096):
                        op("dve", lambda e, ar=ar, a0=a0: e.tensor_scalar_add(out=ar[:, a0:a0 + 2048], in0=ar[:, a0:a0 + 2048], scalar1=1.0),
                           reads=[ar.b], writes=[ar.b])
                else:
                    op("dve", lambda e, ar=ar: e.tensor_scalar_add(out=ar[:, 1024:2048], in0=ar[:, 1024:2048], scalar1=1.0),
                       reads=[ar.b], writes=[ar.b])
                dma("sp", lambda e, ar=ar, l=l, width=width: e.dma_start(out=ada_d[l:l + 1, 0:width], in_=ar[:, 0:width]),
                    reads=[ar.b], writes=[B_ada], sembuf=ar.b)
            S_.wait_all("sp", [B_ada])
            flush()
            free_dsem([ct, brow] + wa + arow)

        def load_bc(es, name, src_row):
            t = sbt(es, name, [128, D], F32)
            dma("sp", lambda e: e.dma_start(out=t[:], in_=src_row.partition_broadcast(128)),
                reads=[B_ada], writes=[t.b], sembuf=t.b)
            return t

        cast_rr = [0]

        def load_w(es, name, src, kch, n, col0=0, stg=None):
            t = es.enter_context(nc.sbuf_tensor(f"{name}_{uid[0]}", [128, kch, n], BF16))
            uid[0] += 1
            bufs = []
            engs = ("dve", "pool", "act")
            for k in range(kch):
                kb = []
                for c0 in range(0, n, 1024):
                    wd = min(1024, n - c0)
                    b = Buf(f"{name}k{k}c{c0}_{uid[0]}")
                    uid[0] += 1
                    sg_ = stg[cast_rr[0] % len(stg)]
                    eng = engs[cast_rr[0] % 3]
                    cast_rr[0] += 1
                    dma("sp", lambda e, k=k, c0=c0, wd=wd, sg_=sg_: e.dma_start(
                        out=sg_[:, 0:wd], in_=src[k * 128:(k + 1) * 128, col0 + c0:col0 + c0 + wd]),
                        writes=[sg_.b], sembuf=sg_.b)
                    if eng == "act":
                        op("act", lambda e, k=k, c0=c0, wd=wd, sg_=sg_: e.copy(out=t[:, k, c0:c0 + wd], in_=sg_[:, 0:wd]),
                           reads=[sg_.b], writes=[b])
                    else:
                        op(eng, lambda e, k=k, c0=c0, wd=wd, sg_=sg_: e.tensor_copy(out=t[:, k, c0:c0 + wd], in_=sg_[:, 0:wd]),
                           reads=[sg_.b], writes=[b])
                    kb.append(b)
                bufs.append(kb)
            return t, bufs

        def src_tile(first, t):
            base = x_in if first else xs
            return base[t * 128:(t + 1) * 128, :]

        def modulate_T(xt, scp, sh, h, hT, pbank, tmp, ncol=128, col0=0):
            op("dve", lambda e: e.tensor_tensor(out=tmp[:], in0=xt[:], in1=scp[:], op=ALU.mult),
               reads=[xt.b, scp.b], writes=[tmp.b])
            op("pool", lambda e: e.tensor_tensor(out=h[:], in0=tmp[:], in1=sh[:], op=ALU.add),
               reads=[tmp.b, sh.b], writes=[h.b])
            pv = PS[:, pbank, :].bitcast(BF16)
            for k in range(8):
                op("pe", lambda e, k=k: e.transpose(out=pv[:, k * 128:(k + 1) * 128], in_=h[:, k * 128:(k + 1) * 128], identity=ident_b[:]),
                   reads=[h.b, ident_b.b], writes=[PB[pbank]])
            op("act", lambda e: e.copy(out=hT[:, :, col0:col0 + 128], in_=pv.rearrange("p (k n) -> p k n", k=8)),
               reads=[PB[pbank]], writes=[hT.b])

        def resid_ln(xt, yb0, gp, lng, lnb, tmp, z, xo, stt, mv):
            yv = PS[:, yb0:yb0 + 2, :].rearrange("p b n -> p (b n)")
            op("dve", lambda e: e.tensor_tensor(out=tmp[:], in0=yv, in1=gp[:], op=ALU.mult),
               reads=[PB[yb0], PB[yb0 + 1], gp.b], writes=[tmp.b])
            op("dve", lambda e: e.scalar_tensor_tensor(out=z[:], in0=xt[:], scalar=ALPHA, in1=tmp[:], op0=ALU.mult, op1=ALU.add),
               reads=[xt.b, tmp.b], writes=[z.b])
            for c in range(2):
                op("dve", lambda e, c=c: e.bn_stats(out=stt[:, c, :], in_=z[:, c * 512:(c + 1) * 512]), reads=[z.b], writes=[stt.b])
            op("dve", lambda e: e.bn_aggr(out=mv[:, 0:2], in_=stt[:]), reads=[stt.b], writes=[mv.b])
            op("act", lambda e: e.activation(out=mv[:, 2:3], in_=mv[:, 1:2], func=ACT.Ln, bias=LN_EPS), reads=[mv.b], writes=[mv.b])
            op("act", lambda e: e.activation(out=mv[:, 3:4], in_=mv[:, 2:3], func=ACT.Exp, scale=-0.5), reads=[mv.b], writes=[mv.b])
            op("dve", lambda e: e.scalar_tensor_tensor(out=mv[:, 4:5], in0=mv[:, 0:1], scalar=-1.0, in1=mv[:, 3:4], op0=ALU.mult, op1=ALU.mult),
               reads=[mv.b], writes=[mv.b])
            op("act", lambda e: e.activation(out=tmp[:], in_=z[:], func=ACT.Identity, scale=mv[:, 3:4], bias=mv[:, 4:5]),
               reads=[z.b, mv.b], writes=[tmp.b])
            op("dve", lambda e: e.tensor_tensor(out=z[:], in0=tmp[:], in1=lng[:], op=ALU.mult), reads=[tmp.b, lng.b], writes=[z.b])
            op("pool", lambda e: e.tensor_tensor(out=xo[:], in0=z[:], in1=lnb[:], op=ALU.add), reads=[z.b, lnb.b], writes=[xo.b])

        state = {"first": True}

        def x_dst(last):
            return out if last else xs

        def gla_phase(l, last=False):
            first = state["first"]
            state["first"] = False
            with contextlib.ExitStack() as es:
                scp = load_bc(es, "scp", ada_d[l:l + 1, 1024:2048])
                sh = load_bc(es, "sh", ada_d[l:l + 1, 0:1024])
                gp = load_bc(es, "gp", ada_d[l:l + 1, 2048:3072])
                lng = load_bc(es, "lng", ln_g[l, 0:1, :])
                lnb = load_bc(es, "lnb", ln_b[l, 0:1, :])
                ngb = sbt(es, "ngb", [128, 4, 256], F32)
                for h in range(4):
                    dma("sp", lambda e, h=h: e.dma_start(out=ngb[:, h, :], in_=gla_ng[l:l + 1, :].partition_broadcast(128)),
                        writes=[ngb.b], sembuf=ngb.b)
                xts = [sbt(es, f"xt{i}", [128, D], F32) for i in range(3)]
                xos = [sbt(es, f"xo{i}", [128, D], F32) for i in range(2)]
                tmp = sbt(es, "tmp", [128, D], F32)
                z = sbt(es, "z", [128, D], F32)
                stg = [tmp, z] + xos
                win, Bwin = load_w(es, "win", gla_w_in[l], 8, GW, stg=stg)
                wout, Bwout = load_w(es, "wout", gla_w_out[l], 8, D, stg=stg)
                wgu = sbt(es, "wgu", [16, 512], BF16)
                bgr = sbt(es, "bgr", [1, 512], BF16)
                wgu_f = sbt(es, "wgu_f", [16, 512], F32)
                bgr_f = sbt(es, "bgr_f", [1, 512], F32)
                dma("sp", lambda e: e.dma_start(out=wgu_f[:], in_=gla_wgu[l]), writes=[wgu_f.b], sembuf=wgu_f.b)
                dma("sp", lambda e: e.dma_start(out=bgr_f[:], in_=gla_bg[l:l + 1, :]), writes=[bgr_f.b], sembuf=bgr_f.b)
                op("dve", lambda e: e.tensor_copy(out=wgu[:], in_=wgu_f[:]), reads=[wgu_f.b], writes=[wgu.b])
                op("dve", lambda e: e.tensor_copy(out=bgr[:], in_=bgr_f[:]), reads=[bgr_f.b], writes=[bgr.b])
                h_ = sbt(es, "h", [128, D], BF16)
                hT = sbt(es, "hT", [128, 8, 128], BF16)
                glrT = sbt(es, "glrT", [16, 128], BF16)
                e1 = sbt(es, "e1", [128, 512], F32)
                sp_ = sbt(es, "sp", [128, 512], F32)
                E = sbt(es, "E", [128, 4, 128], F32)
                Einv = sbt(es, "Einv", [128, 4, 128], F32)
                Ea = sbt(es, "Ea", [128, 512], F32)
                qeT = sbt(es, "qeT", [128, 4, 128], BF16)
                keT = sbt(es, "keT", [128, 4, 128], BF16)
                kd = sbt(es, "kd", [128, 512], BF16)
                v_ = sbt(es, "v", [128, D], BF16)
                sr = sbt(es, "sr", [128, 4, 256], F32)
                gr = sbt(es, "gr", [128, 4, 256], F32)
                attnT = sbt(es, "attnT", [128, 4, 128], BF16)
                S32 = sbt(es, "S32", [128, 4, 256], F32)
                S16 = sbt(es, "S16", [128, 4, 256], BF16)
                osq = sbt(es, "osq", [128, 4, 256], F32)
                ss = sbt(es, "ss", [128, 8], F32)
                og = sbt(es, "og", [128, D], BF16)
                ogT = sbt(es, "ogT", [128, 8, 128], BF16)
                stt = sbt(es, "stt", [128, 2, 6], F32)
                mv = sbt(es, "mv", [128, 8], F32)
                op("dve", lambda e: e.memset(S32[:], 0.0), writes=[S32.b])
                op("dve", lambda e: e.memset(S16[:], 0.0), writes=[S16.b])

                def load_x(t):
                    xt = xts[t % 3]
                    dma("sp", lambda e: e.dma_start(out=xt[:], in_=src_tile(first, t)), reads=[DX[t]] if not first else [],
                        writes=[xt.b], sembuf=xt.b)

                load_x(0)
                if NT > 1:
                    load_x(1)
                for t in range(NT):
                    if t + 2 < NT:
                        load_x(t + 2)
                    xt = xts[t % 3]
                    xo = xos[t % 2]
                    modulate_T(xt, scp, sh, h_, hT, 0, tmp)
                    for (bank, c0) in ((1, 0), (2, 512)):
                        for m in range(4):
                            for k in range(8):
                                op("pe", lambda e, bank=bank, c0=c0, m=m, k=k: e.matmul(
                                    PS[:, bank, m * 128:(m + 1) * 128], lhsT=win[:, k, c0 + m * 128:c0 + (m + 1) * 128], rhs=hT[:, k, :],
                                    start=(k == 0), stop=(k == 7)), reads=[Bwin[k], hT.b], writes=[PB[bank]])
                    for k in range(8):
                        op("pe", lambda e, k=k: e.matmul(PS[0:16, 3, 0:128], lhsT=win[:, k, 3072:3088], rhs=hT[:, k, :],
                                                         start=(k == 0), stop=(k == 7)), reads=[Bwin[k], hT.b], writes=[PB[3]])
                    op("act", lambda e: e.copy(out=glrT[:], in_=PS[0:16, 3, 0:128]), reads=[PB[3]], writes=[glrT.b])
                    for (bank, c0) in ((4, 512), (5, 1024), (6, 1536), (7, 2048), (0, 2560)):
                        for k in range(8):
                            op("pe", lambda e, bank=bank, c0=c0, k=k: e.matmul(
                                PS[:, bank, :], lhsT=hT[:, k, :], rhs=win[:, k, c0:c0 + 512], start=(k == 0), stop=(k == 7)),
                               reads=[Bwin[k], hT.b], writes=[PB[bank]])
                    op("dve", lambda e: e.tensor_copy(out=v_[:, 0:512], in_=PS[:, 5, :]), reads=[PB[5]], writes=[v_.b])
                    op("dve", lambda e: e.tensor_copy(out=v_[:, 512:1024], in_=PS[:, 6, :]), reads=[PB[6]], writes=[v_.b])
                    op("act", lambda e: e.activation(out=sr[:, 0:2, :], in_=PS[:, 7, :].rearrange("p (h d) -> p h d", h=2), func=ACT.Silu),
                       reads=[PB[7]], writes=[sr.b])
                    op("act", lambda e: e.activation(out=sr[:, 2:4, :], in_=PS[:, 0, :].rearrange("p (h d) -> p h d", h=2), func=ACT.Silu),
                       reads=[PB[0]], writes=[sr.b])
                    op("pool", lambda e: e.tensor_tensor(out=gr[:], in0=sr[:], in1=ngb[:], op=ALU.mult), reads=[sr.b, ngb.b], writes=[gr.b])
                    op("pe", lambda e: e.matmul(PS[:, 3, :], lhsT=glrT[:], rhs=wgu[:], start=True, stop=False),
                       reads=[glrT.b, wgu.b], writes=[PB[3]])
                    op("pe", lambda e: e.matmul(PS[:, 3, :], lhsT=ones_b[:], rhs=bgr[:], start=False, stop=True),
                       reads=[ones_b.b, bgr.b], writes=[PB[3]])
                    op("act", lambda e: e.activation(out=e1[:], in_=PS[:, 3, :], func=ACT.Exp, scale=-1.0), reads=[PB[3]], writes=[e1.b])
                    op("act", lambda e: e.activation(out=sp_[:], in_=e1[:], func=ACT.Ln, bias=1.0), reads=[e1.b], writes=[sp_.b])
                    for hh in range(4):
                        op("pe", lambda e, hh=hh: e.matmul(PS[:, 5, hh * 128:(hh + 1) * 128], lhsT=sp_[:, hh * 128:(hh + 1) * 128],
                                                           rhs=tri_incl, start=True, stop=True),
                           reads=[sp_.b, cst_f.b], writes=[PB[5]])
                    op("pe", lambda e: e.matmul(PS[:, 6, :], lhsT=tri_after, rhs=sp_[:], start=True, stop=True),
                       reads=[sp_.b, cst_f.b], writes=[PB[6]])
                    p5 = PS[:, 5, :].rearrange("p (h n) -> p h n", h=4)
                    op("act", lambda e: e.activation(out=E[:], in_=p5, func=ACT.Exp, scale=-1.0 / 16), reads=[PB[5]], writes=[E.b])
                    op("act", lambda e: e.activation(out=Einv[:], in_=p5, func=ACT.Exp, scale=1.0 / 16), reads=[PB[5]], writes=[Einv.b])
                    op("act", lambda e: e.activation(out=Ea[:], in_=PS[:, 6, :], func=ACT.Exp, scale=-1.0 / 16), reads=[PB[6]], writes=[Ea.b])
                    op("dve", lambda e: e.scalar_tensor_tensor(out=qeT[:], in0=PS[:, 1, :].rearrange("p (h n) -> p h n", h=4), scalar=128.0 ** -0.5,
                                                               in1=E[:], op0=ALU.mult, op1=ALU.mult), reads=[PB[1], E.b], writes=[qeT.b])
                    op("dve", lambda e: e.tensor_tensor(out=keT[:], in0=PS[:, 2, :].rearrange("p (h n) -> p h n", h=4), in1=Einv[:], op=ALU.mult),
                       reads=[PB[2], Einv.b], writes=[keT.b])
                    op("dve", lambda e: e.tensor_tensor(out=kd[:], in0=PS[:, 4, :], in1=Ea[:], op=ALU.mult), reads=[PB[4], Ea.b], writes=[kd.b])
                    for hh in range(4):
                        op("pe", lambda e, hh=hh: e.matmul(PS[:, 7, hh * 128:(hh + 1) * 128], lhsT=keT[:, hh, :], rhs=qeT[:, hh, :], start=True, stop=True),
                           reads=[keT.b, qeT.b], writes=[PB[7]])
                    op("dve", lambda e: e.tensor_tensor(out=attnT[:], in0=PS[:, 7, :].rearrange("p (h n) -> p h n", h=4), in1=mask4[:], op=ALU.mult),
                       reads=[PB[7], mask4.b], writes=[attnT.b])
                    for hh in range(4):
                        ob = hh // 2
                        oc = (hh % 2) * 256
                        op("pe", lambda e, hh=hh, ob=ob, oc=oc: e.matmul(PS[:, ob, oc:oc + 256], lhsT=attnT[:, hh, :], rhs=v_[:, hh * 256:(hh + 1) * 256],
                                                                         start=True, stop=False), reads=[attnT.b, v_.b], writes=[PB[ob]])
                        op("pe", lambda e, hh=hh, ob=ob, oc=oc: e.matmul(PS[:, ob, oc:oc + 256], lhsT=qeT[:, hh, :], rhs=S16[:, hh, :],
                                                                         start=False, stop=True), reads=[qeT.b, S16.b], writes=[PB[ob]])
                    for hh in range(4):
                        ob = 2 + hh // 2
                        oc = (hh % 2) * 256
                        op("pe", lambda e, hh=hh, ob=ob, oc=oc: e.matmul(PS[:, ob, oc:oc + 256], lhsT=kd[:, hh * 128:(hh + 1) * 128],
                                                                         rhs=v_[:, hh * 256:(hh + 1) * 256], start=True, stop=True),
                           reads=[kd.b, v_.b], writes=[PB[ob]])
                    for hh in range(4):
                        ob = 2 + hh // 2
                        oc = (hh % 2) * 256
                        op("dve", lambda e, hh=hh, ob=ob, oc=oc: e.scalar_tensor_tensor(
                            out=S32[:, hh, :], in0=S32[:, hh, :], scalar=E[:, hh, 127:128], in1=PS[:, ob, oc:oc + 256], op0=ALU.mult, op1=ALU.add),
                           reads=[S32.b, E.b, PB[ob]], writes=[S32.b])
                    op("pool", lambda e: e.tensor_copy(out=S16[:], in_=S32[:]), reads=[S32.b], writes=[S16.b])
                    ov = PS[:, 0:2, :].rearrange("p b (h d) -> p (b h) d", h=2)
                    op("act", lambda e: e.activation(out=osq[:], in_=ov, func=ACT.Square), reads=[PB[0], PB[1]], writes=[osq.b])
                    op("dve", lambda e: e.tensor_reduce(out=ss[:, 0:4], in_=osq[:], axis=AX.X, op=ALU.add), reads=[osq.b], writes=[ss.b])
                    op("act", lambda e: e.activation(out=ss[:, 4:8], in_=ss[:, 0:4], func=ACT.Ln, scale=1.0 / 256, bias=RMS_EPS),
                       reads=[ss.b], writes=[ss.b])
                    op("act", lambda e: e.activation(out=ss[:, 0:4], in_=ss[:, 4:8], func=ACT.Exp, scale=-0.5), reads=[ss.b], writes=[ss.b])
                    for hh in range(4):
                        ob = hh // 2
                        oc = (hh % 2) * 256
                        op("dve", lambda e, hh=hh, ob=ob, oc=oc: e.scalar_tensor_tensor(
                            out=og[:, hh * 256:(hh + 1) * 256], in0=PS[:, ob, oc:oc + 256], scalar=ss[:, hh:hh + 1], in1=gr[:, hh, :],
                            op0=ALU.mult, op1=ALU.mult), reads=[PB[ob], ss.b, gr.b], writes=[og.b])
                    pv = PS[:, 4, :].bitcast(BF16)
                    for k in range(8):
                        op("pe", lambda e, k=k: e.transpose(out=pv[:, k * 128:(k + 1) * 128], in_=og[:, k * 128:(k + 1) * 128], identity=ident_b[:]),
                           reads=[og.b, ident_b.b], writes=[PB[4]])
                    op("act", lambda e: e.copy(out=ogT[:], in_=pv.rearrange("p (k n) -> p k n", k=8)), reads=[PB[4]], writes=[ogT.b])
                    for cb in range(2):
                        for k in range(8):
                            op("pe", lambda e, cb=cb, k=k: e.matmul(PS[:, 5 + cb, :], lhsT=ogT[:, k, :], rhs=wout[:, k, cb * 512:(cb + 1) * 512],
                                                                    start=(k == 0), stop=(k == 7)), reads=[ogT.b, Bwout[k]], writes=[PB[5 + cb]])
                    resid_ln(xt, 5, gp, lng, lnb, tmp, z, xo, stt, mv)
                    dma("sp", lambda e, t=t, xo=xo: e.dma_start(out=x_dst(last)[t * 128:(t + 1) * 128, :], in_=xo[:]),
                        reads=[xo.b], writes=[DX[t]], sembuf=xo.b)
                S_.wait_all("sp", DX)
                flush()
                free_dsem([scp, sh, gp, lng, lnb, ngb, wgu_f, bgr_f, tmp, z] + xts + xos)

        def ffn_phase(l, last=False):
            first = state["first"]
            state["first"] = False
            with contextlib.ExitStack() as es:
                scp = load_bc(es, "scp", ada_d[l:l + 1, 4096:5120])
                sh = load_bc(es, "sh", ada_d[l:l + 1, 3072:4096])
                gp = load_bc(es, "gp", ada_d[l:l + 1, 5120:6144])
                lng = load_bc(es, "lng", ln_g[l, 1:2, :])
                lnb = load_bc(es, "lnb", ln_b[l, 1:2, :])
                xts = [sbt(es, f"xt{i}", [128, D], F32) for i in range(2)]
                xos = [sbt(es, f"xo{i}", [128, D], F32) for i in range(2)]
                tmp = sbt(es, "tmp", [128, D], F32)
                z = sbt(es, "z", [128, D], F32)
                stg = [tmp, z] + xos
                win, Bwin = load_w(es, "fwin", ffn_w_in[l], 8, 2 * FH, stg=stg)
                wout, Bwout = load_w(es, "fwout", ffn_w_out[l], 22, D, stg=stg)
                hs = [sbt(es, f"h{i}", [128, D], BF16) for i in range(2)]
                hTs = [sbt(es, f"hT{i}", [128, 8, 128], BF16) for i in range(2)]
                tmpm = sbt(es, "tmpm", [128, D], F32)
                sgs = [sbt(es, f"sg{i}", [128, 512], F32) for i in range(1)]
                a_ = sbt(es, "a", [128, FH], BF16)
                aT = sbt(es, "aT", [128, 22, 128], BF16)
                stt = sbt(es, "stt", [128, 2, 6], F32)
                mv = sbt(es, "mv", [128, 8], F32)

                def load_x(t):
                    xt = xts[t % 2]
                    dma("sp", lambda e: e.dma_start(out=xt[:], in_=src_tile(first, t)), reads=[DX[t]] if not first else [],
                        writes=[xt.b], sembuf=xt.b)

                def front(t):
                    modulate_T(xts[t % 2], scp, sh, hs[t % 2], hTs[t % 2], 0, tmpm)

                load_x(0)
                front(0)
                for t in range(NT):
                    if t + 1 < NT:
                        load_x(t + 1)
                    xt = xts[t % 2]
                    xo = xos[t % 2]
                    hT = hTs[t % 2]
                    for j in range(6):
                        wd = 512 if j < 5 else 256
                        gb = 1 + 2 * (j % 2)
                        ub = gb + 1
                        sg = sgs[0]
                        for (bank, c0) in ((gb, j * 512), (ub, FH + j * 512)):
                            for k in range(8):
                                op("pe", lambda e, bank=bank, c0=c0, k=k, wd=wd, hT=hT: e.matmul(
                                    PS[:, bank, 0:wd], lhsT=hT[:, k, :], rhs=win[:, k, c0:c0 + wd], start=(k == 0), stop=(k == 7)),
                                   reads=[hT.b, Bwin[k]], writes=[PB[bank]])
                        op("act", lambda e, gb=gb, wd=wd, sg=sg: e.activation(out=sg[:, 0:wd], in_=PS[:, gb, 0:wd], func=ACT.Silu),
                           reads=[PB[gb]], writes=[sg.b])
                        op("dve", lambda e, ub=ub, wd=wd, j=j, sg=sg: e.tensor_tensor(out=a_[:, j * 512:j * 512 + wd], in0=PS[:, ub, 0:wd], in1=sg[:, 0:wd], op=ALU.mult),
                           reads=[PB[ub], sg.b], writes=[a_.b])
                    if t + 1 < NT:
                        front(t + 1)
                    for rnd in range(3):
                        bank = 5 if rnd % 2 == 0 else 0
                        n = 8 if rnd < 2 else 6
                        pv = PS[:, bank, :].bitcast(BF16)
                        for i in range(n):
                            kk = rnd * 8 + i
                            op("pe", lambda e, i=i, kk=kk, pv=pv: e.transpose(out=pv[:, i * 128:(i + 1) * 128], in_=a_[:, kk * 128:(kk + 1) * 128], identity=ident_b[:]),
                               reads=[a_.b, ident_b.b], writes=[PB[bank]])
                        op("act", lambda e, rnd=rnd, n=n, pv=pv: e.copy(out=aT[:, rnd * 8:rnd * 8 + n, :], in_=pv[:, 0:n * 128].rearrange("p (k n) -> p k n", k=n)),
                           reads=[PB[bank]], writes=[aT.b])
                    for cb in range(2):
                        for k in range(22):
                            op("pe", lambda e, cb=cb, k=k: e.matmul(PS[:, 6 + cb, :], lhsT=aT[:, k, :], rhs=wout[:, k, cb * 512:(cb + 1) * 512],
                                                                    start=(k == 0), stop=(k == 21)), reads=[aT.b, Bwout[k]], writes=[PB[6 + cb]])
                    resid_ln(xt, 6, gp, lng, lnb, tmp, z, xo, stt, mv)
                    dma("sp", lambda e, t=t, xo=xo: e.dma_start(out=x_dst(last)[t * 128:(t + 1) * 128, :], in_=xo[:]),
                        reads=[xo.b], writes=[DX[t]], sembuf=xo.b)
                S_.wait_all("sp", DX)
                flush()
                free_dsem([scp, sh, gp, lng, lnb, tmp, z] + xts + xos)


        def proj_phase(row, wsrc, nfm, fm_col0, dstT, BdT, fm_scale, tok_cols=None):
            with contextlib.ExitStack() as es:
                scp = load_bc(es, "scp", ada_d[row:row + 1, 1024:2048])
                sh = load_bc(es, "sh", ada_d[row:row + 1, 0:1024])
                ncols = fm_col0 + nfm * 128 if tok_cols is None else max(fm_col0 + nfm * 128, tok_cols[0] + tok_cols[1])
                xts = [sbt(es, f"xt{i}", [128, D], F32) for i in range(3)]
                tmp = sbt(es, "tmp", [128, D], F32)
                stg2 = sbt(es, "stg2", [128, D], F32)
                w, Bw = load_w(es, "pw", wsrc, 8, ncols, stg=[tmp, stg2])
                h_ = sbt(es, "h", [128, D], BF16)
                hT4 = [sbt(es, f"hT4{i}", [128, 8, 512], BF16) for i in range(2)]
                fmo = [sbt(es, f"fmo{i}", [128, nfm, 512], BF16) for i in range(2)]
                vo = [sbt(es, f"vo{i}", [128, D], BF16) for i in range(2)]

                def load_x(t):
                    xt = xts[t % 3]
                    dma("sp", lambda e: e.dma_start(out=xt[:], in_=xs[t * 128:(t + 1) * 128, :]), reads=[DX[t]],
                        writes=[xt.b], sembuf=xt.b)
                load_x(0)
                load_x(1)
                for blk in range(S // 512):
                    hT = hT4[blk % 2]
                    fo = fmo[blk % 2]
                    for tt in range(4):
                        t = blk * 4 + tt
                        if t + 2 < NT:
                            load_x(t + 2)
                        modulate_T(xts[t % 3], scp, sh, h_, hT, 0, tmp, col0=tt * 128)
                        if tok_cols is not None:
                            v16 = vo[t % 2]
                            for cb in range(2):
                                for k in range(8):
                                    op("pe", lambda e, cb=cb, k=k, hT=hT, tt=tt: e.matmul(
                                        PS[:, 1 + cb, :], lhsT=hT[:, k, tt * 128:(tt + 1) * 128],
                                        rhs=w[:, k, tok_cols[0] + cb * 512:tok_cols[0] + (cb + 1) * 512], start=(k == 0), stop=(k == 7)),
                                       reads=[hT.b, Bw[k]], writes=[PB[1 + cb]])
                            op("dve", lambda e, v16=v16: e.tensor_copy(out=v16[:], in_=PS[:, 1:3, :].rearrange("p b n -> p (b n)")),
                               reads=[PB[1], PB[2]], writes=[v16.b])
                            dma("sp", lambda e, v16=v16, t=t: e.dma_start(out=v_d[t * 128:(t + 1) * 128, :], in_=v16[:]),
                                reads=[v16.b], writes=[B_vd[t]], sembuf=v16.b)
                    for m in range(nfm):
                        bank = 3 + (m % 4)
                        for k in range(8):
                            op("pe", lambda e, bank=bank, m=m, k=k, hT=hT: e.matmul(
                                PS[:, bank, :], lhsT=w[:, k, fm_col0 + m * 128:fm_col0 + (m + 1) * 128], rhs=hT[:, k, :],
                                start=(k == 0), stop=(k == 7)), reads=[hT.b, Bw[k]], writes=[PB[bank]])
                        eng = "act" if m % 2 == 0 else "dve"
                        if eng == "act":
                            op("act", lambda e, bank=bank, m=m, fo=fo: e.activation(out=fo[:, m, :], in_=PS[:, bank, :], func=ACT.Identity, scale=fm_scale),
                               reads=[PB[bank]], writes=[fo.b])
                        else:
                            op("dve", lambda e, bank=bank, m=m, fo=fo: e.tensor_scalar_mul(out=fo[:, m, :], in0=PS[:, bank, :], scalar1=fm_scale),
                               reads=[PB[bank]], writes=[fo.b])
                    dma("sp", lambda e, fo=fo, blk=blk: e.dma_start(
                        out=dstT[:, blk * 512:(blk + 1) * 512].rearrange("(m p) n -> p m n", p=128), in_=fo[:]),
                        reads=[fo.b], writes=[BdT[blk]], sembuf=fo.b)
                S_.wait_all("sp", BdT + (B_vd if tok_cols is not None else []))
                flush()
                free_dsem([scp, sh, tmp, stg2] + xts + fmo + vo)

        def attn_phase():
            NMS = S // 2048
            with contextlib.ExitStack() as es:
                maskb = sbt(es, "maskb", [128, 4, 256], F32)
                maskb0 = sbt(es, "maskb0", [128, 4, 256], F32)
                for hh in range(4):
                    op("dve", lambda e, hh=hh: e.tensor_copy(out=maskb[:, hh, 0:128], in_=cst_f[:, 3, :]), reads=[cst_f.b], writes=[maskb.b])
                    op("dve", lambda e, hh=hh: e.tensor_copy(out=maskb[:, hh, 128:256], in_=cst_f[:, 4, :]), reads=[cst_f.b], writes=[maskb.b])
                    op("dve", lambda e, hh=hh: e.tensor_copy(out=maskb0[:, hh, 0:128], in_=cst_f[:, 5, :]), reads=[cst_f.b], writes=[maskb0.b])
                    op("dve", lambda e, hh=hh: e.tensor_copy(out=maskb0[:, hh, 128:256], in_=cst_f[:, 4, :]), reads=[cst_f.b], writes=[maskb0.b])
                kTb = [sbt(es, f"kTb{i}", [128, 8, 2048], BF16) for i in range(2)]
                qz = sbt(es, "qz", [128, 16, 2048], BF16)
                qzE = Buf("qzE_%d" % uid[0])
                qzO = Buf("qzO_%d" % uid[0])
                uid[0] += 1
                op("pool", lambda e: e.memset(qz[:], 0.0), writes=[qz.b, qzE, qzO])
                vvs = [sbt(es, f"vv{i}", [128, 2, D], BF16) for i in range(3)]
                sm = [sbt(es, f"sm{i}", [128, 4, 256], F32) for i in range(2)]
                pbf = [sbt(es, f"pbf{i}", [128, 4, 256], BF16) for i in range(2)]
                pTs = [sbt(es, f"pTs{i}", [128, 8, 128], BF16) for i in range(2)]
                mdt = [sbt(es, f"mdt{i}", [128, 32], F32) for i in range(2)]
                negm = [sbt(es, f"negm{i}", [128, 8], F32) for i in range(2)]
                rden = sbt(es, "rden", [128, 16], F32)
                ogt = [sbt(es, f"ogt{i}", [128, 16, 64], BF16) for i in range(2)]
                items = []
                ucnt = 0
                mdMb = [Buf("mdM%d_%d" % (i_, uid[0])) for i_ in range(2)]
                mdDb = [Buf("mdD%d_%d" % (i_, uid[0])) for i_ in range(2)]
                uid[0] += 1
                for ms in range(NMS):
                    base = ms * 2048
                    for g, d in enumerate(DILS):
                        nbk = 16 // d
                        for r in range(d):
                            for b in range(nbk):
                                U = dict(ms=ms, base=base, g=g, d=d, r=r, b=b, nbk=nbk, gb0=(ms == 0 and b == 0),
                                         n0=base // d + b * 128, vv=vvs[ucnt % 3], md=mdt[ucnt % 2], og_t=ogt[ucnt % 2],
                                         mdM=mdMb[ucnt % 2], mdD=mdDb[ucnt % 2],
                                         first_of_group=(r == 0 and b == 0), first_of_ms=(g == 0 and r == 0 and b == 0))
                                ucnt += 1
                                for hg in range(4):
                                    items.append(dict(U=U, hg=hg, idx=len(items)))

                def scores(it):
                    U = it["U"]; hg = it["hg"]; par = it["idx"] % 2
                    ms, base, g, d, r, b, nbk, gb0, n0 = (U[k_] for k_ in ("ms", "base", "g", "d", "r", "b", "nbk", "gb0", "n0"))
                    kc = kTb[ms % 2]
                    kp = kTb[(ms + 1) % 2]
                    if hg == 0:
                        if U["first_of_ms"]:
                            dma("sp", lambda e: e.dma_start(out=kc[:], in_=kT_d[:, base:base + 2048].rearrange("(m p) n -> p m n", p=128)),
                                reads=B_kT, writes=[kc.b], sembuf=kc.b)
                        if U["first_of_group"]:
                            qsrc = qT_d[g * D:(g + 1) * D, base:base + 2048].rearrange("(m two p) n -> two p m n", two=2, p=64)
                            dma("sp", lambda e: e.dma_start(out=qz[0:64, 0:16:2, :], in_=qsrc[0]),
                                reads=B_qT + [qz.b], writes=[qzE], sembuf=qzE)
                            dma("sp", lambda e: e.dma_start(out=qz[64:128, 1:16:2, :], in_=qsrc[1]),
                                reads=B_qT + [qz.b], writes=[qzO], sembuf=qzO)
                        vview = v_d.rearrange("(n dd) c -> dd n c", dd=d)
                        vv = U["vv"]
                        if gb0:
                            dma("sp", lambda e: e.dma_start(out=vv[:, 1, :], in_=vview[r, n0:n0 + 128, :]),
                                reads=B_vd, writes=[vv.b], sembuf=vv.b)
                        else:
                            dma("sp", lambda e: e.dma_start(
                                out=vv[:], in_=vview[r, n0 - 128:n0 + 128, :].rearrange("(two p) c -> p two c", p=128)),
                                reads=B_vd, writes=[vv.b], sembuf=vv.b)
                    q0 = r + b * 128 * d
                    if b >= 1:
                        ksrc, kp0 = kc, r + (b - 1) * 128 * d
                    elif not gb0:
                        ksrc, kp0 = kp, r + (nbk - 1) * 128 * d
                    else:
                        ksrc, kp0 = kc, q0
                    sb0 = 2 * par
                    for hh in range(4):
                        hd = hg * 4 + hh
                        c = hd // 2
                        bank = sb0 + hh // 2
                        co = (hh % 2) * 256
                        qB = qzE if hd % 2 == 0 else qzO
                        op("pe", lambda e, bank=bank, co=co, hd=hd, c=c: e.matmul(
                            PS[:, bank, co:co + 128], lhsT=qz[:, hd, q0:q0 + 127 * d + 1:d],
                            rhs=ksrc[:, c, kp0:kp0 + 127 * d + 1:d], start=True, stop=True),
                           reads=[qB, ksrc.b], writes=[PB[bank]])
                        op("pe", lambda e, bank=bank, co=co, hd=hd, c=c: e.matmul(
                            PS[:, bank, co + 128:co + 256], lhsT=qz[:, hd, q0:q0 + 127 * d + 1:d],
                            rhs=kc[:, c, q0:q0 + 127 * d + 1:d], start=True, stop=True),
                           reads=[qB, kc.b], writes=[PB[bank]])

                def softmax(it):
                    U = it["U"]; hg = it["hg"]; par = it["idx"] % 2
                    sb0 = 2 * par
                    smt, pb_, ng, md = sm[par], pbf[par], negm[par], U["md"]
                    mk = maskb0 if U["gb0"] else maskb
                    op("dve", lambda e: e.tensor_tensor(
                        out=smt[:], in0=PS[:, sb0:sb0 + 2, :].rearrange("p b (h n) -> p (b h) n", h=2), in1=mk[:], op=ALU.add),
                       reads=[PB[sb0], PB[sb0 + 1], mk.b], writes=[smt.b])
                    op("dve", lambda e: e.tensor_reduce(out=md[:, hg * 4:(hg + 1) * 4], in_=smt[:], axis=AX.X, op=ALU.max),
                       reads=[smt.b], writes=[U["mdM"]])
                    op("dve", lambda e: e.tensor_scalar_mul(out=ng[:, 0:4], in0=md[:, hg * 4:(hg + 1) * 4], scalar1=-1.0),
                       reads=[U["mdM"]], writes=[ng.b])
                    for hh in range(4):
                        hd = hg * 4 + hh
                        op("act", lambda e, hh=hh, hd=hd: e.activation(
                            out=pb_[:, hh, :], in_=smt[:, hh, :], func=ACT.Exp, bias=ng[:, hh:hh + 1], accum_out=md[:, 16 + hd:17 + hd]),
                           reads=[smt.b, ng.b], writes=[pb_.b, U["mdD"]])

                def tail(it):
                    U = it["U"]; hg = it["hg"]; par = it["idx"] % 2
                    gb0, vv, md, og_t, d, r, n0, g = (U[k_] for k_ in ("gb0", "vv", "md", "og_t", "d", "r", "n0", "g"))
                    tb = 4 + par
                    pb_, pT = pbf[par], pTs[par]
                    pv = PS[:, tb, :].bitcast(BF16)
                    for hh in range(4):
                        for half in range(2):
                            i8 = hh * 2 + half
                            op("pe", lambda e, i8=i8, hh=hh, half=half: e.transpose(
                                out=pv[:, i8 * 128:(i8 + 1) * 128], in_=pb_[:, hh, half * 128:(half + 1) * 128], identity=ident_b[:]),
                               reads=[pb_.b, ident_b.b], writes=[PB[tb]])
                    op("act", lambda e: e.copy(out=pT[:], in_=pv.rearrange("p (k n) -> p k n", k=8)),
                       reads=[PB[tb]], writes=[pT.b])
                    for hh in range(4):
                        hd = hg * 4 + hh
                        ob = 6 + hd // 8
                        oc = (hd % 8) * 64
                        if not gb0:
                            op("pe", lambda e, ob=ob, oc=oc, hh=hh, hd=hd: e.matmul(
                                PS[:, ob, oc:oc + 64], lhsT=pT[:, hh * 2, :], rhs=vv[:, 0, hd * 64:(hd + 1) * 64], start=True, stop=False),
                               reads=[pT.b, vv.b], writes=[PB[ob]])
                        op("pe", lambda e, ob=ob, oc=oc, hh=hh, hd=hd: e.matmul(
                            PS[:, ob, oc:oc + 64], lhsT=pT[:, hh * 2 + 1, :], rhs=vv[:, 1, hd * 64:(hd + 1) * 64], start=gb0, stop=True),
                           reads=[pT.b, vv.b], writes=[PB[ob]])
                    if hg == 3:
                        ogview = og_d[g].rearrange("(n dd) c -> dd n c", dd=d)
                        mdview = md_d[g].rearrange("(n dd) c -> dd n c", dd=d)
                        op("dve", lambda e: e.reciprocal(out=rden[:], in_=md[:, 16:32]), reads=[U["mdD"]], writes=[rden.b])
                        op("dve", lambda e: e.tensor_tensor(
                            out=og_t[:], in0=PS[:, 6:8, :].rearrange("p b (h n) -> p (b h) n", h=8),
                            in1=rden[:].unsqueeze(2).to_broadcast([128, 16, 64]), op=ALU.mult),
                           reads=[PB[6], PB[7], rden.b], writes=[og_t.b])
                        dma("sp", lambda e: e.dma_start(out=ogview[r, n0:n0 + 128, :], in_=og_t[:].rearrange("p h n -> p (h n)")),
                            reads=[og_t.b], writes=[B_og], sembuf=og_t.b)
                        dma("sp", lambda e: e.dma_start(out=mdview[r, n0:n0 + 128, :], in_=md[:]),
                            reads=[U["mdM"], U["mdD"]], writes=[B_og], sembuf=md.b)

                if items:
                    scores(items[0])
                    softmax(items[0])
                for i_, it in enumerate(items):
                    if i_ + 1 < len(items):
                        scores(items[i_ + 1])
                        softmax(items[i_ + 1])
                    tail(it)
                S_.wait_all("sp", [B_og])
                flush()
                free_dsem(kTb + [qzE, qzO] + vvs + mdt + ogt)

        def comb_phase(l, last=False):
            li = l - 2
            with contextlib.ExitStack() as es:
                gp = load_bc(es, "gp", ada_d[l:l + 1, 2048:3072])
                lng = load_bc(es, "lng", ln_g[l, 0:1, :])
                lnb = load_bc(es, "lnb", ln_b[l, 0:1, :])
                xts = [sbt(es, f"xt{i}", [128, D], F32) for i in range(2)]
                xos = [sbt(es, f"xo{i}", [128, D], F32) for i in range(2)]
                wout, Bwout = load_w(es, "dwout", dil_w_out[li], 8, D, stg=xos)
                ogs = [[sbt(es, f"ogl{i}_{g}", [128, 16, 64], BF16) for g in range(3)] for i in range(2)]
                mds = [[sbt(es, f"mdl{i}_{g}", [128, 32], F32) for g in range(3)] for i in range(2)]
                tmp = sbt(es, "tmp", [128, D], F32)
                z = sbt(es, "z", [128, D], F32)
                o_ = sbt(es, "o", [128, D], BF16)
                oT = sbt(es, "oT", [128, 8, 128], BF16)
                M = sbt(es, "M", [128, 16], F32)
                ew = sbt(es, "ew", [128, 3, 16], F32)
                W = sbt(es, "W", [128, 16], F32)
                stt = sbt(es, "stt", [128, 2, 6], F32)
                mv = sbt(es, "mv", [128, 8], F32)

                def load(t):
                    i = t % 2
                    dma("sp", lambda e: e.dma_start(out=xts[i][:], in_=xs[t * 128:(t + 1) * 128, :]), reads=[DX[t]],
                        writes=[xts[i].b], sembuf=xts[i].b)
                    for g in range(3):
                        dma("sp", lambda e, g=g: e.dma_start(out=ogs[i][g][:].rearrange("p h n -> p (h n)"), in_=og_d[g][t * 128:(t + 1) * 128, :]),
                            reads=[B_og], writes=[ogs[i][g].b], sembuf=ogs[i][g].b)
                        dma("sp", lambda e, g=g: e.dma_start(out=mds[i][g][:], in_=md_d[g][t * 128:(t + 1) * 128, :]),
                            reads=[B_og], writes=[mds[i][g].b], sembuf=mds[i][g].b)
                load(0)
                for t in range(NT):
                    if t + 1 < NT:
                        load(t + 1)
                    i = t % 2
                    xt, xo, og3, md3 = xts[i], xos[i], ogs[i], mds[i]
                    op("dve", lambda e, md3=md3: e.tensor_tensor(out=M[:], in0=md3[0][:, 0:16], in1=md3[1][:, 0:16], op=ALU.max),
                       reads=[md3[0].b, md3[1].b], writes=[M.b])
                    op("dve", lambda e, md3=md3: e.tensor_tensor(out=M[:], in0=M[:], in1=md3[2][:, 0:16], op=ALU.max),
                       reads=[M.b, md3[2].b], writes=[M.b])
                    for g in range(3):
                        op("dve", lambda e, g=g, md3=md3: e.tensor_tensor(out=ew[:, g, :], in0=md3[g][:, 0:16], in1=M[:], op=ALU.subtract),
                           reads=[md3[g].b, M.b], writes=[ew.b])
                    op("act", lambda e: e.activation(out=ew[:], in_=ew[:], func=ACT.Exp), reads=[ew.b], writes=[ew.b])
                    for g in range(3):
                        op("dve", lambda e, g=g, md3=md3: e.tensor_tensor(out=ew[:, g, :], in0=ew[:, g, :], in1=md3[g][:, 16:32], op=ALU.mult),
                           reads=[ew.b, md3[g].b], writes=[ew.b])
                    op("dve", lambda e: e.tensor_tensor(out=W[:], in0=ew[:, 0, :], in1=ew[:, 1, :], op=ALU.add), reads=[ew.b], writes=[W.b])
                    op("dve", lambda e: e.tensor_tensor(out=W[:], in0=W[:], in1=ew[:, 2, :], op=ALU.add), reads=[ew.b, W.b], writes=[W.b])
                    op("dve", lambda e: e.reciprocal(out=W[:], in_=W[:]), reads=[W.b], writes=[W.b])
                    for g in range(3):
                        op("dve", lambda e, g=g: e.tensor_tensor(out=ew[:, g, :], in0=ew[:, g, :], in1=W[:], op=ALU.mult),
                           reads=[ew.b, W.b], writes=[ew.b])
                    tv = tmp[:].rearrange("p (h n) -> p h n", h=16)
                    zv = z[:].rearrange("p (h n) -> p h n", h=16)
                    op("dve", lambda e, og3=og3: e.tensor_tensor(out=tv, in0=og3[0][:], in1=ew[:, 0, :].unsqueeze(2).to_broadcast([128, 16, 64]), op=ALU.mult),
                       reads=[og3[0].b, ew.b], writes=[tmp.b])
                    op("dve", lambda e, og3=og3: e.tensor_tensor(out=zv, in0=og3[1][:], in1=ew[:, 1, :].unsqueeze(2).to_broadcast([128, 16, 64]), op=ALU.mult),
                       reads=[og3[1].b, ew.b], writes=[z.b])
                    op("dve", lambda e: e.tensor_tensor(out=tmp[:], in0=tmp[:], in1=z[:], op=ALU.add), reads=[tmp.b, z.b], writes=[tmp.b])
                    op("dve", lambda e, og3=og3: e.tensor_tensor(out=zv, in0=og3[2][:], in1=ew[:, 2, :].unsqueeze(2).to_broadcast([128, 16, 64]), op=ALU.mult),
                       reads=[og3[2].b, ew.b], writes=[z.b])
                    op("dve", lambda e: e.tensor_tensor(out=o_[:], in0=tmp[:], in1=z[:], op=ALU.add), reads=[tmp.b, z.b], writes=[o_.b])
                    pv = PS[:, 0, :].bitcast(BF16)
                    for k in range(8):
                        op("pe", lambda e, k=k: e.transpose(out=pv[:, k * 128:(k + 1) * 128], in_=o_[:, k * 128:(k + 1) * 128], identity=ident_b[:]),
                           reads=[o_.b, ident_b.b], writes=[PB[0]])
                    op("act", lambda e: e.copy(out=oT[:], in_=pv.rearrange("p (k n) -> p k n", k=8)), reads=[PB[0]], writes=[oT.b])
                    yb = 1 + 2 * (t % 2)
                    for cb in range(2):
                        for k in range(8):
                            op("pe", lambda e, cb=cb, k=k, yb=yb: e.matmul(PS[:, yb + cb, :], lhsT=oT[:, k, :], rhs=wout[:, k, cb * 512:(cb + 1) * 512],
                                                                           start=(k == 0), stop=(k == 7)), reads=[oT.b, Bwout[k]], writes=[PB[yb + cb]])
                    resid_ln(xt, yb, gp, lng, lnb, tmp, z, xo, stt, mv)
                    dma("sp", lambda e, t=t, xo=xo: e.dma_start(out=x_dst(last)[t * 128:(t + 1) * 128, :], in_=xo[:]),
                        reads=[xo.b], writes=[DX[t]], sembuf=xo.b)
                S_.wait_all("sp", DX)
                flush()
                free_dsem([gp, lng, lnb] + xts + xos + [a for b_ in ogs for a in b_] + [a for b_ in mds for a in b_])

        phases = []
        for l in range(2):
            phases.append(("gla%d" % l, lambda last, l=l: gla_phase(l, last)))
            phases.append(("ffn%d" % l, lambda last, l=l: ffn_phase(l, last)))
        for l in (2, 3):
            def dil(last, l=l):
                if l == 2:
                    proj_phase(4, w_kv, 8, 0, kT_d, B_kT, 1.0, tok_cols=(1024, 1024))
                if sub == "kv":
                    return
                proj_phase(l, dil_w_q[l - 2], 24, 0, qT_d, B_qT, 0.125)
                if sub == "q":
                    return
                attn_phase()
                if sub == "attn":
                    return
                comb_phase(l, last)
            phases.append(("dil%d" % l, dil))
            phases.append(("ffn%d" % l, lambda last, l=l: ffn_phase(l, last)))
        if attn_only:
            attn_phase()
            phases = []
            stop_after = None
        names = [p[0] for p in phases]
        sub = None
        if stop_after is not None and ":" in stop_after:
            stop_after, sub = stop_after.split(":")
        stop_idx = len(phases) - 1 if stop_after is None else names.index(stop_after)
        if attn_only:
            stop_idx = -1
        for i, (nm, fn) in enumerate(phases[:stop_idx + 1]):
            fn(i == stop_idx)
        print("instructions:", S_.ninst)
    return nc


def make_consts():
    cst = np.zeros((128, 6, 128), np.float32)
    j = np.arange(128)[:, None]
    i = np.arange(128)[None, :]
    cst[:, 0, :] = np.eye(128)
    cst[:, 1, :] = (j <= i)
    cst[:, 2, :] = (j > i)
    cst[:, 3, :] = np.where(i >= j, 0.0, NEG)
    cst[:, 4, :] = np.where(i <= j, 0.0, NEG)
    cst[:, 5, :] = NEG
    return cst


def make_in_maps(inputs, nb, S):
    cst = make_consts()
    shared = {k: np.ascontiguousarray(v) for k, v in inputs.items() if k not in ("x", "c")}
    shared["kv_ada_b"] = shared["kv_ada_b"].reshape(1, -1)
    shared["cst"] = cst
    maps = []
    for b in range(nb):
        m = dict(shared)
        m["x"] = np.ascontiguousarray(inputs["x"][b, :S])
        m["c_t"] = np.ascontiguousarray(inputs["c"][b].reshape(8, 128).T)
        maps.append(m)
    return maps


def kernel(**inputs):
    S = inputs["x"].shape[1]
    nc = build_nc(S)
    maps = make_in_maps(inputs, 8, S)
    res = run_bass_kernel_spmd(nc, maps, core_ids=list(range(8)))
    return np.stack([r["out"] for r in res.results], axis=0)
```

```python
import contextlib
import os
DBG = os.environ.get('KDBG', '')
import numpy as np
import concourse.bass as bass
import concourse.mybir as mybir
from concourse.bass_utils import run_bass_kernel_spmd

F32 = mybir.dt.float32
BF16 = mybir.dt.bfloat16
ACT = mybir.ActivationFunctionType
ALU = mybir.AluOpType
AX = mybir.AxisListType

D = 1024
DEPTH = 4
FH = 2816
ALPHA = (2.0 * DEPTH) ** 0.25
LN_EPS = 1e-5
RMS_EPS = 1e-5
GW = 3088
DILS = (1, 4, 16)
NEG = -30000.0

COMPUTE = ("pe", "act", "dve", "pool")
ALL = COMPUTE + ("sp",)


class Buf:
    __slots__ = ("name", "w", "r", "dsem", "dcnt")

    def __init__(self, name):
        self.name = name
        self.w = None
        self.r = {}
        self.dsem = None
        self.dcnt = 0


class Sched:
    def __init__(self, nc, esems, dma_sems):
        self.nc = nc
        self.ops = {e: [] for e in ALL}
        self.seq = {e: 0 for e in COMPUTE}
        self.esem = esems
        self.free_dsems = [(s_, 0) for s_ in dma_sems]
        self.known = {e: {} for e in ALL}
        self.semobj = dict(esems)
        self.ninst = 0

    def _need(self, eng, tok, acc):
        if tok is None:
            return
        k, v = tok
        if k == eng and eng == "pe":
            return
        if acc.get(k, 0) < v:
            acc[k] = v

    @staticmethod
    def _flat(lst):
        out = []
        for b in lst:
            if isinstance(b, (list, tuple)):
                out.extend(b)
            else:
                out.append(b)
        return out

    def _deps(self, eng, reads, writes):
        acc = {}
        for b in reads:
            self._need(eng, b.w, acc)
        for b in writes:
            self._need(eng, b.w, acc)
            for k, v in b.r.items():
                self._need(eng, (k, v), acc)
        kn = self.known[eng]
        for k, v in acc.items():
            if kn.get(k, 0) < v:
                kn[k] = v
                self.ops[eng].append(("wait", self.semobj[k], v))
                self.ninst += 1

    def _mark(self, tok, reads, writes):
        k, v = tok
        for b in reads:
            if b.r.get(k, 0) < v:
                b.r[k] = v
        for b in writes:
            b.w = tok
            b.r = {}

    def op(self, eng, fn, reads=(), writes=()):
        reads = self._flat(reads)
        writes = self._flat(writes)
        self._deps(eng, reads, writes)
        self.seq[eng] += 1
        tok = (eng, self.seq[eng])
        self.ops[eng].append(("op", fn, self.esem[eng]))
        self.ninst += 1
        self._mark(tok, reads, writes)
        return tok

    def dma(self, q, fn, reads=(), writes=(), sembuf=None):
        reads = self._flat(reads)
        writes = self._flat(writes)
        self._deps(q, reads, writes)
        b = sembuf
        if b.dsem is None:
            b.dsem, b.dcnt = self.free_dsems.pop()
            self.semobj[("d", b.name)] = b.dsem
        b.dcnt += 16
        tok = (("d", b.name), b.dcnt)
        self.ops[q].append(("dma", fn, b.dsem))
        self.ninst += 1
        self._mark(tok, reads, writes)
        return tok

    def wait_all(self, eng, bufs):
        self._deps(eng, (), bufs)

    def emit(self, block):
        amap = {"pe": block.tensor, "act": block.scalar, "dve": block.vector,
                "pool": block.gpsimd, "sp": block.sync}
        for e in ALL:
            lst = self.ops[e]

            def body(engobj, lst=lst):
                for item in lst:
                    if item[0] == "wait":
                        engobj.wait_ge(item[1], item[2])
                    elif item[0] == "op":
                        item[1](engobj).then_inc(item[2], 1)
                    else:
                        item[1](engobj).then_inc(item[2], 16)
            if lst:
                amap[e](body)
            self.ops[e] = []


class T:
    def __init__(self, t, name):
        self.t = t
        self.b = Buf(name)

    def __getitem__(self, k):
        return self.t[k]


def build_nc(S, stop_after=None, dbg=False, attn_only=False):
    NT = S // 128
    nc = bass.Bass("TRN2", target_bir_lowering=False)

    def din(name, shape):
        if attn_only and name != "cst":
            return nc.dram_tensor(name, list(shape), F32).ap()
        return nc.dram_tensor(name, list(shape), F32, kind="ExternalInput").ap()

    x_in = din("x", [S, D])
    c_t = din("c_t", [128, 8])
    gla_w_in = din("gla_w_in", [2, D, GW])
    gla_wgu = din("gla_w_gate_up", [2, 16, 512])
    gla_bg = din("gla_b_gate", [2, 512])
    gla_ng = din("gla_norm_g", [2, 256])
    gla_w_out = din("gla_w_out", [2, D, D])
    dil_w_q = din("dil_w_q", [2, D, 3 * D])
    dil_w_out = din("dil_w_out", [2, D, D])
    kv_ada_w = din("kv_ada_w", [D, 2 * D])
    kv_ada_b = din("kv_ada_b", [1, 2 * D])
    w_kv = din("w_kv", [D, 2 * D])
    ffn_w_in = din("ffn_w_in", [4, D, 2 * FH])
    ffn_w_out = din("ffn_w_out", [4, FH, D])
    ada_w = din("ada_w", [4, D, 6 * D])
    ada_b = din("ada_b", [4, 6 * D])
    ln_g = din("ln_g", [4, 2, D])
    ln_b = din("ln_b", [4, 2, D])
    cst = din("cst", [128, 6, 128])
    out = nc.dram_tensor("out", [S, D], F32, kind="ExternalOutput").ap()

    xs = nc.dram_tensor("xs", [S, D], F32).ap()
    ada_d = nc.dram_tensor("ada_d", [5, 6 * D], F32).ap()
    kin_ = {"kind": "ExternalInput"} if attn_only else {}
    kout_ = {"kind": "ExternalOutput"} if attn_only else {}
    kT_d = nc.dram_tensor("kT_d", [D, S], BF16, **kin_).ap()
    v_d = nc.dram_tensor("v_d", [S, D], BF16, **kin_).ap()
    qT_d = nc.dram_tensor("qT_d", [3 * D, S], BF16, **kin_).ap()
    og_d = [nc.dram_tensor(f"og_d{g}", [S, D], BF16, **kout_).ap() for g in range(3)]
    md_d = [nc.dram_tensor(f"md_d{g}", [S, 32], F32, **kout_).ap() for g in range(3)]

    DX = [Buf(f"dx{t}") for t in range(NT)]
    B_ada = Buf("ada_d")
    B_kT = [Buf(f"dkT{i}") for i in range(max(1, S // 512))]
    B_vd = [Buf(f"dv{t}") for t in range(NT)]
    B_qT = [Buf(f"dqT{i}") for i in range(max(1, S // 512))]
    B_og = Buf("dog")

    with contextlib.ExitStack() as es0:
        esems = {e: es0.enter_context(nc.semaphore("s_" + e)) for e in COMPUTE}
        dsems = [es0.enter_context(nc.semaphore(f"d{i}")) for i in range(92)]
        S_ = Sched(nc, esems, dsems)
        op = S_.op
        dma = S_.dma
        uid = [0]

        def flush():
            for e_ in ALL:
                for k_ in COMPUTE:
                    if k_ != e_ and S_.known[e_].get(k_, 0) < S_.seq[k_]:
                        S_.known[e_][k_] = S_.seq[k_]
                        S_.ops[e_].append(("wait", S_.esem[k_], S_.seq[k_]))
            with nc.Block() as block:
                S_.emit(block)

        def free_dsem(tiles):
            for tt in tiles:
                b = tt.b if isinstance(tt, T) else tt
                if b.dsem is not None:
                    S_.free_dsems.append((b.dsem, b.dcnt))
                    b.dsem = None

        PS = es0.enter_context(nc.psum_tensor("PS", [128, 8, 512], F32))
        PB = [Buf(f"ps{i}") for i in range(8)]

        def sbt(es, name, shape, dt):
            uid[0] += 1
            nm = f"{name}_{uid[0]}"
            return T(es.enter_context(nc.sbuf_tensor(nm, list(shape), dt)), nm)

        cst_f = sbt(es0, "cst_f", [128, 6, 128], F32)
        ident_b = sbt(es0, "ident_b", [128, 128], BF16)
        mask4 = sbt(es0, "mask4", [128, 4, 128], F32)
        ones_b = sbt(es0, "ones_b", [1, 128], BF16)
        dma("sp", lambda e: e.dma_start(out=cst_f[:], in_=cst), writes=[cst_f.b], sembuf=cst_f.b)
        op("dve", lambda e: e.tensor_copy(out=ident_b[:], in_=cst_f[:, 0, :]), reads=[cst_f.b], writes=[ident_b.b])
        for h in range(4):
            op("dve", lambda e, h=h: e.tensor_copy(out=mask4[:, h, :], in_=cst_f[:, 1, :]), reads=[cst_f.b], writes=[mask4.b])
        op("dve", lambda e: e.memset(ones_b[:], 1.0), writes=[ones_b.b])
        tri_incl = cst_f[:, 1, :]
        tri_after = cst_f[:, 2, :]

        with contextlib.ExitStack() as es:
          if not attn_only:
            ct = sbt(es, "ct", [128, 8], F32)
            sc = sbt(es, "sc", [128, 8], F32)
            dma("sp", lambda e: e.dma_start(out=ct[:], in_=c_t), writes=[ct.b], sembuf=ct.b)
            op("act", lambda e: e.activation(out=sc[:], in_=ct[:], func=ACT.Silu), reads=[ct.b], writes=[sc.b])
            wa = [sbt(es, f"wa{i}", [128, 8, 512], F32) for i in range(3)]
            brow = sbt(es, "brow", [1, 6 * D], F32)
            arow = [sbt(es, f"arow{i}", [1, 6 * D], F32) for i in range(2)]
            cnt = 0
            for l in range(5):
                ncb = 12 if l < 4 else 4
                width = ncb * 512
                wsrc = ada_w[l] if l < 4 else kv_ada_w
                bsrc = ada_b[l:l + 1, :] if l < 4 else kv_ada_b
                ar = arow[l % 2]
                dma("sp", lambda e, bsrc=bsrc, width=width: e.dma_start(out=brow[:, 0:width], in_=bsrc),
                    writes=[brow.b], sembuf=brow.b)
                for cb in range(ncb):
                    w = wa[cnt % 3]
                    cnt += 1
                    dma("sp", lambda e, w=w, wsrc=wsrc, cb=cb: e.dma_start(
                        out=w[:], in_=wsrc[:, cb * 512:(cb + 1) * 512].rearrange("(k p) n -> p k n", p=128)),
                        writes=[w.b], sembuf=w.b)
                    pb = cnt % 2
                    for k in range(8):
                        op("pe", lambda e, w=w, k=k, pb=pb: e.matmul(PS[0:1, pb, :], lhsT=sc[:, k:k + 1], rhs=w[:, k, :],
                                                                      start=(k == 0), stop=(k == 7)),
                           reads=[sc.b, w.b], writes=[PB[pb]])
                    op("dve", lambda e, ar=ar, cb=cb, pb=pb: e.tensor_tensor(
                        out=ar[:, cb * 512:(cb + 1) * 512], in0=PS[0:1, pb, :], in1=brow[:, cb * 512:(cb + 1) * 512], op=ALU.add),
                       reads=[PB[pb], brow.b], writes=[ar.b])
                if l < 4:
                    for a0 in (1024, 4096):
                        op("dve", lambda e, ar=ar, a0=a0: e.tensor_scalar_add(out=ar[:, a0:a0 + 2048], in0=ar[:, a0:a0 + 2048], scalar1=1.0),
                           reads=[ar.b], writes=[ar.b])
                else:
                    op("dve", lambda e, ar=ar: e.tensor_scalar_add(out=ar[:, 1024:2048], in0=ar[:, 1024:2048], scalar1=1.0),
                       reads=[ar.b], writes=[ar.b])
                dma("sp", lambda e, ar=ar, l=l, width=width: e.dma_start(out=ada_d[l:l + 1, 0:width], in_=ar[:, 0:width]),
                    reads=[ar.b], writes=[B_ada], sembuf=ar.b)
            S_.wait_all("sp", [B_ada])
            flush()
            free_dsem([ct, brow] + wa + arow)

        def load_bc(es, name, src_row):
            t = sbt(es, name, [128, D], F32)
            dma("sp", lambda e: e.dma_start(out=t[:], in_=src_row.partition_broadcast(128)),
                reads=[B_ada], writes=[t.b], sembuf=t.b)
            return t

        cast_rr = [0]

        def load_w(es, name, src, kch, n, col0=0, stg=None):
            t = es.enter_context(nc.sbuf_tensor(f"{name}_{uid[0]}", [128, kch, n], BF16))
            uid[0] += 1
            bufs = []
            engs = ("dve", "pool", "act")
            for k in range(kch):
                kb = []
                for c0 in range(0, n, 1024):
                    wd = min(1024, n - c0)
                    b = Buf(f"{name}k{k}c{c0}_{uid[0]}")
                    uid[0] += 1
                    sg_ = stg[cast_rr[0] % len(stg)]
                    eng = engs[cast_rr[0] % 3]
                    cast_rr[0] += 1
                    dma("sp", lambda e, k=k, c0=c0, wd=wd, sg_=sg_: e.dma_start(
                        out=sg_[:, 0:wd], in_=src[k * 128:(k + 1) * 128, col0 + c0:col0 + c0 + wd]),
                        writes=[sg_.b], sembuf=sg_.b)
                    if eng == "act":
                        op("act", lambda e, k=k, c0=c0, wd=wd, sg_=sg_: e.copy(out=t[:, k, c0:c0 + wd], in_=sg_[:, 0:wd]),
                           reads=[sg_.b], writes=[b])
                    else:
                        op(eng, lambda e, k=k, c0=c0, wd=wd, sg_=sg_: e.tensor_copy(out=t[:, k, c0:c0 + wd], in_=sg_[:, 0:wd]),
                           reads=[sg_.b], writes=[b])
                    kb.append(b)
                bufs.append(kb)
            return t, bufs

        def src_tile(first, t):
            base = x_in if first else xs
            return base[t * 128:(t + 1) * 128, :]

        def modulate_T(xt, scp, sh, h, hT, pbank, tmp, ncol=128, col0=0):
            op("dve", lambda e: e.tensor_tensor(out=tmp[:], in0=xt[:], in1=scp[:], op=ALU.mult),
               reads=[xt.b, scp.b], writes=[tmp.b])
            op("pool", lambda e: e.tensor_tensor(out=h[:], in0=tmp[:], in1=sh[:], op=ALU.add),
               reads=[tmp.b, sh.b], writes=[h.b])
            pv = PS[:, pbank, :].bitcast(BF16)
            for k in range(8):
                op("pe", lambda e, k=k: e.transpose(out=pv[:, k * 128:(k + 1) * 128], in_=h[:, k * 128:(k + 1) * 128], identity=ident_b[:]),
                   reads=[h.b, ident_b.b], writes=[PB[pbank]])
            op("act", lambda e: e.copy(out=hT[:, :, col0:col0 + 128], in_=pv.rearrange("p (k n) -> p k n", k=8)),
               reads=[PB[pbank]], writes=[hT.b])

        def resid_ln(xt, yb0, gp, lng, lnb, tmp, z, xo, stt, mv):
            yv = PS[:, yb0:yb0 + 2, :].rearrange("p b n -> p (b n)")
            op("dve", lambda e: e.tensor_tensor(out=tmp[:], in0=yv, in1=gp[:], op=ALU.mult),
               reads=[PB[yb0], PB[yb0 + 1], gp.b], writes=[tmp.b])
            op("dve", lambda e: e.scalar_tensor_tensor(out=z[:], in0=xt[:], scalar=ALPHA, in1=tmp[:], op0=ALU.mult, op1=ALU.add),
               reads=[xt.b, tmp.b], writes=[z.b])
            for c in range(2):
                op("dve", lambda e, c=c: e.bn_stats(out=stt[:, c, :], in_=z[:, c * 512:(c + 1) * 512]), reads=[z.b], writes=[stt.b])
            op("dve", lambda e: e.bn_aggr(out=mv[:, 0:2], in_=stt[:]), reads=[stt.b], writes=[mv.b])
            op("act", lambda e: e.activation(out=mv[:, 2:3], in_=mv[:, 1:2], func=ACT.Ln, bias=LN_EPS), reads=[mv.b], writes=[mv.b])
            op("act", lambda e: e.activation(out=mv[:, 3:4], in_=mv[:, 2:3], func=ACT.Exp, scale=-0.5), reads=[mv.b], writes=[mv.b])
            op("dve", lambda e: e.scalar_tensor_tensor(out=mv[:, 4:5], in0=mv[:, 0:1], scalar=-1.0, in1=mv[:, 3:4], op0=ALU.mult, op1=ALU.mult),
               reads=[mv.b], writes=[mv.b])
            op("act", lambda e: e.activation(out=tmp[:], in_=z[:], func=ACT.Identity, scale=mv[:, 3:4], bias=mv[:, 4:5]),
               reads=[z.b, mv.b], writes=[tmp.b])
            op("dve", lambda e: e.tensor_tensor(out=z[:], in0=tmp[:], in1=lng[:], op=ALU.mult), reads=[tmp.b, lng.b], writes=[z.b])
            op("pool", lambda e: e.tensor_tensor(out=xo[:], in0=z[:], in1=lnb[:], op=ALU.add), reads=[z.b, lnb.b], writes=[xo.b])

        state = {"first": True}

        def x_dst(last):
            return out if last else xs

        def gla_phase(l, last=False):
            first = state["first"]
            state["first"] = False
            with contextlib.ExitStack() as es:
                scp = load_bc(es, "scp", ada_d[l:l + 1, 1024:2048])
                sh = load_bc(es, "sh", ada_d[l:l + 1, 0:1024])
                gp = load_bc(es, "gp", ada_d[l:l + 1, 2048:3072])
                lng = load_bc(es, "lng", ln_g[l, 0:1, :])
                lnb = load_bc(es, "lnb", ln_b[l, 0:1, :])
                ngb = sbt(es, "ngb", [128, 4, 256], F32)
                for h in range(4):
                    dma("sp", lambda e, h=h: e.dma_start(out=ngb[:, h, :], in_=gla_ng[l:l + 1, :].partition_broadcast(128)),
                        writes=[ngb.b], sembuf=ngb.b)
                xts = [sbt(es, f"xt{i}", [128, D], F32) for i in range(3)]
                xos = [sbt(es, f"xo{i}", [128, D], F32) for i in range(2)]
                tmp = sbt(es, "tmp", [128, D], F32)
                z = sbt(es, "z", [128, D], F32)
                stg = [tmp, z] + xos
                win, Bwin = load_w(es, "win", gla_w_in[l], 8, GW, stg=stg)
                wout, Bwout = load_w(es, "wout", gla_w_out[l], 8, D, stg=stg)
                wgu = sbt(es, "wgu", [16, 512], BF16)
                bgr = sbt(es, "bgr", [1, 512], BF16)
                wgu_f = sbt(es, "wgu_f", [16, 512], F32)
                bgr_f = sbt(es, "bgr_f", [1, 512], F32)
                dma("sp", lambda e: e.dma_start(out=wgu_f[:], in_=gla_wgu[l]), writes=[wgu_f.b], sembuf=wgu_f.b)
                dma("sp", lambda e: e.dma_start(out=bgr_f[:], in_=gla_bg[l:l + 1, :]), writes=[bgr_f.b], sembuf=bgr_f.b)
                op("dve", lambda e: e.tensor_copy(out=wgu[:], in_=wgu_f[:]), reads=[wgu_f.b], writes=[wgu.b])
                op("dve", lambda e: e.tensor_copy(out=bgr[:], in_=bgr_f[:]), reads=[bgr_f.b], writes=[bgr.b])
                h_ = sbt(es, "h", [128, D], BF16)
                hT = sbt(es, "hT", [128, 8, 128], BF16)
                glrT = sbt(es, "glrT", [16, 128], BF16)
                e1 = sbt(es, "e1", [128, 512], F32)
                sp_ = sbt(es, "sp", [128, 512], F32)
                E = sbt(es, "E", [128, 4, 128], F32)
                Einv = sbt(es, "Einv", [128, 4, 128], F32)
                Ea = sbt(es, "Ea", [128, 512], F32)
                qeT = sbt(es, "qeT", [128, 4, 128], BF16)
                keT = sbt(es, "keT", [128, 4, 128], BF16)
                kd = sbt(es, "kd", [128, 512], BF16)
                v_ = sbt(es, "v", [128, D], BF16)
                sr = sbt(es, "sr", [128, 4, 256], F32)
                gr = sbt(es, "gr", [128, 4, 256], F32)
                attnT = sbt(es, "attnT", [128, 4, 128], BF16)
                S32 = sbt(es, "S32", [128, 4, 256], F32)
                S16 = sbt(es, "S16", [128, 4, 256], BF16)
                osq = sbt(es, "osq", [128, 4, 256], F32)
                ss = sbt(es, "ss", [128, 8], F32)
                og = sbt(es, "og", [128, D], BF16)
                ogT = sbt(es, "ogT", [128, 8, 128], BF16)
                stt = sbt(es, "stt", [128, 2, 6], F32)
                mv = sbt(es, "mv", [128, 8], F32)
                op("dve", lambda e: e.memset(S32[:], 0.0), writes=[S32.b])
                op("dve", lambda e: e.memset(S16[:], 0.0), writes=[S16.b])

                def load_x(t):
                    xt = xts[t % 3]
                    dma("sp", lambda e: e.dma_start(out=xt[:], in_=src_tile(first, t)), reads=[DX[t]] if not first else [],
                        writes=[xt.b], sembuf=xt.b)

                load_x(0)
                if NT > 1:
                    load_x(1)
                for t in range(NT):
                    if t + 2 < NT:
                        load_x(t + 2)
                    xt = xts[t % 3]
                    xo = xos[t % 2]
                    modulate_T(xt, scp, sh, h_, hT, 0, tmp)
                    for (bank, c0) in ((1, 0), (2, 512)):
                        for m in range(4):
                            for k in range(8):
                                op("pe", lambda e, bank=bank, c0=c0, m=m, k=k: e.matmul(
                                    PS[:, bank, m * 128:(m + 1) * 128], lhsT=win[:, k, c0 + m * 128:c0 + (m + 1) * 128], rhs=hT[:, k, :],
                                    start=(k == 0), stop=(k == 7)), reads=[Bwin[k], hT.b], writes=[PB[bank]])
                    for k in range(8):
                        op("pe", lambda e, k=k: e.matmul(PS[0:16, 3, 0:128], lhsT=win[:, k, 3072:3088], rhs=hT[:, k, :],
                                                         start=(k == 0), stop=(k == 7)), reads=[Bwin[k], hT.b], writes=[PB[3]])
                    op("act", lambda e: e.copy(out=glrT[:], in_=PS[0:16, 3, 0:128]), reads=[PB[3]], writes=[glrT.b])
                    for (bank, c0) in ((4, 512), (5, 1024), (6, 1536), (7, 2048), (0, 2560)):
                        for k in range(8):
                            op("pe", lambda e, bank=bank, c0=c0, k=k: e.matmul(
                                PS[:, bank, :], lhsT=hT[:, k, :], rhs=win[:, k, c0:c0 + 512], start=(k == 0), stop=(k == 7)),
                               reads=[Bwin[k], hT.b], writes=[PB[bank]])
                    op("dve", lambda e: e.tensor_copy(out=v_[:, 0:512], in_=PS[:, 5, :]), reads=[PB[5]], writes=[v_.b])
                    op("dve", lambda e: e.tensor_copy(out=v_[:, 512:1024], in_=PS[:, 6, :]), reads=[PB[6]], writes=[v_.b])
                    op("act", lambda e: e.activation(out=sr[:, 0:2, :], in_=PS[:, 7, :].rearrange("p (h d) -> p h d", h=2), func=ACT.Silu),
                       reads=[PB[7]], writes=[sr.b])
                    op("act", lambda e: e.activation(out=sr[:, 2:4, :], in_=PS[:, 0, :].rearrange("p (h d) -> p h d", h=2), func=ACT.Silu),
                       reads=[PB[0]], writes=[sr.b])
                    op("pool", lambda e: e.tensor_tensor(out=gr[:], in0=sr[:], in1=ngb[:], op=ALU.mult), reads=[sr.b, ngb.b], writes=[gr.b])
                    op("pe", lambda e: e.matmul(PS[:, 3, :], lhsT=glrT[:], rhs=wgu[:], start=True, stop=False),
                       reads=[glrT.b, wgu.b], writes=[PB[3]])
                    op("pe", lambda e: e.matmul(PS[:, 3, :], lhsT=ones_b[:], rhs=bgr[:], start=False, stop=True),
                       reads=[ones_b.b, bgr.b], writes=[PB[3]])
                    op("act", lambda e: e.activation(out=e1[:], in_=PS[:, 3, :], func=ACT.Exp, scale=-1.0), reads=[PB[3]], writes=[e1.b])
                    op("act", lambda e: e.activation(out=sp_[:], in_=e1[:], func=ACT.Ln, bias=1.0), reads=[e1.b], writes=[sp_.b])
                    for hh in range(4):
                        op("pe", lambda e, hh=hh: e.matmul(PS[:, 5, hh * 128:(hh + 1) * 128], lhsT=sp_[:, hh * 128:(hh + 1) * 128],
                                                           rhs=tri_incl, start=True, stop=True),
                           reads=[sp_.b, cst_f.b], writes=[PB[5]])
                    op("pe", lambda e: e.matmul(PS[:, 6, :], lhsT=tri_after, rhs=sp_[:], start=True, stop=True),
                       reads=[sp_.b, cst_f.b], writes=[PB[6]])
                    p5 = PS[:, 5, :].rearrange("p (h n) -> p h n", h=4)
                    op("act", lambda e: e.activation(out=E[:], in_=p5, func=ACT.Exp, scale=-1.0 / 16), reads=[PB[5]], writes=[E.b])
                    op("act", lambda e: e.activation(out=Einv[:], in_=p5, func=ACT.Exp, scale=1.0 / 16), reads=[PB[5]], writes=[Einv.b])
                    op("act", lambda e: e.activation(out=Ea[:], in_=PS[:, 6, :], func=ACT.Exp, scale=-1.0 / 16), reads=[PB[6]], writes=[Ea.b])
                    op("dve", lambda e: e.scalar_tensor_tensor(out=qeT[:], in0=PS[:, 1, :].rearrange("p (h n) -> p h n", h=4), scalar=128.0 ** -0.5,
                                                               in1=E[:], op0=ALU.mult, op1=ALU.mult), reads=[PB[1], E.b], writes=[qeT.b])
                    op("dve", lambda e: e.tensor_tensor(out=keT[:], in0=PS[:, 2, :].rearrange("p (h n) -> p h n", h=4), in1=Einv[:], op=ALU.mult),
                       reads=[PB[2], Einv.b], writes=[keT.b])
                    op("dve", lambda e: e.tensor_tensor(out=kd[:], in0=PS[:, 4, :], in1=Ea[:], op=ALU.mult), reads=[PB[4], Ea.b], writes=[kd.b])
                    for hh in range(4):
                        op("pe", lambda e, hh=hh: e.matmul(PS[:, 7, hh * 128:(hh + 1) * 128], lhsT=keT[:, hh, :], rhs=qeT[:, hh, :], start=True, stop=True),
                           reads=[keT.b, qeT.b], writes=[PB[7]])
                    op("dve", lambda e: e.tensor_tensor(out=attnT[:], in0=PS[:, 7, :].rearrange("p (h n) -> p h n", h=4), in1=mask4[:], op=ALU.mult),
                       reads=[PB[7], mask4.b], writes=[attnT.b])
                    for hh in range(4):
                        ob = hh // 2
                        oc = (hh % 2) * 256
                        op("pe", lambda e, hh=hh, ob=ob, oc=oc: e.matmul(PS[:, ob, oc:oc + 256], lhsT=attnT[:, hh, :], rhs=v_[:, hh * 256:(hh + 1) * 256],
                                                                         start=True, stop=False), reads=[attnT.b, v_.b], writes=[PB[ob]])
                        op("pe", lambda e, hh=hh, ob=ob, oc=oc: e.matmul(PS[:, ob, oc:oc + 256], lhsT=qeT[:, hh, :], rhs=S16[:, hh, :],
                                                                         start=False, stop=True), reads=[qeT.b, S16.b], writes=[PB[ob]])
                    for hh in range(4):
                        ob = 2 + hh // 2
                        oc = (hh % 2) * 256
                        op("pe", lambda e, hh=hh, ob=ob, oc=oc: e.matmul(PS[:, ob, oc:oc + 256], lhsT=kd[:, hh * 128:(hh + 1) * 128],
                                                                         rhs=v_[:, hh * 256:(hh + 1) * 256], start=True, stop=True),
                           reads=[kd.b, v_.b], writes=[PB[ob]])
                    for hh in range(4):
                        ob = 2 + hh // 2
                        oc = (hh % 2) * 256
                        op("dve", lambda e, hh=hh, ob=ob, oc=oc: e.scalar_tensor_tensor(
                            out=S32[:, hh, :], in0=S32[:, hh, :], scalar=E[:, hh, 127:128], in1=PS[:, ob, oc:oc + 256], op0=ALU.mult, op1=ALU.add),
                           reads=[S32.b, E.b, PB[ob]], writes=[S32.b])
                    op("pool", lambda e: e.tensor_copy(out=S16[:], in_=S32[:]), reads=[S32.b], writes=[S16.b])
                    ov = PS[:, 0:2, :].rearrange("p b (h d) -> p (b h) d", h=2)
                    op("act", lambda e: e.activation(out=osq[:], in_=ov, func=ACT.Square), reads=[PB[0], PB[1]], writes=[osq.b])
                    op("dve", lambda e: e.tensor_reduce(out=ss[:, 0:4], in_=osq[:], axis=AX.X, op=ALU.add), reads=[osq.b], writes=[ss.b])
                    op("act", lambda e: e.activation(out=ss[:, 4:8], in_=ss[:, 0:4], func=ACT.Ln, scale=1.0 / 256, bias=RMS_EPS),
                       reads=[ss.b], writes=[ss.b])
                    op("act", lambda e: e.activation(out=ss[:, 0:4], in_=ss[:, 4:8], func=ACT.Exp, scale=-0.5), reads=[ss.b], writes=[ss.b])
                    for hh in range(4):
                        ob = hh // 2
                        oc = (hh % 2) * 256
                        op("dve", lambda e, hh=hh, ob=ob, oc=oc: e.scalar_tensor_tensor(
                            out=og[:, hh * 256:(hh + 1) * 256], in0=PS[:, ob, oc:oc + 256], scalar=ss[:, hh:hh + 1], in1=gr[:, hh, :],
                            op0=ALU.mult, op1=ALU.mult), reads=[PB[ob], ss.b, gr.b], writes=[og.b])
                    pv = PS[:, 4, :].bitcast(BF16)
                    for k in range(8):
                        op("pe", lambda e, k=k: e.transpose(out=pv[:, k * 128:(k + 1) * 128], in_=og[:, k * 128:(k + 1) * 128], identity=ident_b[:]),
                           reads=[og.b, ident_b.b], writes=[PB[4]])
                    op("act", lambda e: e.copy(out=ogT[:], in_=pv.rearrange("p (k n) -> p k n", k=8)), reads=[PB[4]], writes=[ogT.b])
                    for cb in range(2):
                        for k in range(8):
                            op("pe", lambda e, cb=cb, k=k: e.matmul(PS[:, 5 + cb, :], lhsT=ogT[:, k, :], rhs=wout[:, k, cb * 512:(cb + 1) * 512],
                                                                    start=(k == 0), stop=(k == 7)), reads=[ogT.b, Bwout[k]], writes=[PB[5 + cb]])
                    resid_ln(xt, 5, gp, lng, lnb, tmp, z, xo, stt, mv)
                    dma("sp", lambda e, t=t, xo=xo: e.dma_start(out=x_dst(last)[t * 128:(t + 1) * 128, :], in_=xo[:]),
                        reads=[xo.b], writes=[DX[t]], sembuf=xo.b)
                S_.wait_all("sp", DX)
                flush()
                free_dsem([scp, sh, gp, lng, lnb, ngb, wgu_f, bgr_f, tmp, z] + xts + xos)

        def ffn_phase(l, last=False):
            first = state["first"]
            state["first"] = False
            with contextlib.ExitStack() as es:
                scp = load_bc(es, "scp", ada_d[l:l + 1, 4096:5120])
                sh = load_bc(es, "sh", ada_d[l:l + 1, 3072:4096])
                gp = load_bc(es, "gp", ada_d[l:l + 1, 5120:6144])
                lng = load_bc(es, "lng", ln_g[l, 1:2, :])
                lnb = load_bc(es, "lnb", ln_b[l, 1:2, :])
                xts = [sbt(es, f"xt{i}", [128, D], F32) for i in range(2)]
                xos = [sbt(es, f"xo{i}", [128, D], F32) for i in range(2)]
                tmp = sbt(es, "tmp", [128, D], F32)
                z = sbt(es, "z", [128, D], F32)
                stg = [tmp, z] + xos
                win, Bwin = load_w(es, "fwin", ffn_w_in[l], 8, 2 * FH, stg=stg)
                wout, Bwout = load_w(es, "fwout", ffn_w_out[l], 22, D, stg=stg)
                hs = [sbt(es, f"h{i}", [128, D], BF16) for i in range(2)]
                hTs = [sbt(es, f"hT{i}", [128, 8, 128], BF16) for i in range(2)]
                tmpm = sbt(es, "tmpm", [128, D], F32)
                sgs = [sbt(es, f"sg{i}", [128, 512], F32) for i in range(1)]
                a_ = sbt(es, "a", [128, FH], BF16)
                aT = sbt(es, "aT", [128, 22, 128], BF16)
                stt = sbt(es, "stt", [128, 2, 6], F32)
                mv = sbt(es, "mv", [128, 8], F32)

                def load_x(t):
                    xt = xts[t % 2]
                    dma("sp", lambda e: e.dma_start(out=xt[:], in_=src_tile(first, t)), reads=[DX[t]] if not first else [],
                        writes=[xt.b], sembuf=xt.b)

                def front(t):
                    modulate_T(xts[t % 2], scp, sh, hs[t % 2], hTs[t % 2], 0, tmpm)

                load_x(0)
                front(0)
                for t in range(NT):
                    if t + 1 < NT:
                        load_x(t + 1)
                    xt = xts[t % 2]
                    xo = xos[t % 2]
                    hT = hTs[t % 2]
                    for j in range(6):
                        wd = 512 if j < 5 else 256
                        gb = 1 + 2 * (j % 2)
                        ub = gb + 1
                        sg = sgs[0]
                        for (bank, c0) in ((gb, j * 512), (ub, FH + j * 512)):
                            for k in range(8):
                                op("pe", lambda e, bank=bank, c0=c0, k=k, wd=wd, hT=hT: e.matmul(
                                    PS[:, bank, 0:wd], lhsT=hT[:, k, :], rhs=win[:, k, c0:c0 + wd], start=(k == 0), stop=(k == 7)),
                                   reads=[hT.b, Bwin[k]], writes=[PB[bank]])
                        op("act", lambda e, gb=gb, wd=wd, sg=sg: e.activation(out=sg[:, 0:wd], in_=PS[:, gb, 0:wd], func=ACT.Silu),
                           reads=[PB[gb]], writes=[sg.b])
                        op("dve", lambda e, ub=ub, wd=wd, j=j, sg=sg: e.tensor_tensor(out=a_[:, j * 512:j * 512 + wd], in0=PS[:, ub, 0:wd], in1=sg[:, 0:wd], op=ALU.mult),
                           reads=[PB[ub], sg.b], writes=[a_.b])
                    if t + 1 < NT:
                        front(t + 1)
                    for rnd in range(3):
                        bank = 5 if rnd % 2 == 0 else 0
                        n = 8 if rnd < 2 else 6
                        pv = PS[:, bank, :].bitcast(BF16)
                        for i in range(n):
                            kk = rnd * 8 + i
                            op("pe", lambda e, i=i, kk=kk, pv=pv: e.transpose(out=pv[:, i * 128:(i + 1) * 128], in_=a_[:, kk * 128:(kk + 1) * 128], identity=ident_b[:]),
                               reads=[a_.b, ident_b.b], writes=[PB[bank]])
                        op("act", lambda e, rnd=rnd, n=n, pv=pv: e.copy(out=aT[:, rnd * 8:rnd * 8 + n, :], in_=pv[:, 0:n * 128].rearrange("p (k n) -> p k n", k=n)),
                           reads=[PB[bank]], writes=[aT.b])
                    for cb in range(2):
                        for k in range(22):
                            op("pe", lambda e, cb=cb, k=k: e.matmul(PS[:, 6 + cb, :], lhsT=aT[:, k, :], rhs=wout[:, k, cb * 512:(cb + 1) * 512],
                                                                    start=(k == 0), stop=(k == 21)), reads=[aT.b, Bwout[k]], writes=[PB[6 + cb]])
                    resid_ln(xt, 6, gp, lng, lnb, tmp, z, xo, stt, mv)
                    dma("sp", lambda e, t=t, xo=xo: e.dma_start(out=x_dst(last)[t * 128:(t + 1) * 128, :], in_=xo[:]),
                        reads=[xo.b], writes=[DX[t]], sembuf=xo.b)
                S_.wait_all("sp", DX)
                flush()
                free_dsem([scp, sh, gp, lng, lnb, tmp, z] + xts + xos)


        def proj_phase(row, wsrc, nfm, fm_col0, dstT, BdT, fm_scale, tok_cols=None):
            with contextlib.ExitStack() as es:
                scp = load_bc(es, "scp", ada_d[row:row + 1, 1024:2048])
                sh = load_bc(es, "sh", ada_d[row:row + 1, 0:1024])
                ncols = fm_col0 + nfm * 128 if tok_cols is None else max(fm_col0 + nfm * 128, tok_cols[0] + tok_cols[1])
                xts = [sbt(es, f"xt{i}", [128, D], F32) for i in range(3)]
                tmp = sbt(es, "tmp", [128, D], F32)
                stg2 = sbt(es, "stg2", [128, D], F32)
                w, Bw = load_w(es, "pw", wsrc, 8, ncols, stg=[tmp, stg2])
                h_ = sbt(es, "h", [128, D], BF16)
                hT4 = [sbt(es, f"hT4{i}", [128, 8, 512], BF16) for i in range(2)]
                fmo = [sbt(es, f"fmo{i}", [128, nfm, 512], BF16) for i in range(2)]
                vo = [sbt(es, f"vo{i}", [128, D], BF16) for i in range(2)]

                def load_x(t):
                    xt = xts[t % 3]
                    dma("sp", lambda e: e.dma_start(out=xt[:], in_=xs[t * 128:(t + 1) * 128, :]), reads=[DX[t]],
                        writes=[xt.b], sembuf=xt.b)
                load_x(0)
                load_x(1)
                for blk in range(S // 512):
                    hT = hT4[blk % 2]
                    fo = fmo[blk % 2]
                    for tt in range(4):
                        t = blk * 4 + tt
                        if t + 2 < NT:
                            load_x(t + 2)
                        modulate_T(xts[t % 3], scp, sh, h_, hT, 0, tmp, col0=tt * 128)
                        if tok_cols is not None:
                            v16 = vo[t % 2]
                            for cb in range(2):
                                for k in range(8):
                                    op("pe", lambda e, cb=cb, k=k, hT=hT, tt=tt: e.matmul(
                                        PS[:, 1 + cb, :], lhsT=hT[:, k, tt * 128:(tt + 1) * 128],
                                        rhs=w[:, k, tok_cols[0] + cb * 512:tok_cols[0] + (cb + 1) * 512], start=(k == 0), stop=(k == 7)),
                                       reads=[hT.b, Bw[k]], writes=[PB[1 + cb]])
                            op("dve", lambda e, v16=v16: e.tensor_copy(out=v16[:], in_=PS[:, 1:3, :].rearrange("p b n -> p (b n)")),
                               reads=[PB[1], PB[2]], writes=[v16.b])
                            dma("sp", lambda e, v16=v16, t=t: e.dma_start(out=v_d[t * 128:(t + 1) * 128, :], in_=v16[:]),
                                reads=[v16.b], writes=[B_vd[t]], sembuf=v16.b)
                    for m in range(nfm):
                        bank = 3 + (m % 4)
                        for k in range(8):
                            op("pe", lambda e, bank=bank, m=m, k=k, hT=hT: e.matmul(
                                PS[:, bank, :], lhsT=w[:, k, fm_col0 + m * 128:fm_col0 + (m + 1) * 128], rhs=hT[:, k, :],
                                start=(k == 0), stop=(k == 7)), reads=[hT.b, Bw[k]], writes=[PB[bank]])
                        eng = "act" if m % 2 == 0 else "dve"
                        if eng == "act":
                            op("act", lambda e, bank=bank, m=m, fo=fo: e.activation(out=fo[:, m, :], in_=PS[:, bank, :], func=ACT.Identity, scale=fm_scale),
                               reads=[PB[bank]], writes=[fo.b])
                        else:
                            op("dve", lambda e, bank=bank, m=m, fo=fo: e.tensor_scalar_mul(out=fo[:, m, :], in0=PS[:, bank, :], scalar1=fm_scale),
                               reads=[PB[bank]], writes=[fo.b])
                    dma("sp", lambda e, fo=fo, blk=blk: e.dma_start(
                        out=dstT[:, blk * 512:(blk + 1) * 512].rearrange("(m p) n -> p m n", p=128), in_=fo[:]),
                        reads=[fo.b], writes=[BdT[blk]], sembuf=fo.b)
                S_.wait_all("sp", BdT + (B_vd if tok_cols is not None else []))
                flush()
                free_dsem([scp, sh, tmp, stg2] + xts + fmo + vo)

        def attn_phase():
            NMS = S // 2048
            with contextlib.ExitStack() as es:
                maskb = sbt(es, "maskb", [128, 4, 256], F32)
                maskb0 = sbt(es, "maskb0", [128, 4, 256], F32)
                for hh in range(4):
                    op("dve", lambda e, hh=hh: e.tensor_copy(out=maskb[:, hh, 0:128], in_=cst_f[:, 3, :]), reads=[cst_f.b], writes=[maskb.b])
                    op("dve", lambda e, hh=hh: e.tensor_copy(out=maskb[:, hh, 128:256], in_=cst_f[:, 4, :]), reads=[cst_f.b], writes=[maskb.b])
                    op("dve", lambda e, hh=hh: e.tensor_copy(out=maskb0[:, hh, 0:128], in_=cst_f[:, 5, :]), reads=[cst_f.b], writes=[maskb0.b])
                    op("dve", lambda e, hh=hh: e.tensor_copy(out=maskb0[:, hh, 128:256], in_=cst_f[:, 4, :]), reads=[cst_f.b], writes=[maskb0.b])
                kTb = [sbt(es, f"kTb{i}", [128, 8, 2048], BF16) for i in range(2)]
                qz = sbt(es, "qz", [128, 16, 2048], BF16)
                qzE = Buf("qzE_%d" % uid[0])
                qzO = Buf("qzO_%d" % uid[0])
                uid[0] += 1
                op("pool", lambda e: e.memset(qz[:], 0.0), writes=[qz.b, qzE, qzO])
                vvs = [sbt(es, f"vv{i}", [128, 2, D], BF16) for i in range(3)]
                sm = [sbt(es, f"sm{i}", [128, 4, 256], F32) for i in range(2)]
                pbf = [sbt(es, f"pbf{i}", [128, 4, 256], BF16) for i in range(2)]
                pTs = [sbt(es, f"pTs{i}", [128, 8, 128], BF16) for i in range(2)]
                mdt = [sbt(es, f"mdt{i}", [128, 32], F32) for i in range(2)]
                negm = [sbt(es, f"negm{i}", [128, 8], F32) for i in range(2)]
                rden = sbt(es, "rden", [128, 16], F32)
                ogt = [sbt(es, f"ogt{i}", [128, 16, 64], BF16) for i in range(2)]
                items = []
                ucnt = 0
                mdMb = [Buf("mdM%d_%d" % (i_, uid[0])) for i_ in range(2)]
                mdDb = [Buf("mdD%d_%d" % (i_, uid[0])) for i_ in range(2)]
                uid[0] += 1
                for ms in range(NMS):
                    base = ms * 2048
                    for g, d in enumerate(DILS):
                        nbk = 16 // d
                        for r in range(d):
                            for b in range(nbk):
                                U = dict(ms=ms, base=base, g=g, d=d, r=r, b=b, nbk=nbk, gb0=(ms == 0 and b == 0),
                                         n0=base // d + b * 128, vv=vvs[ucnt % 3], md=mdt[ucnt % 2], og_t=ogt[ucnt % 2],
                                         mdM=mdMb[ucnt % 2], mdD=mdDb[ucnt % 2],
                                         first_of_group=(r == 0 and b == 0), first_of_ms=(g == 0 and r == 0 and b == 0))
                                ucnt += 1
                                for hg in range(4):
                                    items.append(dict(U=U, hg=hg, idx=len(items)))

                def scores(it):
                    U = it["U"]; hg = it["hg"]; par = it["idx"] % 2
                    ms, base, g, d, r, b, nbk, gb0, n0 = (U[k_] for k_ in ("ms", "base", "g", "d", "r", "b", "nbk", "gb0", "n0"))
                    kc = kTb[ms % 2]
                    kp = kTb[(ms + 1) % 2]
                    if hg == 0:
                        if U["first_of_ms"]:
                            dma("sp", lambda e: e.dma_start(out=kc[:], in_=kT_d[:, base:base + 2048].rearrange("(m p) n -> p m n", p=128)),
                                reads=B_kT, writes=[kc.b], sembuf=kc.b)
                        if U["first_of_group"]:
                            qsrc = qT_d[g * D:(g + 1) * D, base:base + 2048].rearrange("(m two p) n -> two p m n", two=2, p=64)
                            dma("sp", lambda e: e.dma_start(out=qz[0:64, 0:16:2, :], in_=qsrc[0]),
                                reads=B_qT + [qz.b], writes=[qzE], sembuf=qzE)
                            dma("sp", lambda e: e.dma_start(out=qz[64:128, 1:16:2, :], in_=qsrc[1]),
                                reads=B_qT + [qz.b], writes=[qzO], sembuf=qzO)
                        vview = v_d.rearrange("(n dd) c -> dd n c", dd=d)
                        vv = U["vv"]
                        if gb0:
                            dma("sp", lambda e: e.dma_start(out=vv[:, 1, :], in_=vview[r, n0:n0 + 128, :]),
                                reads=B_vd, writes=[vv.b], sembuf=vv.b)
                        else:
                            dma("sp", lambda e: e.dma_start(
                                out=vv[:], in_=vview[r, n0 - 128:n0 + 128, :].rearrange("(two p) c -> p two c", p=128)),
                                reads=B_vd, writes=[vv.b], sembuf=vv.b)
                    q0 = r + b * 128 * d
                    if b >= 1:
                        ksrc, kp0 = kc, r + (b - 1) * 128 * d
                    elif not gb0:
                        ksrc, kp0 = kp, r + (nbk - 1) * 128 * d
                    else:
                        ksrc, kp0 = kc, q0
                    sb0 = 2 * par
                    for hh in range(4):
                        hd = hg * 4 + hh
                        c = hd // 2
                        bank = sb0 + hh // 2
                        co = (hh % 2) * 256
                        qB = qzE if hd % 2 == 0 else qzO
                        op("pe", lambda e, bank=bank, co=co, hd=hd, c=c: e.matmul(
                            PS[:, bank, co:co + 128], lhsT=qz[:, hd, q0:q0 + 127 * d + 1:d],
                            rhs=ksrc[:, c, kp0:kp0 + 127 * d + 1:d], start=True, stop=True),
                           reads=[qB, ksrc.b], writes=[PB[bank]])
                        op("pe", lambda e, bank=bank, co=co, hd=hd, c=c: e.matmul(
                            PS[:, bank, co + 128:co + 256], lhsT=qz[:, hd, q0:q0 + 127 * d + 1:d],
                            rhs=kc[:, c, q0:q0 + 127 * d + 1:d], start=True, stop=True),
                           reads=[qB, kc.b], writes=[PB[bank]])

                def softmax(it):
                    U = it["U"]; hg = it["hg"]; par = it["idx"] % 2
                    sb0 = 2 * par
                    smt, pb_, ng, md = sm[par], pbf[par], negm[par], U["md"]
                    mk = maskb0 if U["gb0"] else maskb
                    op("dve", lambda e: e.tensor_tensor(
                        out=smt[:], in0=PS[:, sb0:sb0 + 2, :].rearrange("p b (h n) -> p (b h) n", h=2), in1=mk[:], op=ALU.add),
                       reads=[PB[sb0], PB[sb0 + 1], mk.b], writes=[smt.b])
                    op("dve", lambda e: e.tensor_reduce(out=md[:, hg * 4:(hg + 1) * 4], in_=smt[:], axis=AX.X, op=ALU.max),
                       reads=[smt.b], writes=[U["mdM"]])
                    op("dve", lambda e: e.tensor_scalar_mul(out=ng[:, 0:4], in0=md[:, hg * 4:(hg + 1) * 4], scalar1=-1.0),
                       reads=[U["mdM"]], writes=[ng.b])

                def softmax_act(it):
                    U = it["U"]; hg = it["hg"]; par = it["idx"] % 2
                    smt, pb_, ng, md = sm[par], pbf[par], negm[par], U["md"]
                    for hh in range(4):
                        hd = hg * 4 + hh
                        op("act", lambda e, hh=hh, hd=hd: e.activation(
                            out=pb_[:, hh, :], in_=smt[:, hh, :], func=ACT.Exp, bias=ng[:, hh:hh + 1], accum_out=md[:, 16 + hd:17 + hd]),
                           reads=[smt.b, ng.b], writes=[pb_.b, U["mdD"]])

                def tail(it):
                    U = it["U"]; hg = it["hg"]; par = it["idx"] % 2
                    gb0, vv, md, og_t, d, r, n0, g = (U[k_] for k_ in ("gb0", "vv", "md", "og_t", "d", "r", "n0", "g"))
                    tb = 4 + par
                    pb_, pT = pbf[par], pTs[par]
                    pv = PS[:, tb, :].bitcast(BF16)
                    for hh in range(4):
                        for half in range(2):
                            i8 = hh * 2 + half
                            op("pe", lambda e, i8=i8, hh=hh, half=half: e.transpose(
                                out=pv[:, i8 * 128:(i8 + 1) * 128], in_=pb_[:, hh, half * 128:(half + 1) * 128], identity=ident_b[:]),
                               reads=[pb_.b, ident_b.b], writes=[PB[tb]])
                    op("act", lambda e: e.copy(out=pT[:], in_=pv.rearrange("p (k n) -> p k n", k=8)),
                       reads=[PB[tb]], writes=[pT.b])

                def tail_pv(it):
                    U = it["U"]; hg = it["hg"]; par = it["idx"] % 2
                    gb0, vv, md, og_t, d, r, n0, g = (U[k_] for k_ in ("gb0", "vv", "md", "og_t", "d", "r", "n0", "g"))
                    pT = pTs[par]
                    for hh in range(4):
                        hd = hg * 4 + hh
                        ob = 6 + hd // 8
                        oc = (hd % 8) * 64
                        if not gb0:
                            op("pe", lambda e, ob=ob, oc=oc, hh=hh, hd=hd: e.matmul(
                                PS[:, ob, oc:oc + 64], lhsT=pT[:, hh * 2, :], rhs=vv[:, 0, hd * 64:(hd + 1) * 64], start=True, stop=False),
                               reads=[pT.b, vv.b], writes=[PB[ob]])
                        op("pe", lambda e, ob=ob, oc=oc, hh=hh, hd=hd: e.matmul(
                            PS[:, ob, oc:oc + 64], lhsT=pT[:, hh * 2 + 1, :], rhs=vv[:, 1, hd * 64:(hd + 1) * 64], start=gb0, stop=True),
                           reads=[pT.b, vv.b], writes=[PB[ob]])
                    if hg == 3:
                        ogview = og_d[g].rearrange("(n dd) c -> dd n c", dd=d)
                        mdview = md_d[g].rearrange("(n dd) c -> dd n c", dd=d)
                        op("dve", lambda e: e.reciprocal(out=rden[:], in_=md[:, 16:32]), reads=[U["mdD"]], writes=[rden.b])
                        op("dve", lambda e: e.tensor_tensor(
                            out=og_t[:], in0=PS[:, 6:8, :].rearrange("p b (h n) -> p (b h) n", h=8),
                            in1=rden[:].unsqueeze(2).to_broadcast([128, 16, 64]), op=ALU.mult),
                           reads=[PB[6], PB[7], rden.b], writes=[og_t.b])
                        dma("sp", lambda e: e.dma_start(out=ogview[r, n0:n0 + 128, :], in_=og_t[:].rearrange("p h n -> p (h n)")),
                            reads=[og_t.b], writes=[B_og], sembuf=og_t.b)
                        dma("sp", lambda e: e.dma_start(out=mdview[r, n0:n0 + 128, :], in_=md[:]),
                            reads=[U["mdM"], U["mdD"]], writes=[B_og], sembuf=md.b)

                if items:
                    scores(items[0])
                    softmax(items[0])
                    softmax_act(items[0])
                for i_, it in enumerate(items):
                    nxt = items[i_ + 1] if i_ + 1 < len(items) else None
                    if nxt is not None:
                        scores(nxt)
                        softmax(nxt)
                    tail(it)
                    if nxt is not None:
                        softmax_act(nxt)
                    tail_pv(it)
                S_.wait_all("sp", [B_og])
                flush()
                free_dsem(kTb + [qzE, qzO] + vvs + mdt + ogt)

        def comb_phase(l, last=False):
            li = l - 2
            with contextlib.ExitStack() as es:
                gp = load_bc(es, "gp", ada_d[l:l + 1, 2048:3072])
                lng = load_bc(es, "lng", ln_g[l, 0:1, :])
                lnb = load_bc(es, "lnb", ln_b[l, 0:1, :])
                xts = [sbt(es, f"xt{i}", [128, D], F32) for i in range(2)]
                xos = [sbt(es, f"xo{i}", [128, D], F32) for i in range(2)]
                wout, Bwout = load_w(es, "dwout", dil_w_out[li], 8, D, stg=xos)
                ogs = [[sbt(es, f"ogl{i}_{g}", [128, 16, 64], BF16) for g in range(3)] for i in range(2)]
                mds = [[sbt(es, f"mdl{i}_{g}", [128, 32], F32) for g in range(3)] for i in range(2)]
                tmp = sbt(es, "tmp", [128, D], F32)
                z = sbt(es, "z", [128, D], F32)
                o_ = sbt(es, "o", [128, D], BF16)
                oT = sbt(es, "oT", [128, 8, 128], BF16)
                M = sbt(es, "M", [128, 16], F32)
                ew = sbt(es, "ew", [128, 3, 16], F32)
                W = sbt(es, "W", [128, 16], F32)
                stt = sbt(es, "stt", [128, 2, 6], F32)
                mv = sbt(es, "mv", [128, 8], F32)

                def load(t):
                    i = t % 2
                    dma("sp", lambda e: e.dma_start(out=xts[i][:], in_=xs[t * 128:(t + 1) * 128, :]), reads=[DX[t]],
                        writes=[xts[i].b], sembuf=xts[i].b)
                    for g in range(3):
                        dma("sp", lambda e, g=g: e.dma_start(out=ogs[i][g][:].rearrange("p h n -> p (h n)"), in_=og_d[g][t * 128:(t + 1) * 128, :]),
                            reads=[B_og], writes=[ogs[i][g].b], sembuf=ogs[i][g].b)
                        dma("sp", lambda e, g=g: e.dma_start(out=mds[i][g][:], in_=md_d[g][t * 128:(t + 1) * 128, :]),
                            reads=[B_og], writes=[mds[i][g].b], sembuf=mds[i][g].b)
                load(0)
                for t in range(NT):
                    if t + 1 < NT:
                        load(t + 1)
                    i = t % 2
                    xt, xo, og3, md3 = xts[i], xos[i], ogs[i], mds[i]
                    op("dve", lambda e, md3=md3: e.tensor_tensor(out=M[:], in0=md3[0][:, 0:16], in1=md3[1][:, 0:16], op=ALU.max),
                       reads=[md3[0].b, md3[1].b], writes=[M.b])
                    op("dve", lambda e, md3=md3: e.tensor_tensor(out=M[:], in0=M[:], in1=md3[2][:, 0:16], op=ALU.max),
                       reads=[M.b, md3[2].b], writes=[M.b])
                    for g in range(3):
                        op("dve", lambda e, g=g, md3=md3: e.tensor_tensor(out=ew[:, g, :], in0=md3[g][:, 0:16], in1=M[:], op=ALU.subtract),
                           reads=[md3[g].b, M.b], writes=[ew.b])
                    op("act", lambda e: e.activation(out=ew[:], in_=ew[:], func=ACT.Exp), reads=[ew.b], writes=[ew.b])
                    for g in range(3):
                        op("dve", lambda e, g=g, md3=md3: e.tensor_tensor(out=ew[:, g, :], in0=ew[:, g, :], in1=md3[g][:, 16:32], op=ALU.mult),
                           reads=[ew.b, md3[g].b], writes=[ew.b])
                    op("dve", lambda e: e.tensor_tensor(out=W[:], in0=ew[:, 0, :], in1=ew[:, 1, :], op=ALU.add), reads=[ew.b], writes=[W.b])
                    op("dve", lambda e: e.tensor_tensor(out=W[:], in0=W[:], in1=ew[:, 2, :], op=ALU.add), reads=[ew.b, W.b], writes=[W.b])
                    op("dve", lambda e: e.reciprocal(out=W[:], in_=W[:]), reads=[W.b], writes=[W.b])
                    for g in range(3):
                        op("dve", lambda e, g=g: e.tensor_tensor(out=ew[:, g, :], in0=ew[:, g, :], in1=W[:], op=ALU.mult),
                           reads=[ew.b, W.b], writes=[ew.b])
                    tv = tmp[:].rearrange("p (h n) -> p h n", h=16)
                    zv = z[:].rearrange("p (h n) -> p h n", h=16)
                    op("dve", lambda e, og3=og3: e.tensor_tensor(out=tv, in0=og3[0][:], in1=ew[:, 0, :].unsqueeze(2).to_broadcast([128, 16, 64]), op=ALU.mult),
                       reads=[og3[0].b, ew.b], writes=[tmp.b])
                    op("dve", lambda e, og3=og3: e.tensor_tensor(out=zv, in0=og3[1][:], in1=ew[:, 1, :].unsqueeze(2).to_broadcast([128, 16, 64]), op=ALU.mult),
                       reads=[og3[1].b, ew.b], writes=[z.b])
                    op("dve", lambda e: e.tensor_tensor(out=tmp[:], in0=tmp[:], in1=z[:], op=ALU.add), reads=[tmp.b, z.b], writes=[tmp.b])
                    op("dve", lambda e, og3=og3: e.tensor_tensor(out=zv, in0=og3[2][:], in1=ew[:, 2, :].unsqueeze(2).to_broadcast([128, 16, 64]), op=ALU.mult),
                       reads=[og3[2].b, ew.b], writes=[z.b])
                    op("dve", lambda e: e.tensor_tensor(out=o_[:], in0=tmp[:], in1=z[:], op=ALU.add), reads=[tmp.b, z.b], writes=[o_.b])
                    pv = PS[:, 0, :].bitcast(BF16)
                    for k in range(8):
                        op("pe", lambda e, k=k: e.transpose(out=pv[:, k * 128:(k + 1) * 128], in_=o_[:, k * 128:(k + 1) * 128], identity=ident_b[:]),
                           reads=[o_.b, ident_b.b], writes=[PB[0]])
                    op("act", lambda e: e.copy(out=oT[:], in_=pv.rearrange("p (k n) -> p k n", k=8)), reads=[PB[0]], writes=[oT.b])
                    yb = 1 + 2 * (t % 2)
                    for cb in range(2):
                        for k in range(8):
                            op("pe", lambda e, cb=cb, k=k, yb=yb: e.matmul(PS[:, yb + cb, :], lhsT=oT[:, k, :], rhs=wout[:, k, cb * 512:(cb + 1) * 512],
                                                                           start=(k == 0), stop=(k == 7)), reads=[oT.b, Bwout[k]], writes=[PB[yb + cb]])
                    resid_ln(xt, yb, gp, lng, lnb, tmp, z, xo, stt, mv)
                    dma("sp", lambda e, t=t, xo=xo: e.dma_start(out=x_dst(last)[t * 128:(t + 1) * 128, :], in_=xo[:]),
                        reads=[xo.b], writes=[DX[t]], sembuf=xo.b)
                S_.wait_all("sp", DX)
                flush()
                free_dsem([gp, lng, lnb] + xts + xos + [a for b_ in ogs for a in b_] + [a for b_ in mds for a in b_])

        phases = []
        for l in range(2):
            phases.append(("gla%d" % l, lambda last, l=l: gla_phase(l, last)))
            phases.append(("ffn%d" % l, lambda last, l=l: ffn_phase(l, last)))
        for l in (2, 3):
            def dil(last, l=l):
                if l == 2:
                    proj_phase(4, w_kv, 8, 0, kT_d, B_kT, 1.0, tok_cols=(1024, 1024))
                if sub == "kv":
                    return
                proj_phase(l, dil_w_q[l - 2], 24, 0, qT_d, B_qT, 0.125)
                if sub == "q":
                    return
                attn_phase()
                if sub == "attn":
                    return
                comb_phase(l, last)
            phases.append(("dil%d" % l, dil))
            phases.append(("ffn%d" % l, lambda last, l=l: ffn_phase(l, last)))
        if attn_only:
            attn_phase()
            phases = []
            stop_after = None
        names = [p[0] for p in phases]
        sub = None
        if stop_after is not None and ":" in stop_after:
            stop_after, sub = stop_after.split(":")
        stop_idx = len(phases) - 1 if stop_after is None else names.index(stop_after)
        if attn_only:
            stop_idx = -1
        for i, (nm, fn) in enumerate(phases[:stop_idx + 1]):
            fn(i == stop_idx)
        print("instructions:", S_.ninst)
    return nc


def make_consts():
    cst = np.zeros((128, 6, 128), np.float32)
    j = np.arange(128)[:, None]
    i = np.arange(128)[None, :]
    cst[:, 0, :] = np.eye(128)
    cst[:, 1, :] = (j <= i)
    cst[:, 2, :] = (j > i)
    cst[:, 3, :] = np.where(i >= j, 0.0, NEG)
    cst[:, 4, :] = np.where(i <= j, 0.0, NEG)
    cst[:, 5, :] = NEG
    return cst


def make_in_maps(inputs, nb, S):
    cst = make_consts()
    shared = {k: np.ascontiguousarray(v) for k, v in inputs.items() if k not in ("x", "c")}
    shared["kv_ada_b"] = shared["kv_ada_b"].reshape(1, -1)
    shared["cst"] = cst
    maps = []
    for b in range(nb):
        m = dict(shared)
        m["x"] = np.ascontiguousarray(inputs["x"][b, :S])
        m["c_t"] = np.ascontiguousarray(inputs["c"][b].reshape(8, 128).T)
        maps.append(m)
    return maps


def kernel(**inputs):
    S = inputs["x"].shape[1]
    nc = build_nc(S)
    maps = make_in_maps(inputs, 8, S)
    res = run_bass_kernel_spmd(nc, maps, core_ids=list(range(8)))
    return np.stack([r["out"] for r in res.results], axis=0)
```
